# Optimizing a Trainium2 kernel written in Bass

```python
import math
import jax, jax.numpy as jnp
from jax import lax
import numpy as np

D_MODEL = 1024
BATCH = 2
SEQ = 8192
DEPTH = 4

GRID_W = 64
HEAD_DIM = 64
N_HEADS_NA = 8
N_HEADS_DIL = 8
NA_KH_MAX = 8
NA_KW = 16
DIL_PATTERNS = ((128, 1), (512, 4), (2048, 16))
DIL_BLOCK = 64
N_BUCKETS = 32
T5_MAX_DIST = 1024
CONF_CH = 512
CONF_K = 31
SC_CH = 512
SC_K = 3
D_FF = 2816
FFN_K = 3
LN_EPS = 1e-5
NEG = -1e30
ALPHA = (2 * DEPTH) ** 0.25
BETA = (8 * DEPTH) ** -0.25
N_ATTN_LAYERS = (DEPTH + 1) // 2
N_CONV_LAYERS = DEPTH // 2
ATTN_W = (N_HEADS_NA + N_HEADS_DIL) * HEAD_DIM
CONV_W = CONF_CH + SC_CH

kernel_name = "hybrid_natten_dilated_conformer_shortconv_encoder"


def layer_norm(x, g, b):
    xf = x.astype(jnp.float32)
    mu = jnp.mean(xf, axis=-1, keepdims=True)
    var = jnp.mean(jnp.square(xf - mu), axis=-1, keepdims=True)
    y = (xf - mu) * lax.rsqrt(var + LN_EPS) * g.astype(jnp.float32) + b.astype(jnp.float32)
    return y.astype(x.dtype)


def dwconv(x, w):
    k, c = w.shape
    return lax.conv_general_dilated(
        x, w[:, None, :].astype(x.dtype), window_strides=(1,),
        padding=[(k // 2, k - 1 - k // 2)],
        dimension_numbers=('NWC', 'WIO', 'NWC'), feature_group_count=c)


def t5_bucket(rel):
    nb = N_BUCKETS // 2
    max_exact = nb // 2
    ret = jnp.where(rel > 0, nb, 0)
    n = jnp.abs(rel)
    large = max_exact + (jnp.log(jnp.maximum(n, 1).astype(jnp.float32) / max_exact)
                         / math.log(T5_MAX_DIST / max_exact) * (nb - max_exact)).astype(jnp.int32)
    large = jnp.minimum(large, nb - 1)
    return ret + jnp.where(n < max_exact, n, large)


def neighbourhood_attention(q, k, v, rpb):
    B, S, H, Dh = q.shape
    rows = S // GRID_W
    kh = min(NA_KH_MAX, rows)
    shp = (B, rows, GRID_W, H, Dh)
    qg, kg, vg = q.reshape(shp), k.reshape(shp), v.reshape(shp)
    cols = jnp.arange(GRID_W)
    c0 = jnp.clip(cols - NA_KW // 2, 0, GRID_W - NA_KW)
    col_idx = c0[:, None] + jnp.arange(NA_KW)[None, :]
    col_rel = col_idx - cols[:, None] + NA_KW - 1

    def one_row(r):
        r0 = jnp.clip(r - kh // 2, 0, rows - kh)
        qr = lax.dynamic_index_in_dim(qg, r, axis=1, keepdims=False)
        kr = lax.dynamic_slice_in_dim(kg, r0, kh, axis=1)[:, :, col_idx]
        vr = lax.dynamic_slice_in_dim(vg, r0, kh, axis=1)[:, :, col_idx]
        s = jnp.einsum('bchd,bicjhd->bhcij', qr, kr).astype(jnp.float32)
        row_rel = r0 + jnp.arange(kh) - r + NA_KH_MAX - 1
        bias = rpb[:, row_rel][:, :, col_rel].astype(jnp.float32)
        s = s + jnp.transpose(bias, (0, 2, 1, 3))[None]
        a = jax.nn.softmax(s.reshape(B, H, GRID_W, kh * NA_KW), axis=-1).reshape(s.shape)
        return jnp.einsum('bhcij,bicjhd->bchd', a.astype(vr.dtype), vr)

    out = lax.map(one_row, jnp.arange(rows))
    return jnp.transpose(out, (1, 0, 2, 3, 4)).reshape(B, S, H * Dh)


def dilated_branch(q, k, v, t5_table, window, dil):
    B, S, H, Dh = q.shape
    L = S // dil
    half = window // (2 * dil)
    nblk = -(-L // DIL_BLOCK)
    Lp = nblk * DIL_BLOCK

    def to_res(t):
        t = t.reshape(B, L, dil, H, Dh)
        return jnp.pad(t, ((0, 0), (0, Lp - L), (0, 0), (0, 0), (0, 0)))

    def band(t):
        t = jnp.pad(to_res(t), ((0, 0), (DIL_BLOCK, DIL_BLOCK), (0, 0), (0, 0), (0, 0)))
        t = t.reshape(B, nblk + 2, DIL_BLOCK, dil, H, Dh)
        return jnp.concatenate([t[:, :-2], t[:, 1:-1], t[:, 2:]], axis=2)

    qr = to_res(q).reshape(B, nblk, DIL_BLOCK, dil, H, Dh)
    kb, vb = band(k), band(v)
    s = jnp.einsum('bnqrhd,bnkrhd->bnrhqk', qr, kb).astype(jnp.float32)
    qi = jnp.arange(DIL_BLOCK)
    ki = jnp.arange(3 * DIL_BLOCK)
    rel = ki[None, :] - DIL_BLOCK - qi[:, None]
    bias = jnp.transpose(t5_table[t5_bucket(rel * dil)], (2, 0, 1)).astype(jnp.float32)
    kpos = jnp.arange(nblk)[:, None] * DIL_BLOCK + ki[None, :] - DIL_BLOCK
    valid = ((jnp.abs(rel) <= half)[None]
             & ((kpos >= 0) & (kpos < L))[:, None, :])
    s = jnp.where(valid[None, :, None, None], s + bias, NEG)
    m = jnp.max(s, axis=-1, keepdims=True)
    p = jnp.exp(s - m)
    den = jnp.sum(p, axis=-1, keepdims=True)
    o = jnp.einsum('bnrhqk,bnkrhd->bnqrhd', (p / den).astype(vb.dtype), vb)
    lse = (m + jnp.log(den))[..., 0]
    o = o.reshape(B, Lp, dil, H, Dh)[:, :L].reshape(B, S, H, Dh)
    lse = jnp.transpose(lse, (0, 1, 4, 2, 3)).reshape(B, Lp, dil, H)[:, :L].reshape(B, S, H)
    return o, lse


def dilated_attention(q, k, v, t5_table):
    B, S, H, Dh = q.shape
    res = [dilated_branch(q, k, v, t5_table, w, d) for (w, d) in DIL_PATTERNS]
    outs = jnp.stack([r[0] for r in res]).astype(jnp.float32)
    lses = jnp.stack([r[1] for r in res])
    wts = jax.nn.softmax(lses, axis=0)
    y = jnp.einsum('pbsh,pbshd->bshd', wts, outs)
    return y.astype(q.dtype).reshape(B, S, H * Dh)


def attn_mixer(x, w_in, w_out, rpb, t5_table):
    B, S, _ = x.shape
    h = x @ w_in
    qa, ka, va, qb, kb, vb = jnp.split(h, 6, axis=-1)
    scale = HEAD_DIM ** -0.5
    hs = lambda t, nh: t.reshape(B, S, nh, HEAD_DIM)
    ya = neighbourhood_attention(hs(qa * scale, N_HEADS_NA), hs(ka, N_HEADS_NA), hs(va, N_HEADS_NA), rpb)
    yb = dilated_attention(hs(qb * scale, N_HEADS_DIL), hs(kb, N_HEADS_DIL), hs(vb, N_HEADS_DIL), t5_table)
    return jnp.concatenate([ya, yb], axis=-1) @ w_out


def conv_mixer(x, w_in, conf_dw_w, conf_dw_b, conf_ln_g, conf_ln_b, sconv_w, w_out):
    h = x @ w_in
    ca, cg, gb, gc, hx = jnp.split(
        h, [CONF_CH, 2 * CONF_CH, 2 * CONF_CH + SC_CH, 2 * CONF_CH + 2 * SC_CH], axis=-1)
    u = ca * jax.nn.sigmoid(cg)
    u = dwconv(u, conf_dw_w) + conf_dw_b
    u = jax.nn.silu(layer_norm(u, conf_ln_g, conf_ln_b))
    z = gb * dwconv(gc * hx, sconv_w)
    return jnp.concatenate([u, z], axis=-1) @ w_out


def conv_ffn(x, w_up, dw_w, w_down):
    h = dwconv(x @ w_up, dw_w)
    g, u = jnp.split(h, 2, axis=-1)
    return (jax.nn.silu(g) * u) @ w_down


def setup_inputs(seed: int = 0) -> dict:
    key = jax.random.key(seed)
    ks = jax.random.split(key, 20)
    n = lambda k, s, sc: jax.random.normal(k, s, jnp.float32) * sc
    return {
        "x": n(ks[0], (BATCH, SEQ, D_MODEL), 1.0),
        "t5_bias": n(ks[1], (N_BUCKETS, N_HEADS_DIL), 0.1),
        "attn_w_in": n(ks[2], (N_ATTN_LAYERS, D_MODEL, 3 * ATTN_W), D_MODEL ** -0.5),
        "attn_w_out": n(ks[3], (N_ATTN_LAYERS, ATTN_W, D_MODEL), ATTN_W ** -0.5 * BETA),
        "na_rpb": n(ks[4], (N_ATTN_LAYERS, N_HEADS_NA, 2 * NA_KH_MAX - 1, 2 * NA_KW - 1), 0.1),
        "conv_w_in": n(ks[5], (N_CONV_LAYERS, D_MODEL, 2 * CONF_CH + 3 * SC_CH), D_MODEL ** -0.5),
        "conf_dw_w": n(ks[6], (N_CONV_LAYERS, CONF_K, CONF_CH), CONF_K ** -0.5),
        "conf_dw_b": n(ks[7], (N_CONV_LAYERS, CONF_CH), 0.01),
        "conf_ln_g": 1.0 + n(ks[8], (N_CONV_LAYERS, CONF_CH), 0.01),
        "conf_ln_b": n(ks[9], (N_CONV_LAYERS, CONF_CH), 0.01),
        "sconv_w": n(ks[10], (N_CONV_LAYERS, SC_K, SC_CH), SC_K ** -0.5),
        "conv_w_out": n(ks[11], (N_CONV_LAYERS, CONV_W, D_MODEL), CONV_W ** -0.5 * BETA),
        "ffn_w_up": n(ks[12], (DEPTH, D_MODEL, 2 * D_FF), D_MODEL ** -0.5),
        "ffn_dw_w": n(ks[13], (DEPTH, FFN_K, 2 * D_FF), FFN_K ** -0.5),
        "ffn_w_down": n(ks[14], (DEPTH, D_FF, D_MODEL), D_FF ** -0.5 * BETA),
        "mix_ln_g": 1.0 + n(ks[15], (DEPTH, D_MODEL), 0.01),
        "mix_ln_b": n(ks[16], (DEPTH, D_MODEL), 0.01),
        "ffn_ln_g": 1.0 + n(ks[17], (DEPTH, D_MODEL), 0.01),
        "ffn_ln_b": n(ks[18], (DEPTH, D_MODEL), 0.01),
    }


def reference(x, t5_bias, attn_w_in, attn_w_out, na_rpb, conv_w_in, conf_dw_w, conf_dw_b,
              conf_ln_g, conf_ln_b, sconv_w, conv_w_out, ffn_w_up, ffn_dw_w, ffn_w_down,
              mix_ln_g, mix_ln_b, ffn_ln_g, ffn_ln_b):
    for i in range(DEPTH):
        j = i // 2
        if i % 2 == 0:
            y = attn_mixer(x, attn_w_in[j], attn_w_out[j], na_rpb[j], t5_bias)
        else:
            y = conv_mixer(x, conv_w_in[j], conf_dw_w[j], conf_dw_b[j], conf_ln_g[j],
                           conf_ln_b[j], sconv_w[j], conv_w_out[j])
        x = layer_norm(ALPHA * x + y, mix_ln_g[i], mix_ln_b[i])
        x = layer_norm(ALPHA * x + conv_ffn(x, ffn_w_up[i], ffn_dw_w[i], ffn_w_down[i]),
                       ffn_ln_g[i], ffn_ln_b[i])
    return x
```

```python
from contextlib import ExitStack

import numpy as np
import concourse.bass as bass
import concourse.mybir as mybir
from concourse.bass_utils import run_bass_kernel_spmd

F32 = mybir.dt.float32
BF16 = mybir.dt.bfloat16
AF = mybir.ActivationFunctionType
ALU = mybir.AluOpType
AX = mybir.AxisListType

D = 1024
KC = D // 128
SEQ = 8192
BATCH = 2
NCORES = 8
TOK = 2048
DFF = 2816
NPAIR = DFF // 128
ALPHA = 8.0 ** 0.25
LN_EPS = 1e-5
import os
SAME_SYNC_ENGINES = set(os.environ.get('SAME_SYNC', 'pe,act,dve,pool,sp').split(','))
NDMA = 24


class Buf:
    __slots__ = ("name", "w", "r")

    def __init__(self, name):
        self.name = name
        self.w = None
        self.r = []


class KB:
    def __init__(self, nc, es):
        self.nc = nc
        self.es = es
        self.E = {"pe": nc.tensor, "act": nc.scalar, "dve": nc.vector, "pool": nc.gpsimd, "sp": nc.sync}
        self.sems = {}
        self.cnt = {}
        for k in ("pe", "act", "dve", "pool"):
            self.sems[k] = es.enter_context(nc.semaphore("s_" + k))
            self.cnt[k] = 0
        for i in range(NDMA):
            self.sems[("dma", i)] = es.enter_context(nc.semaphore("s_dma%d" % i))
            self.cnt[("dma", i)] = 0
        NSW = 12
        for i in range(NSW):
            self.sems[("swdma", i)] = es.enter_context(nc.semaphore("s_swdma%d" % i))
            self.cnt[("swdma", i)] = 0
        self.nsw = NSW
        self.rr_sw = 0
        self.rr = 0
        self.seen = {e: {} for e in self.E}
        self.pending = {e: False for e in self.E}
        self.out_events = []
        self.nbuf = 0

    def buf(self, name=None):
        self.nbuf += 1
        return Buf(name or "b%d" % self.nbuf)

    def sb(self, name, shape, dt):
        return self.es.enter_context(self.nc.sbuf_tensor(name, list(shape), dt))

    def ps(self, name, shape, dt):
        return self.es.enter_context(self.nc.psum_tensor(name, list(shape), dt))

    def _wait(self, eng, key, val):
        if val <= 0:
            return
        if key == eng and (eng not in SAME_SYNC_ENGINES or val > self.cnt[eng]):
            return
        if self.seen[eng].get(key, 0) >= val:
            return
        self.E[eng].wait_ge(self.sems[key], val)
        self.seen[eng][key] = val

    def _deps(self, eng, reads, writes):
        need = {}

        def add(ev):
            if ev is None:
                return
            k, v = ev
            if need.get(k, 0) < v:
                need[k] = v

        for b in reads:
            add(b.w)
        for b in writes:
            add(b.w)
            for r in b.r:
                add(r)
        for k, v in need.items():
            self._wait(eng, k, v)

    def _record(self, ev, reads, writes):
        for b in reads:
            b.r.append(ev)
            if len(b.r) > 64:
                mx = {}
                for k, v in b.r:
                    if mx.get(k, 0) < v:
                        mx[k] = v
                b.r = list(mx.items())
        for b in writes:
            b.w = ev
            b.r = []

    def op(self, eng, fn, reads=(), writes=(), sig=True):
        self._deps(eng, reads, writes)
        ins = fn(self.E[eng])
        if sig:
            self.cnt[eng] += 1
            ins.then_inc(self.sems[eng], 1)
            ev = (eng, self.cnt[eng])
            self.pending[eng] = False
        else:
            ev = (eng, self.cnt[eng] + 1)
            self.pending[eng] = True
        self._record(ev, reads, writes)
        return ev

    def dma(self, fn, reads=(), writes=(), q="sp", is_output=False):
        self._deps(q, reads, writes)
        if q == "pool":
            i = self.rr_sw
            self.rr_sw = (self.rr_sw + 1) % self.nsw
            key = ("swdma", i)
        else:
            i = self.rr
            self.rr = (self.rr + 1) % NDMA
            key = ("dma", i)
        self._wait(q, key, self.cnt[key])
        ins = fn(self.E[q])
        self.cnt[key] += 16
        ins.then_inc(self.sems[key], 16)
        ev = (key, self.cnt[key])
        self._record(ev, reads, writes)
        if is_output:
            self.out_events.append(ev)
        return ev

    def barrier(self):
        for e in ("pe", "act", "dve", "pool", "sp"):
            for k, v in self.cnt.items():
                if k != e:
                    self._wait(e, k, v)

    def finish(self):
        for e, p in self.pending.items():
            assert not p, "engine %s has unsignaled trailing ops" % e
        for k, v in self.out_events:
            self._wait("sp", k, v)


class Common:
    def __init__(self, kb):
        nc = kb.nc
        self.kb = kb
        self.bankpair = [kb.ps("bankpair%d" % i, [128, 1024], F32) for i in range(4)]
        self.banks = []
        for bp in self.bankpair:
            self.banks += [bp[:, 0:512], bp[:, 512:1024]]
        self.bank_b = [kb.buf("bank%d" % i) for i in range(8)]
        self.ident_f = kb.sb("ident_f", [128, 128], F32)
        self.ident_b = kb.sb("ident_b", [128, 128], BF16)
        self.b_ident = kb.buf("ident")
        self.ones_f = kb.sb("ones_f", [128, 128], F32)

        kb.op("pool", lambda e: e.memset(self.ones_f[:], 1.0), writes=[self.b_ident])
        kb.op("pool", lambda e: e.affine_select(
            out=self.ident_f[:], in_=self.ones_f[:], pattern=[[-1, 128]], compare_op=ALU.is_equal,
            fill=0.0, base=0, channel_multiplier=1), reads=[self.b_ident], writes=[self.b_ident])
        kb.op("pool", lambda e: e.tensor_copy(out=self.ident_b[:], in_=self.ident_f[:]),
              reads=[self.b_ident], writes=[self.b_ident])


def load_transpose_tile(kb, cm, x_src, xT, col0, b_xT, xtile, b_xtile, banks, evac_eng="act"):
    kb.dma(lambda e: e.dma_start(out=xtile[:], in_=x_src), writes=[b_xtile])
    transpose_tile(kb, cm, xtile, b_xtile, xT, col0, b_xT, banks, evac_eng)


def transpose_tile(kb, cm, xtile, b_xtile, xT, col0, b_xT, banks, evac_eng="act"):
    for half in range(2):
        bk = banks[half]
        for j in range(4):
            c = half * 4 + j
            kb.op("pe", lambda e, c=c, j=j, bk=bk: e.transpose(
                out=cm.banks[bk][:, j * 128:(j + 1) * 128], in_=xtile[:, c * 128:(c + 1) * 128],
                identity=cm.ident_f[:]),
                reads=[b_xtile, cm.b_ident], writes=[cm.bank_b[bk]], sig=(j == 3))
        src = cm.banks[bk][:].rearrange("p (c t) -> p c t", c=4)
        dst = xT[:, half * 4:(half + 1) * 4, col0:col0 + 128]
        if evac_eng == "act":
            kb.op("act", lambda e, src=src, dst=dst: e.copy(out=dst, in_=src),
                  reads=[cm.bank_b[bk]], writes=[b_xT])
        else:
            kb.op("dve", lambda e, src=src, dst=dst: e.tensor_copy(out=dst, in_=src),
                  reads=[cm.bank_b[bk]], writes=[b_xT])


def load_transpose_bf16(kb, cm, src_rows, rows, xtile, b_xtile, xT, col0, b_xT, bank, evac_eng):
    if rows < 128:
        kb.op("dve", lambda e: e.memset(xtile[:], 0.0), writes=[b_xtile])
    kb.dma(lambda e: e.dma_start(out=xtile[0:rows, :], in_=src_rows), writes=[b_xtile], q="pool")
    pv = cm.banks[bank].bitcast(BF16)
    for c in range(KC):
        kb.op("pe", lambda e, c=c: e.transpose(out=pv[:, c * 128:(c + 1) * 128], in_=xtile[:, c * 128:(c + 1) * 128],
                                               identity=cm.ident_b[:]),
              reads=[b_xtile, cm.b_ident], writes=[cm.bank_b[bank]], sig=(c == KC - 1))
    src = pv[:, 0:KC * 128].rearrange("p (c t) -> p c t", c=KC)
    dst = xT[:, :, col0:col0 + 128]
    if evac_eng == "act":
        kb.op("act", lambda e: e.copy(out=dst, in_=src), reads=[cm.bank_b[bank]], writes=[b_xT])
    else:
        kb.op("dve", lambda e: e.tensor_copy(out=dst, in_=src), reads=[cm.bank_b[bank]], writes=[b_xT])


def emit_epilogue(kb, cm, lnp, lns, ntiles, xh, row0, out, xres, b_xres, otile, b_ot, mm_fn, yb=(6, 7)):
    def load(tg):
        s = tg % 2
        kb.dma(lambda e: e.dma_start(out=xres[s][:], in_=xh[row0 + tg * 128:row0 + (tg + 1) * 128, :]),
               writes=[b_xres[s]])
    load(0)
    for tg in range(ntiles):
        s = tg % 2
        if tg + 1 < ntiles:
            load(tg + 1)
        mm_fn(tg, yb)
        residual_ln(kb, cm, lns, lnp, xres[s], b_xres[s], yb, otile[s][:], b_ot[s])
        kb.dma(lambda e, s=s, tg=tg: e.dma_start(out=out[tg * 128:(tg + 1) * 128, :], in_=otile[s][:]),
               reads=[b_ot[s]], is_output=True)


class LNParams:
    def __init__(self, kb, name, g_ap, b_ap):
        self.g = kb.sb(name + "_g", [128, D], F32)
        self.b = kb.sb(name + "_b", [128, D], F32)
        self.buf = kb.buf(name)
        kb.dma(lambda e: e.dma_start(out=self.g[:], in_=g_ap.partition_broadcast(128)), writes=[self.buf])
        kb.dma(lambda e: e.dma_start(out=self.b[:], in_=b_ap.partition_broadcast(128)), writes=[self.buf])


class LNScratch:
    def __init__(self, kb, name, n=2):
        self.n = n
        self.i = 0
        self.z = [kb.sb("%s_z%d" % (name, i), [128, D], F32) for i in range(n)]
        self.st = [kb.sb("%s_st%d" % (name, i), [128, 12], F32) for i in range(n)]
        self.mv = [kb.sb("%s_mv%d" % (name, i), [128, 4], F32) for i in range(n)]
        self.bz = [kb.buf("%s_bz%d" % (name, i)) for i in range(n)]
        self.bs = [kb.buf("%s_bs%d" % (name, i)) for i in range(n)]

    def next(self):
        i = self.i
        self.i = (self.i + 1) % self.n
        return i


def residual_ln(kb, cm, lns, lnp, xres, b_xres, ybanks, out_tile, b_out):
    i = lns.next()
    z, st, mv, bz, bs = lns.z[i], lns.st[i], lns.mv[i], lns.bz[i], lns.bs[i]
    for h in range(2):
        kb.op("dve", lambda e, h=h: e.scalar_tensor_tensor(
            out=z[:, h * 512:(h + 1) * 512], in0=xres[:, h * 512:(h + 1) * 512], scalar=ALPHA,
            in1=cm.banks[ybanks[h]][:], op0=ALU.mult, op1=ALU.add),
            reads=[b_xres, cm.bank_b[ybanks[h]]], writes=[bz])
    ln_core(kb, lns, i, lnp, out_tile, b_out)


def ln_core(kb, lns, i, lnp, out_tile, b_out, width=D):
    z, st, mv, bz, bs = lns.z[i], lns.st[i], lns.mv[i], lns.bz[i], lns.bs[i]
    nch = width // 512
    for h in range(nch):
        kb.op("dve", lambda e, h=h: e.bn_stats(out=st[:, h * 6:(h + 1) * 6], in_=z[:, h * 512:(h + 1) * 512]),
              reads=[bz], writes=[bs])
    kb.op("dve", lambda e: e.bn_aggr(out=mv[:, 0:2], in_=st[:, 0:6 * nch]), reads=[bs], writes=[bs])
    kb.op("dve", lambda e: e.tensor_scalar_add(out=mv[:, 2:3], in0=mv[:, 1:2], scalar1=LN_EPS), reads=[bs], writes=[bs])
    kb.op("act", lambda e: e.activation(out=mv[:, 2:3], in_=mv[:, 2:3], func=AF.Sqrt), reads=[bs], writes=[bs])
    kb.op("dve", lambda e: e.reciprocal(out=mv[:, 2:3], in_=mv[:, 2:3]), reads=[bs], writes=[bs])
    kb.op("dve", lambda e: e.scalar_tensor_tensor(out=mv[:, 3:4], in0=mv[:, 0:1], scalar=-1.0, in1=mv[:, 2:3],
                                                  op0=ALU.mult, op1=ALU.mult), reads=[bs], writes=[bs])
    kb.op("act", lambda e: e.activation(out=z[:, 0:width], in_=z[:, 0:width], func=AF.Identity,
                                        bias=mv[:, 3:4], scale=mv[:, 2:3]),
          reads=[bz, bs], writes=[bz])
    kb.op("dve", lambda e: e.tensor_tensor(out=z[:, 0:width], in0=z[:, 0:width], in1=lnp.g[:, 0:width], op=ALU.mult),
          reads=[bz, lnp.buf], writes=[bz])
    kb.op("pool", lambda e: e.tensor_tensor(out=out_tile, in0=z[:, 0:width], in1=lnp.b[:, 0:width], op=ALU.add),
          reads=[bz, lnp.buf], writes=[b_out])


def emit_ffn(kb, cm, xh, out, w_up, dw, w_down, ln_g, ln_b, ntok=TOK, npair=NPAIR, half_tok=1024):
    nc = kb.nc
    ncols = ntok + 2
    xT = kb.sb("f_xT", [128, KC, ncols + 126], BF16)
    ntile_in = (ncols + 127) // 128
    b_xT = [kb.buf("f_xT%d" % i) for i in range(ntile_in)]
    xt_stage = [kb.sb("f_xst%d" % i, [128, D], F32) for i in range(2)]
    b_xst = [kb.buf("f_xst%d" % i) for i in range(2)]
    xbf = [kb.sb("f_xbf%d" % i, [128, D], BF16) for i in range(2)]
    b_xbf = [kb.buf("f_xbf%d" % i) for i in range(2)]
    lnp = LNParams(kb, "f_ln", ln_g, ln_b)
    lns = LNScratch(kb, "f_lns")
    ncht = dw.shape[1] // 128
    dwt = kb.sb("f_dw", [128, 3, ncht], F32)
    b_dw = kb.buf("f_dw")
    for k in range(3):
        kb.dma(lambda e, k=k: e.dma_start(out=dwt[:, k, :], in_=dw[k, :].rearrange("(c p) -> p c", p=128),
                                          allow_slow_non_contiguous=True), writes=[b_dw])
    wd = kb.sb("f_wd", [128, npair, D], BF16)
    b_wd = [kb.buf("f_wd%d" % c) for c in range(npair)]
    NWU = 3
    wu = [kb.sb("f_wu%d" % i, [128, KC, 2, 128], BF16) for i in range(NWU)]
    b_wu = [kb.buf("f_wu%d" % i) for i in range(NWU)]
    aT = kb.sb("f_aT", [128, npair, half_tok], BF16)
    blks = []
    o = 0
    while o < half_tok:
        n_ = min(510, half_tok - o)
        blks.append((o, n_))
        o += n_
    b_aT = [[kb.buf("f_aT%d_%d" % (c, b)) for b in range(len(blks))] for c in range(npair)]
    t1 = [[kb.sb("f_t1_%d_%d" % (i, j), [128, 512], F32) for j in range(2)] for i in range(2)]
    b_t1 = [[kb.buf("f_t1_%d_%d" % (i, j)) for j in range(2)] for i in range(2)]
    sg = [kb.sb("f_sg%d" % i, [128, 512], F32) for i in range(2)]
    b_sg = [kb.buf("f_sg%d" % i) for i in range(2)]
    xres, b_xres = xt_stage, b_xst
    otile = [kb.sb("f_ot%d" % i, [128, D], F32) for i in range(2)]
    b_ot = [kb.buf("f_ot%d" % i) for i in range(2)]

    for t in range(ntile_in):
        rows = min(128, ncols - t * 128)
        s = t % 2
        load_transpose_bf16(kb, cm, xh[t * 128:t * 128 + rows, :], rows, xbf[s], b_xbf[s], xT, t * 128, b_xT[t],
                            bank=6 + s, evac_eng=("act" if s == 0 else "dve"))

    w_down_v = w_down.rearrange("(c p) n -> p c n", p=128)
    nhalf = ntok // half_tok
    seq = [(hf, c) for hf in range(nhalf) for c in range(npair)]

    def load_pair(idx):
        hf_, c_ = seq[idx]
        s_ = idx % NWU
        kb.dma(lambda e: e.dma_start(out=wu[s_][:].rearrange("p a b c -> p (a b c)"), in_=w_up[c_]),
               writes=[b_wu[s_]], q="pool")
        if hf_ == 0:
            kb.dma(lambda e: e.dma_start(out=wd[:, c_, :], in_=w_down_v[:, c_, :]), writes=[b_wd[c_]], q="pool")

    load_pair(0)
    for idx, (hf, c) in enumerate(seq):
        if True:
            s = idx % NWU
            if idx + 1 < len(seq):
                load_pair(idx + 1)
            for b, (o0, nout) in enumerate(blks):
                col0 = hf * half_tok + o0
                nin = nout + 2
                rd_x = [b_xT[t] for t in range(col0 // 128, (col0 + nin - 1) // 128 + 1)]
                par = (idx * len(blks) + b) % 2
                bg, bu = (0, 1) if par == 0 else (2, 3)
                for gi, bk in ((0, bg), (1, bu)):
                    for kc in range(KC):
                        kb.op("pe", lambda e, kc=kc, gi=gi, bk=bk, s=s, col0=col0, nin=nin: e.matmul(
                            cm.banks[bk][:, 0:nin], lhsT=wu[s][:, kc, gi, :], rhs=xT[:, kc, col0:col0 + nin],
                            start=(kc == 0), stop=(kc == KC - 1)),
                            reads=[b_wu[s]] + rd_x, writes=[cm.bank_b[bk]], sig=(kc == KC - 1))
                for gi, bk in ((0, bg), (1, bu)):
                    ch = c if gi == 0 else (ncht // 2 + c)
                    T = t1[par][gi]
                    bT = b_t1[par][gi]
                    A = cm.banks[bk]
                    w0, w1, w2 = (dwt[:, k, ch:ch + 1] for k in range(3))
                    kb.op("act", lambda e, T=T, A=A, w1=w1, nout=nout: e.activation(
                        out=T[:, 0:nout], in_=A[:, 1:nout + 1], func=AF.Copy, scale=w1),
                        reads=[cm.bank_b[bk], b_dw], writes=[bT])
                    kb.op("dve", lambda e, T=T, A=A, w0=w0, nout=nout: e.scalar_tensor_tensor(
                        out=T[:, 0:nout], in0=A[:, 0:nout], scalar=w0, in1=T[:, 0:nout], op0=ALU.mult, op1=ALU.add),
                        reads=[cm.bank_b[bk], b_dw, bT], writes=[bT])
                    kb.op("dve", lambda e, T=T, A=A, w2=w2, nout=nout: e.scalar_tensor_tensor(
                        out=T[:, 0:nout], in0=A[:, 2:nout + 2], scalar=w2, in1=T[:, 0:nout], op0=ALU.mult, op1=ALU.add),
                        reads=[cm.bank_b[bk], b_dw, bT], writes=[bT])
                kb.op("act", lambda e, par=par, nout=nout: e.activation(out=sg[par][:, 0:nout], in_=t1[par][0][:, 0:nout],
                                                                        func=AF.Silu),
                      reads=[b_t1[par][0]], writes=[b_sg[par]])
                kb.op("pool", lambda e, par=par, c=c, o0=o0, nout=nout: e.tensor_tensor(
                    out=aT[:, c, o0:o0 + nout], in0=sg[par][:, 0:nout], in1=t1[par][1][:, 0:nout], op=ALU.mult),
                    reads=[b_sg[par], b_t1[par][1]], writes=[b_aT[c][b]])
        if c != npair - 1:
            continue
        def mm_fn(tt, yb):
            bqs = [bi for bi, (o0, n_) in enumerate(blks) if o0 < (tt + 1) * 128 and o0 + n_ > tt * 128]
            for h in range(2):
                for c in range(npair):
                    kb.op("pe", lambda e, c=c, h=h, tt=tt: e.matmul(
                        cm.banks[yb[h]][:], lhsT=aT[:, c, tt * 128:(tt + 1) * 128], rhs=wd[:, c, h * 512:(h + 1) * 512],
                        start=(c == 0), stop=(c == npair - 1)),
                        reads=[b_aT[c][bq] for bq in bqs] + [b_wd[c]], writes=[cm.bank_b[yb[h]]], sig=(c == npair - 1))
        r0 = hf * half_tok
        emit_epilogue(kb, cm, lnp, lns, half_tok // 128, xh, 1 + r0, out[r0:r0 + half_tok, :], xres, b_xres, otile, b_ot, mm_fn)


def DFF_COLS(npair):
    return npair * 128


def build_ffn(ntok=TOK, npair=NPAIR, half_tok=1024):
    nc = bass.Bass("TRN2", target_bir_lowering=False)
    xh = nc.dram_tensor("xh", [ntok + 2, D], F32, kind="ExternalInput").ap()
    w_up = nc.dram_tensor("w_up", [npair, 128, KC * 2 * 128], F32, kind="ExternalInput").ap()
    dw = nc.dram_tensor("dw", [3, 2 * npair * 128], F32, kind="ExternalInput").ap()
    w_down = nc.dram_tensor("w_down", [npair * 128, D], F32, kind="ExternalInput").ap()
    ln_g = nc.dram_tensor("ln_g", [D], F32, kind="ExternalInput").ap()
    ln_b = nc.dram_tensor("ln_b", [D], F32, kind="ExternalInput").ap()
    out = nc.dram_tensor("out", [ntok, D], F32, kind="ExternalOutput").ap()
    with ExitStack() as es:
        kb = KB(nc, es)
        cm = Common(kb)
        emit_ffn(kb, cm, xh, out, w_up, dw, w_down, ln_g, ln_b, ntok=ntok, npair=npair, half_tok=half_tok)
        kb.finish()
    return nc


def load_cols(kb, cm, src, dst, b_dst, bank, name, alloc=None):
    R, C = src.shape
    nch = C // 128
    assert R * nch <= 512
    st = (alloc or kb.sb)(name + "_rows", [R, C], F32)
    b_st = kb.buf(name + "_rows")
    kb.dma(lambda e: e.dma_start(out=st[:], in_=src), writes=[b_st])
    for c in range(nch):
        kb.op("pe", lambda e, c=c: e.transpose(out=cm.banks[bank][:, c * R:(c + 1) * R],
                                               in_=st[0:R, c * 128:(c + 1) * 128], identity=cm.ident_f[0:R, 0:R]),
              reads=[b_st, cm.b_ident], writes=[cm.bank_b[bank]], sig=(c == nch - 1))
    kb.op("dve", lambda e: e.tensor_copy(out=dst, in_=cm.banks[bank][:, 0:nch * R].rearrange("p (c r) -> p r c", r=R)),
          reads=[cm.bank_b[bank]], writes=[b_dst])


CH = 512
CJ = CH // 128
CK = 31
HALO_C = 16


def emit_conv(kb, cm, xh, out, w_in, cdw_w, cdw_b, cln_g, cln_b, sconv_w, w_out, ln_g, ln_b, ntok=TOK):
    nc = kb.nc
    es_par = ExitStack()
    es_ab = ExitStack()
    es_c = ExitStack()
    es_d = ExitStack()
    mk = lambda es_: (lambda name, shape, dt: es_.enter_context(nc.sbuf_tensor(name, list(shape), dt)))
    sb_par, sb_ab, sb_c, sb_d = mk(es_par), mk(es_ab), mk(es_c), mk(es_d)
    nrows = ntok + 2 * HALO_C
    ntile_in = (nrows + 127) // 128
    xT = kb.sb("c_xT", [128, KC, ntile_in * 128], BF16)
    b_xT = [kb.buf("c_xT%d" % i) for i in range(ntile_in)]
    xst = [kb.sb("c_xst%d" % i, [128, D], F32) for i in range(2)]
    b_xst = [kb.buf("c_xst%d" % i) for i in range(2)]
    xbf = [kb.sb("c_xbf%d" % i, [128, D], BF16) for i in range(2)]
    b_xbf = [kb.buf("c_xbf%d" % i) for i in range(2)]
    lnp = LNParams(kb, "c_ln", ln_g, ln_b)
    lns = LNScratch(kb, "c_lns", n=2)
    dwc = kb.sb("c_dwc", [128, CK, CJ], F32)
    b_par = kb.buf("c_par")
    vec = kb.sb("c_vec", [128, 3, CJ], F32)
    scw = kb.sb("c_scw", [128, 3, CJ], F32)
    dg = kb.sb("c_dg", [128, CK * CJ, 128], BF16)
    b_dg = kb.buf("c_dg")
    ones_m = kb.sb("c_ones", [128, 128], F32)
    b_ones = kb.buf("c_ones")
    wo = kb.sb("c_wo", [128, KC, D], BF16)
    b_wo = [kb.buf("c_wo%d" % c) for c in range(KC)]
    ucols = ntok + 30
    uT = kb.sb("c_uT", [128, CJ, ucols], BF16)
    nub = (ucols + 511) // 512
    b_uT = [[kb.buf("c_uT%d_%d" % (j, b)) for b in range(nub)] for j in range(CJ)]
    yT = kb.sb("c_yT", [128, 2 * CJ, ntok], BF16)
    nob = ntok // 512
    b_yT = [[kb.buf("c_yT%d_%d" % (j, b)) for b in range(nob)] for j in range(2 * CJ)]
    load_cols(kb, cm, cdw_w, dwc[:], b_par, 0, "c_dww", sb_par)
    load_cols(kb, cm, cdw_b.rearrange("(o c) -> o c", o=1), vec[:, 0:1, :], b_par, 1, "c_dwb", sb_par)
    load_cols(kb, cm, cln_g.rearrange("(o c) -> o c", o=1), vec[:, 1:2, :], b_par, 2, "c_lng", sb_par)
    load_cols(kb, cm, cln_b.rearrange("(o c) -> o c", o=1), vec[:, 2:3, :], b_par, 3, "c_lnb", sb_par)
    load_cols(kb, cm, sconv_w, scw[:], b_par, 4, "c_scw", sb_par)
    kb.barrier()
    es_par.close()
    for k in range(CK):
        for j in range(CJ):
            kb.op("dve", lambda e, k=k, j=j: e.tensor_scalar(
                out=dg[:, k * CJ + j, :], in0=cm.ident_f[:], scalar1=dwc[:, k, j:j + 1], scalar2=None, op0=ALU.mult),
                reads=[b_par, cm.b_ident], writes=[b_dg], sig=(k == CK - 1 and j == CJ - 1))
    kb.op("dve", lambda e: e.memset(ones_m[:], 1.0 / CH), writes=[b_ones])
    w_out_v = w_out.rearrange("(c p) n -> p c n", p=128)
    for c in range(KC):
        kb.dma(lambda e, c=c: e.dma_start(out=wo[:, c, :], in_=w_out_v[:, c, :]), writes=[b_wo[c]], q="pool")
    wi = [sb_ab("c_wi%d" % i, [128, 3, KC, 128], BF16) for i in range(2)]
    b_wi = [kb.buf("c_wi%d" % i) for i in range(2)]
    tmpA = [sb_ab("c_tmpA%d" % i, [128, 512], F32) for i in range(2)]
    b_tmpA = [kb.buf("c_tmpA%d" % i) for i in range(2)]
    pT = sb_ab("c_pT", [128, ntok + 2], F32)
    b_pT = kb.buf("c_pT")
    qT = sb_ab("c_qT", [128, ntok], F32)
    b_qT = kb.buf("c_qT")

    for t in range(ntile_in):
        rows = min(128, nrows - t * 128)
        s = t % 2
        load_transpose_bf16(kb, cm, xh[t * 128:t * 128 + rows, :], rows, xbf[s], b_xbf[s], xT, t * 128, b_xT[t],
                            bank=6 + s, evac_eng=("act" if s == 0 else "dve"))

    def xtiles(c0, n):
        return [b_xT[t] for t in range(c0 // 128, (c0 + n - 1) // 128 + 1)]

    def load_group(g):
        s = g % 2
        n = 2 if g < 4 else 3
        kb.dma(lambda e: e.dma_start(out=wi[s][:, 0:n, :, :].rearrange("p g k c -> p g (k c)"), in_=w_in[g][:, 0:n, :]),
               writes=[b_wi[s]], q="pool")
        return s

    def proj(s, i, bank, xc0, n):
        for kc in range(KC):
            kb.op("pe", lambda e, kc=kc: e.matmul(cm.banks[bank][:, 0:n], lhsT=wi[s][:, i, kc, :],
                                                  rhs=xT[:, kc, xc0:xc0 + n], start=(kc == 0), stop=(kc == KC - 1)),
                  reads=[b_wi[s]] + xtiles(xc0, n), writes=[cm.bank_b[bank]], sig=(kc == KC - 1))

    it = 0
    for j in range(CJ):
        s = load_group(j)
        for b in range(nub):
            u0 = b * 512
            n = min(512, ucols - u0)
            par = it % 2
            it += 1
            ba, bg = (0, 1) if par == 0 else (2, 3)
            proj(s, 0, ba, u0 + 1, n)
            proj(s, 1, bg, u0 + 1, n)
            kb.op("act", lambda e, par=par, bg=bg, n=n: e.activation(out=tmpA[par][:, 0:n], in_=cm.banks[bg][:, 0:n],
                                                                     func=AF.Sigmoid),
                  reads=[cm.bank_b[bg]], writes=[b_tmpA[par]])
            kb.op("dve", lambda e, par=par, ba=ba, n=n, j=j, u0=u0: e.tensor_tensor(
                out=uT[:, j, u0:u0 + n], in0=cm.banks[ba][:, 0:n], in1=tmpA[par][:, 0:n], op=ALU.mult),
                reads=[cm.bank_b[ba], b_tmpA[par]], writes=[b_uT[j][b]])

    pcols = ntok + 2
    npb = (pcols + 511) // 512
    for j in range(CJ):
        s = load_group(4 + j)
        for b in range(npb):
            p0 = b * 512
            n = min(512, pcols - p0)
            par = it % 2
            it += 1
            ba, bg = (0, 1) if par == 0 else (2, 3)
            proj(s, 1, ba, p0 + 15, n)
            proj(s, 2, bg, p0 + 15, n)
            kb.op("act", lambda e, par=par, bg=bg, n=n: e.copy(out=tmpA[par][:, 0:n], in_=cm.banks[bg][:, 0:n]),
                  reads=[cm.bank_b[bg]], writes=[b_tmpA[par]])
            kb.op("dve", lambda e, par=par, ba=ba, n=n, p0=p0: e.tensor_tensor(
                out=pT[:, p0:p0 + n], in0=cm.banks[ba][:, 0:n], in1=tmpA[par][:, 0:n], op=ALU.mult),
                reads=[cm.bank_b[ba], b_tmpA[par]], writes=[b_pT])
        w0, w1, w2 = (scw[:, k, j:j + 1] for k in range(3))
        kb.op("act", lambda e, w1=w1: e.activation(out=qT[:, 0:ntok], in_=pT[:, 1:ntok + 1], func=AF.Copy, scale=w1),
              reads=[b_pT, b_par], writes=[b_qT])
        kb.op("dve", lambda e, w0=w0: e.scalar_tensor_tensor(out=qT[:, 0:ntok], in0=pT[:, 0:ntok], scalar=w0,
                                                             in1=qT[:, 0:ntok], op0=ALU.mult, op1=ALU.add),
              reads=[b_pT, b_par, b_qT], writes=[b_qT])
        kb.op("dve", lambda e, w2=w2: e.scalar_tensor_tensor(out=qT[:, 0:ntok], in0=pT[:, 2:ntok + 2], scalar=w2,
                                                             in1=qT[:, 0:ntok], op0=ALU.mult, op1=ALU.add),
              reads=[b_pT, b_par, b_qT], writes=[b_qT])
        for b in range(nob):
            bk = 4 + (b % 2)
            proj(s, 0, bk, b * 512 + 16, 512)
            kb.op("dve", lambda e, bk=bk, b=b, j=j: e.tensor_tensor(
                out=yT[:, CJ + j, b * 512:(b + 1) * 512], in0=cm.banks[bk][:], in1=qT[:, b * 512:(b + 1) * 512],
                op=ALU.mult),
                reads=[cm.bank_b[bk], b_qT], writes=[b_yT[CJ + j][b]])

    kb.barrier()
    es_ab.close()
    ub = sb_c("c_ub", [128, CJ, 512], F32)
    b_ub = [kb.buf("c_ub%d" % j) for j in range(CJ)]
    ub2 = sb_c("c_ub2", [128, CJ, 512], F32)
    b_ub2 = [kb.buf("c_ub2%d" % j) for j in range(CJ)]
    st_m = sb_c("c_stm", [128, 512], F32)
    st_r = sb_c("c_str", [128, 512], F32)
    b_st = kb.buf("c_st")
    for b in range(nob):
        for j in range(CJ):
            bk = j % 2
            rd = [b_uT[j][bb] for bb in range(nub) if bb * 512 < b * 512 + 542 and (bb + 1) * 512 > b * 512]
            for k in range(CK):
                kb.op("pe", lambda e, k=k, j=j, bk=bk, b=b: e.matmul(
                    cm.banks[bk][:], lhsT=dg[:, k * CJ + j, :], rhs=uT[:, j, b * 512 + k:b * 512 + k + 512],
                    start=(k == 0), stop=(k == CK - 1)),
                    reads=[b_dg] + rd, writes=[cm.bank_b[bk]], sig=(k == CK - 1))
            kb.op("act", lambda e, j=j, bk=bk: e.activation(out=ub[:, j, :], in_=cm.banks[bk][:], func=AF.Identity,
                                                            bias=vec[:, 0, j:j + 1], scale=1.0),
                  reads=[cm.bank_b[bk], b_par], writes=[b_ub[j]])
            kb.op("act", lambda e, j=j: e.activation(out=ub2[:, j, :], in_=ub[:, j, :], func=AF.Square),
                  reads=[b_ub[j]], writes=[b_ub2[j]])
        for j in range(CJ):
            kb.op("pe", lambda e, j=j: e.matmul(cm.banks[2][:], lhsT=ones_m[:], rhs=ub[:, j, :],
                                                start=(j == 0), stop=(j == CJ - 1)),
                  reads=[b_ones, b_ub[j]], writes=[cm.bank_b[2]], sig=(j == CJ - 1))
        for j in range(CJ):
            kb.op("pe", lambda e, j=j: e.matmul(cm.banks[3][:], lhsT=ones_m[:], rhs=ub2[:, j, :],
                                                start=(j == 0), stop=(j == CJ - 1)),
                  reads=[b_ones, b_ub2[j]], writes=[cm.bank_b[3]], sig=(j == CJ - 1))
        kb.op("act", lambda e: e.copy(out=st_m[:], in_=cm.banks[2][:]), reads=[cm.bank_b[2]], writes=[b_st])
        kb.op("dve", lambda e: e.tensor_tensor(out=st_r[:], in0=st_m[:], in1=st_m[:], op=ALU.mult),
              reads=[b_st], writes=[b_st])
        kb.op("dve", lambda e: e.tensor_tensor(out=st_r[:], in0=cm.banks[3][:], in1=st_r[:], op=ALU.subtract),
              reads=[b_st, cm.bank_b[3]], writes=[b_st])
        kb.op("dve", lambda e: e.tensor_scalar_add(out=st_r[:], in0=st_r[:], scalar1=LN_EPS), reads=[b_st], writes=[b_st])
        kb.op("act", lambda e: e.activation(out=st_r[:], in_=st_r[:], func=AF.Sqrt), reads=[b_st], writes=[b_st])
        kb.op("dve", lambda e: e.reciprocal(out=st_r[:], in_=st_r[:]), reads=[b_st], writes=[b_st])
        for j in range(CJ):
            kb.op("dve", lambda e, j=j: e.tensor_tensor(out=ub[:, j, :], in0=ub[:, j, :], in1=st_m[:], op=ALU.subtract),
                  reads=[b_ub[j], b_st], writes=[b_ub[j]])
            kb.op("pool", lambda e, j=j: e.tensor_tensor(out=ub[:, j, :], in0=ub[:, j, :], in1=st_r[:], op=ALU.mult),
                  reads=[b_ub[j], b_st], writes=[b_ub[j]])
            kb.op("act", lambda e, j=j, b=b: e.activation(out=yT[:, j, b * 512:(b + 1) * 512], in_=ub[:, j, :],
                                                          func=AF.Silu, bias=vec[:, 2, j:j + 1], scale=vec[:, 1, j:j + 1]),
                  reads=[b_ub[j], b_par], writes=[b_yT[j][b]])

    kb.barrier()
    es_c.close()
    otile = [sb_d("c_ot%d" % i, [128, D], F32) for i in range(2)]
    b_ot = [kb.buf("c_ot%d" % i) for i in range(2)]
    def mm_fn(tg, yb):
        for h in range(2):
            for c in range(2 * CJ):
                kb.op("pe", lambda e, c=c, h=h, tg=tg: e.matmul(
                    cm.banks[yb[h]][:], lhsT=yT[:, c, tg * 128:(tg + 1) * 128], rhs=wo[:, c, h * 512:(h + 1) * 512],
                    start=(c == 0), stop=(c == 2 * CJ - 1)),
                    reads=[b_yT[c][tg // 4], b_wo[c]], writes=[cm.bank_b[yb[h]]], sig=(c == 2 * CJ - 1))
    emit_epilogue(kb, cm, lnp, lns, ntok // 128, xh, HALO_C, out, xst, b_xst, otile, b_ot, mm_fn)
    kb.barrier()
    es_d.close()


def build_conv(ntok=TOK):
    nc = bass.Bass("TRN2", target_bir_lowering=False)
    di = lambda n, s: nc.dram_tensor(n, s, F32, kind="ExternalInput").ap()
    xh = di("xh", [ntok + 2 * HALO_C, D])
    w_in = di("w_in", [8, 128, 3, D])
    cdw_w = di("cdw_w", [CK, CH])
    cdw_b = di("cdw_b", [CH])
    cln_g = di("cln_g", [CH])
    cln_b = di("cln_b", [CH])
    sconv_w = di("sconv_w", [3, CH])
    w_out = di("w_out", [D, D])
    ln_g = di("ln_g", [D])
    ln_b = di("ln_b", [D])
    out = nc.dram_tensor("out", [ntok, D], F32, kind="ExternalOutput").ap()
    with ExitStack() as es:
        kb = KB(nc, es)
        cm = Common(kb)
        emit_conv(kb, cm, xh, out, w_in, cdw_w, cdw_b, cln_g, cln_b, sconv_w, w_out, ln_g, ln_b, ntok=ntok)
        kb.finish()
    return nc


HA = 1024
NTA = TOK + 2 * HA
NEG = -1e30
DILS = (1, 4, 16)
NA_OFFS = {0: (-2, -1, 0, 1, 2, 3), 15: (-3, -2, -1, 0, 1, 2)}
NA_DEF = (-2, -1, 0, 1, 2)
NA_NV = 22


def na_mask_index():
    idx = {}
    n = 0
    for cls, offs in (("I", NA_DEF), (0, NA_OFFS[0]), (1, NA_DEF), (14, NA_DEF), (15, NA_OFFS[15])):
        for d in offs:
            idx[(cls, d)] = n
            n += 1
    return idx, n


def dil_vtiles():
    tiles = []
    for dil in DILS:
        nq = TOK // dil // 128
        for r in range(dil):
            for s in range(nq + 1):
                tiles.append((dil, r, 128 * s - 64))
    return tiles


def emit_attn(kb, cm, xh, out, w_in, w_out, bias_all, ln_g, ln_b, n_pairs=8):
    nc = kb.nc
    dvt = dil_vtiles()
    dvt_idx = {t: i for i, t in enumerate(dvt)}
    NVT = len(dvt)
    mask_idx, nmask = na_mask_index()

    xT = kb.sb("a_xT", [128, KC, NTA], BF16)
    b_xT = [kb.buf("a_xT%d" % i) for i in range(NTA // 128)]
    yT = kb.sb("a_yT", [128, KC, TOK], BF16)
    b_yT = [kb.buf("a_yT%d" % i) for i in range(KC)]
    onesp = kb.sb("a_onesp", [128, 2, 128], BF16)
    b_onesp = kb.buf("a_onesp")
    kb.op("dve", lambda e: e.memset(onesp[:], 0.0), writes=[b_onesp])
    kb.op("dve", lambda e: e.memset(onesp[:, 0, 0:64], 1.0), writes=[b_onesp])
    kb.op("dve", lambda e: e.memset(onesp[:, 1, 64:128], 1.0), writes=[b_onesp])
    ones_b = kb.sb("a_ones_b", [128, 128], BF16)
    kb.op("dve", lambda e: e.memset(ones_b[:], 1.0), writes=[b_onesp])

    with ExitStack() as es2:
        sb2 = lambda name, shape, dt: es2.enter_context(nc.sbuf_tensor(name, list(shape), dt))
        with ExitStack() as es0:
            xbf = [es0.enter_context(nc.sbuf_tensor("a_xbf%d" % i, [128, D], BF16)) for i in range(2)]
            b_xbf = [kb.buf("a_xbf%d" % i) for i in range(2)]
            for t in range(NTA // 128):
                s = t % 2
                load_transpose_bf16(kb, cm, xh[t * 128:(t + 1) * 128, :], 128, xbf[s], b_xbf[s], xT, t * 128, b_xT[t],
                                    bank=6 + s, evac_eng=("act" if s == 0 else "dve"))
        kb.barrier()

        w_bf = [sb2("a_wbf%d" % i, [128, 3, KC, 128], BF16) for i in range(2)]
        b_wbf = [kb.buf("a_wbf%d" % i) for i in range(2)]
        QT = sb2("a_QT", [128, TOK], BF16)
        b_QT = [kb.buf("a_QT%d" % i) for i in range(TOK // 512)]
        KT = sb2("a_KT", [128, NTA], BF16)
        b_KT = [kb.buf("a_KT%d" % i) for i in range(NTA // 512)]
        VT = sb2("a_VT", [128, NTA], BF16)
        b_VT = [kb.buf("a_VT%d" % i) for i in range(NTA // 512)]
        Vp = sb2("a_Vp", [128, NVT, 2, 128], BF16)
        b_Vp = [kb.buf("a_Vp%d" % i) for i in range(NVT)]
        kb.op("pool", lambda e: e.memset(Vp[:], 0.0), writes=b_Vp)
        NCOMB = nmask
        bias_t = sb2("a_bias", [128, NCOMB, 2, 128], BF16)
        b_bias = kb.buf("a_bias")
        NGS = 6
        PT = sb2("a_PT", [128, 2 * NGS, 2, 128], BF16)
        b_PT = [kb.buf("a_PT%d" % i) for i in range(NGS)]
        accN = sb2("a_accN", [128, TOK], F32)
        accD = sb2("a_accD", [128, TOK], F32)
        b_acc = kb.buf("a_acc")
        rden = [sb2("a_rden%d" % i, [128, 128], F32) for i in range(2)]
        b_rden = [kb.buf("a_rden%d" % i) for i in range(2)]

        def load_pair_params(pr_):
            s_ = pr_ % 2
            kb.dma(lambda e: e.dma_start(out=w_bf[s_][:].rearrange("p g k c -> p g (k c)"), in_=w_in[pr_]),
                   writes=[b_wbf[s_]], q="pool")

        def load_pair_bias(pr_):
            ncomb = NCOMB if pr_ < 4 else 24
            for m0 in range(0, ncomb, 7):
                m1 = min(ncomb, m0 + 7)
                kb.dma(lambda e, m0=m0, m1=m1: e.dma_start(
                    out=bias_t[:, m0:m1, :, :].rearrange("p m h q -> p (m h q)"),
                    in_=bias_all[pr_][:, m0:m1, :, :].rearrange("p m h q -> p (m h q)")),
                    writes=[b_bias], q="pool")

        load_pair_params(0)
        b_S = [kb.buf("a_S%d" % i) for i in range(2)]
        ND_slots = [(2, 0), (3, 0)]
        cnt = {"w": 0, "step": 0, "job": 0, "pj": 0}

        def blocks(c0, c1, width, bufs):
            return [bufs[i] for i in range(c0 // width, (c1 - 1) // width + 1)]

        for pr in range(n_pairs):
            is_na = pr < 4
            hp = pr % 4
            s = pr % 2
            load_pair_bias(pr)
            if pr + 1 < n_pairs:
                load_pair_params(pr + 1)
            def projT(i, dst, dst_bufs, c_lo, c_hi, xoff, scale, eng="dve"):
                for c0 in range(c_lo, c_hi, 512):
                    n = min(512, c_hi - c0)
                    bk = 4 + (cnt["pj"] % 2)
                    cnt["pj"] += 1
                    for kc in range(KC):
                        kb.op("pe", lambda e, kc=kc, bk=bk, c0=c0, n=n: e.matmul(
                            cm.banks[bk][:, 0:n], lhsT=w_bf[s][:, i, kc, :], rhs=xT[:, kc, c0 + xoff:c0 + xoff + n],
                            start=(kc == 0), stop=(kc == KC - 1)),
                            reads=[b_wbf[s]] + blocks(c0 + xoff, c0 + xoff + n, 128, b_xT), writes=[cm.bank_b[bk]],
                            sig=(kc == KC - 1))
                    if scale is None and eng == "dve":
                        kb.op("dve", lambda e, bk=bk, c0=c0, n=n: e.tensor_copy(out=dst[:, c0:c0 + n], in_=cm.banks[bk][:, 0:n]),
                              reads=[cm.bank_b[bk]], writes=blocks(c0, c0 + n, 512, dst_bufs))
                    elif scale is None:
                        kb.op("act", lambda e, bk=bk, c0=c0, n=n: e.copy(out=dst[:, c0:c0 + n], in_=cm.banks[bk][:, 0:n]),
                              reads=[cm.bank_b[bk]], writes=blocks(c0, c0 + n, 512, dst_bufs))
                    else:
                        kb.op("act", lambda e, bk=bk, c0=c0, n=n: e.mul(out=dst[:, c0:c0 + n], in_=cm.banks[bk][:, 0:n], mul=scale),
                              reads=[cm.bank_b[bk]], writes=blocks(c0, c0 + n, 512, dst_bufs))

            projT(0, QT, b_QT, 0, TOK, HA, 0.125)
            if is_na:
                projT(1, KT, b_KT, HA - 384, HA + TOK + 384, 0, None)
                projT(2, VT, b_VT, HA - 384, HA + TOK + 384, 0, None, eng="act")
            else:
                projT(1, KT, b_KT, 0, NTA, 0, None)
                projT(2, VT, b_VT, 0, NTA, 0, None, eng="act")

            def vtile(slot, xc0, step):
                bk = 4 + (cnt["pj"] % 2)
                cnt["pj"] += 1
                hi = xc0 + step * 127 + 1
                pv = cm.banks[bk].bitcast(BF16)
                kb.op("pe", lambda e: e.transpose(out=pv[:, 0:128], in_=VT[:, xc0:hi:step], identity=cm.ident_b[:]),
                      reads=blocks(xc0, hi, 512, b_VT) + [cm.b_ident], writes=[cm.bank_b[bk]])
                kb.op("act", lambda e: e.copy(out=Vp[:, slot, 0, 0:64], in_=pv[:, 0:64]),
                      reads=[cm.bank_b[bk]], writes=[b_Vp[slot]])
                kb.op("dve", lambda e: e.tensor_copy(out=Vp[:, slot, 1, 64:128], in_=pv[:, 64:128]),
                      reads=[cm.bank_b[bk]], writes=[b_Vp[slot]])

            if is_na:
                for t in range(NA_NV):
                    vtile(t, HA + (t - 3) * 128, 1)
            else:
                for i, (dil, r, ms) in enumerate(dvt):
                    vtile(i, HA + dil * ms + r, dil)

            jobs = []
            if is_na:
                for qi in range(16):
                    offs = NA_OFFS.get(qi, NA_DEF)
                    cls = qi if qi in (0, 1, 14, 15) else "I"
                    steps = []
                    for d in offs:
                        steps.append(dict(kc0=HA + (qi + d) * 128, kstep=1, vslot=qi + d + 3, kvcol=qi + d + 3,
                                          comb=mask_idx[(cls, d)]))
                    jobs.append(dict(qc0=qi * 128, qstep=1, steps=steps, pat=None))
            else:
                for p, dil in enumerate(DILS):
                    for r in range(dil):
                        for j in range(TOK // dil // 128):
                            steps = []
                            nj = TOK // dil // 128
                            v = (1 if j == 0 else 0) + (2 if j == nj - 1 else 0)
                            for kt in range(2):
                                ms = 128 * (j + kt) - 64
                                vi = dvt_idx[(dil, r, ms)]
                                steps.append(dict(kc0=HA + dil * ms + r, kstep=dil, vslot=vi, kvcol=NA_NV + vi,
                                                  comb=p * 8 + v * 2 + kt))
                            jobs.append(dict(qc0=r + dil * 128 * j, qstep=dil, steps=steps, pat=p))


            def emit_group(jb, si0, nu):
                g = cnt["step"]
                cnt["step"] += 1
                ssl = g % 2
                gs = g % NGS
                bp = cm.bankpair[0 if ssl == 0 else 3]
                q0, qs = jb["qc0"], jb["qstep"]
                qhi = q0 + qs * 127 + 1
                for u in range(nu):
                    st = jb["steps"][si0 + u]
                    st["slot"] = 2 * gs + u
                    st["gs"] = gs
                    k0, ks = st["kc0"], st["kstep"]
                    khi = k0 + ks * 127 + 1
                    for h in range(2):
                        o = bp[:, h * 512 + u * 128:h * 512 + (u + 1) * 128]
                        kb.op("pe", lambda e, o=o, h=h, k0=k0, khi=khi, ks=ks: e.matmul(
                            o, lhsT=KT[h * 64:(h + 1) * 64, k0:khi:ks], rhs=QT[h * 64:(h + 1) * 64, q0:qhi:qs],
                            start=True, stop=True),
                            reads=blocks(k0, khi, 512, b_KT) + blocks(q0, qhi, 512, b_QT), writes=[b_S[ssl]],
                            sig=(h == 1 and u == nu - 1))
                c0 = jb["steps"][si0]["comb"]
                for u in range(nu):
                    assert jb["steps"][si0 + u]["comb"] == c0 + u
                sview = bp[:].rearrange("p (b u q) -> p b u q", b=2, q=128)[:, :, 0:nu, :]
                kb.op("dve", lambda e: e.tensor_tensor(out=sview, in0=sview,
                                                       in1=bias_t[:, c0:c0 + nu, :, :].rearrange("p u h q -> p h u q"),
                                                       op=ALU.add),
                      reads=[b_S[ssl], b_bias], writes=[b_S[ssl]])
                kb.op("act", lambda e: e.activation(out=PT[:, 2 * gs:2 * gs + nu, :, :].rearrange("p u h q -> p h u q"),
                                                    in_=sview, func=AF.Exp),
                      reads=[b_S[ssl]], writes=[b_PT[gs]])

            def emit_pv(jb):
                jb["nd"] = cnt["job"] % 2
                cnt["job"] += 1
                nbk, nco = ND_slots[jb["nd"]]
                ns = len(jb["steps"])
                o = cm.banks[nbk][:, nco:nco + 128]
                for si in range(ns):
                    st = jb["steps"][si]
                    slot = st["slot"]
                    for h in range(2):
                        first = (si == 0 and h == 0)
                        last = (si == ns - 1 and h == 1)
                        kb.op("pe", lambda e, st=st, h=h, first=first, last=last, slot=slot: e.matmul(
                            o, lhsT=Vp[:, st["vslot"], h, :], rhs=PT[:, slot, h, :], start=first, stop=last),
                            reads=[b_Vp[st["vslot"]], b_PT[st["gs"]]], writes=[cm.bank_b[nbk]], sig=False)
                od = cm.banks[nbk][:, nco + 128:nco + 384]
                for si in range(ns):
                    st = jb["steps"][si]
                    slot = st["slot"]
                    kb.op("pe", lambda e, si=si, slot=slot: e.matmul(
                        od, lhsT=ones_b[:], rhs=PT[:, slot, :, :].rearrange("p h q -> p (h q)"),
                        start=(si == 0), stop=(si == ns - 1)),
                        reads=[b_onesp, b_PT[st["gs"]]], writes=[cm.bank_b[nbk]], sig=(si == ns - 1))
                q0, qs = jb["qc0"], jb["qstep"]
                qhi = q0 + qs * 127 + 1
                num = cm.banks[nbk][:, nco:nco + 128]
                dens = [cm.banks[nbk][0:64, nco + 128:nco + 256], cm.banks[nbk][64:128, nco + 256:nco + 384]]
                hsl = [slice(0, 64), slice(64, 128)]
                if is_na:
                    ri = jb["nd"]
                    for h in range(2):
                        kb.op("dve", lambda e, h=h: e.reciprocal(out=rden[ri][hsl[h], :], in_=dens[h]),
                              reads=[cm.bank_b[nbk]], writes=[b_rden[ri]])
                    kb.op("dve", lambda e: e.tensor_tensor(out=yT[:, hp, q0:qhi:qs], in0=num, in1=rden[ri][:],
                                                           op=ALU.mult),
                          reads=[cm.bank_b[nbk], b_rden[ri]], writes=[b_yT[hp]])
                elif jb["pat"] == 0:
                    kb.op("act", lambda e: e.copy(out=accN[:, q0:qhi:qs], in_=num), reads=[cm.bank_b[nbk]], writes=[b_acc])
                    for h in range(2):
                        kb.op("act", lambda e, h=h: e.copy(out=accD[hsl[h], q0:qhi:qs], in_=dens[h]),
                              reads=[cm.bank_b[nbk]], writes=[b_acc])
                else:
                    kb.op("dve", lambda e: e.tensor_tensor(out=accN[:, q0:qhi:qs], in0=num, in1=accN[:, q0:qhi:qs],
                                                           op=ALU.add), reads=[cm.bank_b[nbk], b_acc], writes=[b_acc])
                    for h in range(2):
                        kb.op("dve", lambda e, h=h: e.tensor_tensor(out=accD[hsl[h], q0:qhi:qs], in0=dens[h],
                                                                    in1=accD[hsl[h], q0:qhi:qs], op=ALU.add),
                              reads=[cm.bank_b[nbk], b_acc], writes=[b_acc])

            for ji in range(len(jobs) + 1):
                if ji < len(jobs):
                    ns_ = len(jobs[ji]["steps"])
                    for si in range(0, ns_, 2):
                        emit_group(jobs[ji], si, min(2, ns_ - si))
                if ji >= 1:
                    emit_pv(jobs[ji - 1])
            if not is_na:
                kb.op("dve", lambda e: e.reciprocal(out=accD[:], in_=accD[:]), reads=[b_acc], writes=[b_acc])
                kb.op("dve", lambda e: e.tensor_tensor(out=yT[:, 4 + hp, :], in0=accN[:], in1=accD[:], op=ALU.mult),
                      reads=[b_acc], writes=[b_yT[4 + hp]])
    kb.barrier()

    lnp = LNParams(kb, "a_ln", ln_g, ln_b)
    lns = LNScratch(kb, "a_lns", n=2)
    wo = kb.sb("a_wo", [128, KC, D], BF16)
    b_wo = [kb.buf("a_wo%d" % c) for c in range(KC)]
    w_out_v = w_out.rearrange("(c p) n -> p c n", p=128)
    for c in range(KC):
        kb.dma(lambda e, c=c: e.dma_start(out=wo[:, c, :], in_=w_out_v[:, c, :]), writes=[b_wo[c]], q="pool")
    xres = [kb.sb("a_xres%d" % i, [128, D], F32) for i in range(2)]
    b_xres = [kb.buf("a_xres%d" % i) for i in range(2)]
    otile = [kb.sb("a_ot%d" % i, [128, D], F32) for i in range(2)]
    b_ot = [kb.buf("a_ot%d" % i) for i in range(2)]
    def mm_fn(tg, yb):
        for h in range(2):
            for c in range(KC):
                kb.op("pe", lambda e, c=c, h=h, tg=tg: e.matmul(
                    cm.banks[yb[h]][:], lhsT=yT[:, c, tg * 128:(tg + 1) * 128], rhs=wo[:, c, h * 512:(h + 1) * 512],
                    start=(c == 0), stop=(c == KC - 1)),
                    reads=[b_yT[c], b_wo[c]], writes=[cm.bank_b[yb[h]]], sig=(c == KC - 1))
    emit_epilogue(kb, cm, lnp, lns, TOK // 128, xh, HA, out, xres, b_xres, otile, b_ot, mm_fn)


def build_attn(n_pairs=8):
    nc = bass.Bass("TRN2", target_bir_lowering=False)
    di = lambda n, s: nc.dram_tensor(n, s, F32, kind="ExternalInput").ap()
    _, nmask = na_mask_index()
    xh = di("xh", [NTA, D])
    w_in = di("w_in", [8, 128, 3, D])
    w_out = di("w_out", [D, D])
    bias_all = di("bias_all", [8, 128, nmask, 2, 128])
    ln_g = di("ln_g", [D])
    ln_b = di("ln_b", [D])
    out = nc.dram_tensor("out", [TOK, D], F32, kind="ExternalOutput").ap()
    with ExitStack() as es:
        kb = KB(nc, es)
        cm = Common(kb)
        emit_attn(kb, cm, xh, out, w_in, w_out, bias_all, ln_g, ln_b, n_pairs=n_pairs)
        kb.finish()
    return nc


def t5_bucket_np(rel):
    nb = 16
    max_exact = 8
    ret = np.where(rel > 0, nb, 0)
    n = np.abs(rel)
    large = max_exact + (np.log(np.maximum(n, 1).astype(np.float32) / max_exact)
                         / np.float32(np.log(1024 / max_exact)) * (nb - max_exact)).astype(np.int32)
    large = np.minimum(large, nb - 1)
    return ret + np.where(n < max_exact, n, large)


def make_dil_bias(t5_bias):
    k = np.arange(128)[:, None]
    q = np.arange(128)[None, :]
    res = np.empty((3, 8, 2, 128, 128), np.float32)
    for p, dil in enumerate(DILS):
        for kt in range(2):
            rel = (128 * kt - 64 + k) - q
            bk = t5_bucket_np(rel * dil)
            g = t5_bias[bk]
            valid = np.abs(rel) <= 64
            for h in range(8):
                res[p, h, kt] = np.where(valid, g[:, :, h], np.float32(NEG))
    return res


def make_na_bias(rpb):
    k = np.arange(128)[:, None]
    q = np.arange(128)[None, :]
    res = np.zeros((8, 7, 128, 128), np.float32)
    for di_, d in enumerate(range(-3, 4)):
        rr = 2 * d + k // 64 - q // 64
        cr = k % 64 - q % 64
        ok = (np.abs(rr) <= 7) & (np.abs(cr) <= 15)
        rri = np.clip(rr + 7, 0, 14)
        cri = np.clip(cr + 15, 0, 30)
        for h in range(8):
            res[h, di_] = np.where(ok, rpb[h][rri, cri], np.float32(0))
    return res


def make_na_mask(seg):
    idx, n = na_mask_index()
    res = np.empty((n, 128, 128), np.float32)
    k = np.arange(128)[:, None]
    q = np.arange(128)[None, :]
    for (cls, d), m in idx.items():
        qi = 5 if cls == "I" else cls
        gq = seg * 16 + qi
        r = 2 * gq + q // 64
        c = q % 64
        kr = 2 * (gq + d) + k // 64
        kc = k % 64
        r0 = np.clip(r - 4, 0, 120)
        c0 = np.clip(c - 8, 0, 48)
        ok = (kr >= r0) & (kr < r0 + 8) & (kc >= c0) & (kc < c0 + 16)
        res[m] = np.where(ok, np.float32(0), np.float32(NEG))
    return res


def make_kvb(seg):
    t0 = seg * TOK
    cols = []
    p = np.arange(128)
    for t in range(NA_NV):
        tok = t0 + (t - 3) * 128 + p
        cols.append(tok)
    for (dil, r, ms) in dil_vtiles():
        tok = t0 + dil * (ms + p) + r
        cols.append(tok)
    tok = np.stack(cols, axis=1)
    return np.where((tok >= 0) & (tok < SEQ), np.float32(0), np.float32(NEG)).astype(np.float32)


def relayout_w_up(w_up):
    dff = w_up.shape[1] // 2
    npair = dff // 128
    w = w_up.reshape(KC, 128, 2, npair, 128)
    return np.ascontiguousarray(w.transpose(3, 1, 0, 2, 4)).reshape(npair, 128, KC * 2 * 128)


def relayout_groups(w, groups):
    res = np.zeros((len(groups), 128, 3, KC, 128), np.float32)
    wv = w.reshape(KC, 128, -1)
    for g, cols in enumerate(groups):
        for i, c0 in enumerate(cols):
            res[g, :, i] = wv[:, :, c0:c0 + 128].transpose(1, 0, 2)
    return res.reshape(len(groups), 128, 3, KC * 128)


CONV_GROUPS = [[j * 128, CH + j * 128] for j in range(CJ)] + \
              [[2 * CH + j * 128, 3 * CH + j * 128, 4 * CH + j * 128] for j in range(CJ)]
ATTN_GROUPS = [[(0 if pr < 4 else 3 * CH) + i * CH + (pr % 4) * 128 for i in range(3)] for pr in range(8)]


def make_bias_all(na_bias, na_mask, dil_bias, seg, nseg):
    idx, n = na_mask_index()
    res = np.zeros((8, n, 2, 128, 128), np.float32)
    k = np.arange(128)[:, None]
    for pr in range(8):
        hp = pr % 4
        for h in range(2):
            hg = hp * 2 + h
            if pr < 4:
                for (cls, d), m in idx.items():
                    res[pr, m, h] = np.where(na_mask[m] == 0, na_bias[hg, d + 3], np.float32(NEG))
            else:
                for p in range(3):
                    for v in range(4):
                        for kt in range(2):
                            t = dil_bias[p, hg, kt]
                            if kt == 0 and (v & 1) and seg == 0:
                                t = np.where(k < 64, np.float32(NEG), t)
                            if kt == 1 and (v & 2) and seg == nseg - 1:
                                t = np.where(k >= 64, np.float32(NEG), t)
                            res[pr, p * 8 + v * 2 + kt, h] = t
    return np.ascontiguousarray(res.transpose(0, 3, 1, 2, 4))


_PROGS = {}


def _prog(name):
    if name not in _PROGS:
        _PROGS[name] = {"attn": build_attn, "conv": build_conv, "ffn": build_ffn}[name]()
    return _PROGS[name]


def _shards_with_halo(x, halo):
    B, S, Dm = x.shape
    xp = np.zeros((B, S + 2 * halo, Dm), np.float32)
    xp[:, halo:halo + S] = x
    res = []
    for c in range(NCORES):
        b, seg = divmod(c, S // TOK)
        res.append(np.ascontiguousarray(xp[b, seg * TOK:seg * TOK + TOK + 2 * halo]))
    return res


def _gather(res, B, S):
    out = np.empty((B, S, D), np.float32)
    for c in range(NCORES):
        b, seg = divmod(c, S // TOK)
        out[b, seg * TOK:(seg + 1) * TOK] = res.results[c]["out"]
    return out


def _run(name, in_maps):
    return run_bass_kernel_spmd(_prog(name), in_maps, core_ids=list(range(NCORES)))


def kernel(x, t5_bias, attn_w_in, attn_w_out, na_rpb, conv_w_in, conf_dw_w, conf_dw_b, conf_ln_g, conf_ln_b,
           sconv_w, conv_w_out, ffn_w_up, ffn_dw_w, ffn_w_down, mix_ln_g, mix_ln_b, ffn_ln_g, ffn_ln_b):
    f = lambda a: np.ascontiguousarray(np.asarray(a, dtype=np.float32))
    x = f(x)
    B, S, _ = x.shape
    nseg = S // TOK
    dil_bias = make_dil_bias(f(t5_bias))
    na_masks = [make_na_mask(seg) for seg in range(nseg)]
    for i in range(4):
        j = i // 2
        if i % 2 == 0:
            xs = _shards_with_halo(x, HA)
            na_bias = make_na_bias(f(na_rpb[j]))
            common = dict(w_in=relayout_groups(f(attn_w_in[j]), ATTN_GROUPS), w_out=f(attn_w_out[j]),
                          ln_g=f(mix_ln_g[i]), ln_b=f(mix_ln_b[i]))
            biases = [make_bias_all(na_bias, na_masks[seg], dil_bias, seg, nseg) for seg in range(nseg)]
            maps = [dict(common, xh=xs[c], bias_all=biases[c % nseg]) for c in range(NCORES)]
            x = _gather(_run("attn", maps), B, S)
        else:
            xs = _shards_with_halo(x, HALO_C)
            common = dict(w_in=relayout_groups(f(conv_w_in[j]), CONV_GROUPS), cdw_w=f(conf_dw_w[j]), cdw_b=f(conf_dw_b[j]), cln_g=f(conf_ln_g[j]),
                          cln_b=f(conf_ln_b[j]), sconv_w=f(sconv_w[j]), w_out=f(conv_w_out[j]),
                          ln_g=f(mix_ln_g[i]), ln_b=f(mix_ln_b[i]))
            maps = [dict(common, xh=xs[c]) for c in range(NCORES)]
            x = _gather(_run("conv", maps), B, S)
        xs = _shards_with_halo(x, 1)
        common = dict(w_up=relayout_w_up(f(ffn_w_up[i])), dw=f(ffn_dw_w[i]), w_down=f(ffn_w_down[i]),
                      ln_g=f(ffn_ln_g[i]), ln_b=f(ffn_ln_b[i]))
        maps = [dict(common, xh=xs[c]) for c in range(NCORES)]
        x = _gather(_run("ffn", maps), B, S)
    return x
```

```python
from contextlib import ExitStack

import numpy as np
import concourse.bass as bass
import concourse.mybir as mybir
from concourse.bass_utils import run_bass_kernel_spmd

F32 = mybir.dt.float32
BF16 = mybir.dt.bfloat16
AF = mybir.ActivationFunctionType
ALU = mybir.AluOpType
AX = mybir.AxisListType

D = 1024
KC = D // 128
SEQ = 8192
BATCH = 2
NCORES = 8
TOK = 2048
DFF = 2816
NPAIR = DFF // 128
ALPHA = 8.0 ** 0.25
LN_EPS = 1e-5
import os
SAME_SYNC_ENGINES = set(os.environ.get('SAME_SYNC', 'pe,act,dve,pool,sp').split(','))
NDMA = 24


class Buf:
    __slots__ = ("name", "w", "r")

    def __init__(self, name):
        self.name = name
        self.w = None
        self.r = []


class KB:
    def __init__(self, nc, es):
        self.nc = nc
        self.es = es
        self.E = {"pe": nc.tensor, "act": nc.scalar, "dve": nc.vector, "pool": nc.gpsimd, "sp": nc.sync}
        self.sems = {}
        self.cnt = {}
        for k in ("pe", "act", "dve", "pool"):
            self.sems[k] = es.enter_context(nc.semaphore("s_" + k))
            self.cnt[k] = 0
        for i in range(NDMA):
            self.sems[("dma", i)] = es.enter_context(nc.semaphore("s_dma%d" % i))
            self.cnt[("dma", i)] = 0
        NSW = 12
        for i in range(NSW):
            self.sems[("swdma", i)] = es.enter_context(nc.semaphore("s_swdma%d" % i))
            self.cnt[("swdma", i)] = 0
        self.nsw = NSW
        self.rr_sw = 0
        self.rr = 0
        self.seen = {e: {} for e in self.E}
        self.pending = {e: False for e in self.E}
        self.out_events = []
        self.nbuf = 0

    def buf(self, name=None):
        self.nbuf += 1
        return Buf(name or "b%d" % self.nbuf)

    def sb(self, name, shape, dt):
        return self.es.enter_context(self.nc.sbuf_tensor(name, list(shape), dt))

    def ps(self, name, shape, dt):
        return self.es.enter_context(self.nc.psum_tensor(name, list(shape), dt))

    def _wait(self, eng, key, val):
        if val <= 0:
            return
        if key == eng and (eng not in SAME_SYNC_ENGINES or val > self.cnt[eng]):
            return
        if self.seen[eng].get(key, 0) >= val:
            return
        self.E[eng].wait_ge(self.sems[key], val)
        self.seen[eng][key] = val

    def _deps(self, eng, reads, writes):
        need = {}

        def add(ev):
            if ev is None:
                return
            k, v = ev
            if need.get(k, 0) < v:
                need[k] = v

        for b in reads:
            add(b.w)
        for b in writes:
            add(b.w)
            for r in b.r:
                add(r)
        for k, v in need.items():
            self._wait(eng, k, v)

    def _record(self, ev, reads, writes):
        for b in reads:
            b.r.append(ev)
            if len(b.r) > 64:
                mx = {}
                for k, v in b.r:
                    if mx.get(k, 0) < v:
                        mx[k] = v
                b.r = list(mx.items())
        for b in writes:
            b.w = ev
            b.r = []

    def op(self, eng, fn, reads=(), writes=(), sig=True):
        self._deps(eng, reads, writes)
        ins = fn(self.E[eng])
        if sig:
            self.cnt[eng] += 1
            ins.then_inc(self.sems[eng], 1)
            ev = (eng, self.cnt[eng])
            self.pending[eng] = False
        else:
            ev = (eng, self.cnt[eng] + 1)
            self.pending[eng] = True
        self._record(ev, reads, writes)
        return ev

    def dma(self, fn, reads=(), writes=(), q="sp", is_output=False):
        self._deps(q, reads, writes)
        if q == "pool":
            i = self.rr_sw
            self.rr_sw = (self.rr_sw + 1) % self.nsw
            key = ("swdma", i)
        else:
            i = self.rr
            self.rr = (self.rr + 1) % NDMA
            key = ("dma", i)
        self._wait(q, key, self.cnt[key])
        ins = fn(self.E[q])
        self.cnt[key] += 16
        ins.then_inc(self.sems[key], 16)
        ev = (key, self.cnt[key])
        self._record(ev, reads, writes)
        if is_output:
            self.out_events.append(ev)
        return ev

    def barrier(self):
        for e in ("pe", "act", "dve", "pool", "sp"):
            for k, v in self.cnt.items():
                if k != e:
                    self._wait(e, k, v)

    def finish(self):
        for e, p in self.pending.items():
            assert not p, "engine %s has unsignaled trailing ops" % e
        for k, v in self.out_events:
            self._wait("sp", k, v)


class Common:
    def __init__(self, kb):
        nc = kb.nc
        self.kb = kb
        self.bankpair = [kb.ps("bankpair%d" % i, [128, 1024], F32) for i in range(4)]
        self.banks = []
        for bp in self.bankpair:
            self.banks += [bp[:, 0:512], bp[:, 512:1024]]
        self.bank_b = [kb.buf("bank%d" % i) for i in range(8)]
        self.ident_f = kb.sb("ident_f", [128, 128], F32)
        self.ident_b = kb.sb("ident_b", [128, 128], BF16)
        self.b_ident = kb.buf("ident")
        self.ones_f = kb.sb("ones_f", [128, 128], F32)

        kb.op("pool", lambda e: e.memset(self.ones_f[:], 1.0), writes=[self.b_ident])
        kb.op("pool", lambda e: e.affine_select(
            out=self.ident_f[:], in_=self.ones_f[:], pattern=[[-1, 128]], compare_op=ALU.is_equal,
            fill=0.0, base=0, channel_multiplier=1), reads=[self.b_ident], writes=[self.b_ident])
        kb.op("pool", lambda e: e.tensor_copy(out=self.ident_b[:], in_=self.ident_f[:]),
              reads=[self.b_ident], writes=[self.b_ident])


def load_transpose_tile(kb, cm, x_src, xT, col0, b_xT, xtile, b_xtile, banks, evac_eng="act"):
    kb.dma(lambda e: e.dma_start(out=xtile[:], in_=x_src), writes=[b_xtile])
    transpose_tile(kb, cm, xtile, b_xtile, xT, col0, b_xT, banks, evac_eng)


def transpose_tile(kb, cm, xtile, b_xtile, xT, col0, b_xT, banks, evac_eng="act"):
    for half in range(2):
        bk = banks[half]
        for j in range(4):
            c = half * 4 + j
            kb.op("pe", lambda e, c=c, j=j, bk=bk: e.transpose(
                out=cm.banks[bk][:, j * 128:(j + 1) * 128], in_=xtile[:, c * 128:(c + 1) * 128],
                identity=cm.ident_f[:]),
                reads=[b_xtile, cm.b_ident], writes=[cm.bank_b[bk]], sig=(j == 3))
        src = cm.banks[bk][:].rearrange("p (c t) -> p c t", c=4)
        dst = xT[:, half * 4:(half + 1) * 4, col0:col0 + 128]
        if evac_eng == "act":
            kb.op("act", lambda e, src=src, dst=dst: e.copy(out=dst, in_=src),
                  reads=[cm.bank_b[bk]], writes=[b_xT])
        else:
            kb.op("dve", lambda e, src=src, dst=dst: e.tensor_copy(out=dst, in_=src),
                  reads=[cm.bank_b[bk]], writes=[b_xT])


def load_transpose_bf16(kb, cm, src_rows, rows, xtile, b_xtile, xT, col0, b_xT, bank, evac_eng):
    if rows < 128:
        kb.op("dve", lambda e: e.memset(xtile[:], 0.0), writes=[b_xtile])
    kb.dma(lambda e: e.dma_start(out=xtile[0:rows, :], in_=src_rows), writes=[b_xtile], q="pool")
    pv = cm.banks[bank].bitcast(BF16)
    for c in range(KC):
        kb.op("pe", lambda e, c=c: e.transpose(out=pv[:, c * 128:(c + 1) * 128], in_=xtile[:, c * 128:(c + 1) * 128],
                                               identity=cm.ident_b[:]),
              reads=[b_xtile, cm.b_ident], writes=[cm.bank_b[bank]], sig=(c == KC - 1))
    src = pv[:, 0:KC * 128].rearrange("p (c t) -> p c t", c=KC)
    dst = xT[:, :, col0:col0 + 128]
    if evac_eng == "act":
        kb.op("act", lambda e: e.copy(out=dst, in_=src), reads=[cm.bank_b[bank]], writes=[b_xT])
    else:
        kb.op("dve", lambda e: e.tensor_copy(out=dst, in_=src), reads=[cm.bank_b[bank]], writes=[b_xT])


def emit_epilogue(kb, cm, lnp, lns, ntiles, xh, row0, out, xres, b_xres, otile, b_ot, mm_fn, yb=(6, 7)):
    def load(tg):
        s = tg % 2
        kb.dma(lambda e: e.dma_start(out=xres[s][:], in_=xh[row0 + tg * 128:row0 + (tg + 1) * 128, :]),
               writes=[b_xres[s]])
    load(0)
    for tg in range(ntiles):
        s = tg % 2
        if tg + 1 < ntiles:
            load(tg + 1)
        mm_fn(tg, yb)
        residual_ln(kb, cm, lns, lnp, xres[s], b_xres[s], yb, otile[s][:], b_ot[s])
        kb.dma(lambda e, s=s, tg=tg: e.dma_start(out=out[tg * 128:(tg + 1) * 128, :], in_=otile[s][:]),
               reads=[b_ot[s]], is_output=True)


class LNParams:
    def __init__(self, kb, name, g_ap, b_ap):
        self.g = kb.sb(name + "_g", [128, D], F32)
        self.b = kb.sb(name + "_b", [128, D], F32)
        self.buf = kb.buf(name)
        kb.dma(lambda e: e.dma_start(out=self.g[:], in_=g_ap.partition_broadcast(128)), writes=[self.buf])
        kb.dma(lambda e: e.dma_start(out=self.b[:], in_=b_ap.partition_broadcast(128)), writes=[self.buf])


class LNScratch:
    def __init__(self, kb, name, n=2):
        self.n = n
        self.i = 0
        self.z = [kb.sb("%s_z%d" % (name, i), [128, D], F32) for i in range(n)]
        self.st = [kb.sb("%s_st%d" % (name, i), [128, 12], F32) for i in range(n)]
        self.mv = [kb.sb("%s_mv%d" % (name, i), [128, 4], F32) for i in range(n)]
        self.bz = [kb.buf("%s_bz%d" % (name, i)) for i in range(n)]
        self.bs = [kb.buf("%s_bs%d" % (name, i)) for i in range(n)]

    def next(self):
        i = self.i
        self.i = (self.i + 1) % self.n
        return i


def residual_ln(kb, cm, lns, lnp, xres, b_xres, ybanks, out_tile, b_out):
    i = lns.next()
    z, st, mv, bz, bs = lns.z[i], lns.st[i], lns.mv[i], lns.bz[i], lns.bs[i]
    for h in range(2):
        kb.op("dve", lambda e, h=h: e.scalar_tensor_tensor(
            out=z[:, h * 512:(h + 1) * 512], in0=xres[:, h * 512:(h + 1) * 512], scalar=ALPHA,
            in1=cm.banks[ybanks[h]][:], op0=ALU.mult, op1=ALU.add),
            reads=[b_xres, cm.bank_b[ybanks[h]]], writes=[bz])
    ln_core(kb, lns, i, lnp, out_tile, b_out)


def ln_core(kb, lns, i, lnp, out_tile, b_out, width=D):
    z, st, mv, bz, bs = lns.z[i], lns.st[i], lns.mv[i], lns.bz[i], lns.bs[i]
    nch = width // 512
    for h in range(nch):
        kb.op("dve", lambda e, h=h: e.bn_stats(out=st[:, h * 6:(h + 1) * 6], in_=z[:, h * 512:(h + 1) * 512]),
              reads=[bz], writes=[bs])
    kb.op("dve", lambda e: e.bn_aggr(out=mv[:, 0:2], in_=st[:, 0:6 * nch]), reads=[bs], writes=[bs])
    kb.op("dve", lambda e: e.tensor_scalar_add(out=mv[:, 2:3], in0=mv[:, 1:2], scalar1=LN_EPS), reads=[bs], writes=[bs])
    kb.op("act", lambda e: e.activation(out=mv[:, 2:3], in_=mv[:, 2:3], func=AF.Sqrt), reads=[bs], writes=[bs])
    kb.op("dve", lambda e: e.reciprocal(out=mv[:, 2:3], in_=mv[:, 2:3]), reads=[bs], writes=[bs])
    kb.op("dve", lambda e: e.scalar_tensor_tensor(out=mv[:, 3:4], in0=mv[:, 0:1], scalar=-1.0, in1=mv[:, 2:3],
                                                  op0=ALU.mult, op1=ALU.mult), reads=[bs], writes=[bs])
    kb.op("act", lambda e: e.activation(out=z[:, 0:width], in_=z[:, 0:width], func=AF.Identity,
                                        bias=mv[:, 3:4], scale=mv[:, 2:3]),
          reads=[bz, bs], writes=[bz])
    kb.op("dve", lambda e: e.tensor_tensor(out=z[:, 0:width], in0=z[:, 0:width], in1=lnp.g[:, 0:width], op=ALU.mult),
          reads=[bz, lnp.buf], writes=[bz])
    kb.op("pool", lambda e: e.tensor_tensor(out=out_tile, in0=z[:, 0:width], in1=lnp.b[:, 0:width], op=ALU.add),
          reads=[bz, lnp.buf], writes=[b_out])


def emit_ffn(kb, cm, xh, out, w_up, dw, w_down, ln_g, ln_b, ntok=TOK, npair=NPAIR, half_tok=1024):
    nc = kb.nc
    ncols = ntok + 2
    xT = kb.sb("f_xT", [128, KC, ncols + 126], BF16)
    ntile_in = (ncols + 127) // 128
    b_xT = [kb.buf("f_xT%d" % i) for i in range(ntile_in)]
    xt_stage = [kb.sb("f_xst%d" % i, [128, D], F32) for i in range(2)]
    b_xst = [kb.buf("f_xst%d" % i) for i in range(2)]
    xbf = [kb.sb("f_xbf%d" % i, [128, D], BF16) for i in range(2)]
    b_xbf = [kb.buf("f_xbf%d" % i) for i in range(2)]
    lnp = LNParams(kb, "f_ln", ln_g, ln_b)
    lns = LNScratch(kb, "f_lns")
    ncht = dw.shape[1] // 128
    dwt = kb.sb("f_dw", [128, 3, ncht], F32)
    b_dw = kb.buf("f_dw")
    for k in range(3):
        kb.dma(lambda e, k=k: e.dma_start(out=dwt[:, k, :], in_=dw[k, :].rearrange("(c p) -> p c", p=128),
                                          allow_slow_non_contiguous=True), writes=[b_dw])
    wd = kb.sb("f_wd", [128, npair, D], BF16)
    b_wd = [kb.buf("f_wd%d" % c) for c in range(npair)]
    NWU = 3
    wu = [kb.sb("f_wu%d" % i, [128, KC, 2, 128], BF16) for i in range(NWU)]
    b_wu = [kb.buf("f_wu%d" % i) for i in range(NWU)]
    aT = kb.sb("f_aT", [128, npair, half_tok], BF16)
    blks = []
    o = 0
    while o < half_tok:
        n_ = min(510, half_tok - o)
        blks.append((o, n_))
        o += n_
    b_aT = [[kb.buf("f_aT%d_%d" % (c, b)) for b in range(len(blks))] for c in range(npair)]
    t1 = [[kb.sb("f_t1_%d_%d" % (i, j), [128, 512], F32) for j in range(2)] for i in range(2)]
    b_t1 = [[kb.buf("f_t1_%d_%d" % (i, j)) for j in range(2)] for i in range(2)]
    sg = [kb.sb("f_sg%d" % i, [128, 512], F32) for i in range(2)]
    b_sg = [kb.buf("f_sg%d" % i) for i in range(2)]
    xres, b_xres = xt_stage, b_xst
    otile = [kb.sb("f_ot%d" % i, [128, D], F32) for i in range(2)]
    b_ot = [kb.buf("f_ot%d" % i) for i in range(2)]

    for t in range(ntile_in):
        rows = min(128, ncols - t * 128)
        s = t % 2
        load_transpose_bf16(kb, cm, xh[t * 128:t * 128 + rows, :], rows, xbf[s], b_xbf[s], xT, t * 128, b_xT[t],
                            bank=6 + s, evac_eng=("act" if s == 0 else "dve"))

    w_down_v = w_down.rearrange("(c p) n -> p c n", p=128)
    nhalf = ntok // half_tok
    seq = [(hf, c) for hf in range(nhalf) for c in range(npair)]

    def load_pair(idx):
        hf_, c_ = seq[idx]
        s_ = idx % NWU
        kb.dma(lambda e: e.dma_start(out=wu[s_][:].rearrange("p a b c -> p (a b c)"), in_=w_up[c_]),
               writes=[b_wu[s_]], q="pool")
        if hf_ == 0:
            kb.dma(lambda e: e.dma_start(out=wd[:, c_, :], in_=w_down_v[:, c_, :]), writes=[b_wd[c_]], q="pool")

    load_pair(0)
    for idx, (hf, c) in enumerate(seq):
        if True:
            s = idx % NWU
            if idx + 1 < len(seq):
                load_pair(idx + 1)
            for b, (o0, nout) in enumerate(blks):
                col0 = hf * half_tok + o0
                nin = nout + 2
                rd_x = [b_xT[t] for t in range(col0 // 128, (col0 + nin - 1) // 128 + 1)]
                par = (idx * len(blks) + b) % 2
                bg, bu = (0, 1) if par == 0 else (2, 3)
                for gi, bk in ((0, bg), (1, bu)):
                    for kc in range(KC):
                        kb.op("pe", lambda e, kc=kc, gi=gi, bk=bk, s=s, col0=col0, nin=nin: e.matmul(
                            cm.banks[bk][:, 0:nin], lhsT=wu[s][:, kc, gi, :], rhs=xT[:, kc, col0:col0 + nin],
                            start=(kc == 0), stop=(kc == KC - 1)),
                            reads=[b_wu[s]] + rd_x, writes=[cm.bank_b[bk]], sig=(kc == KC - 1))
                for gi, bk in ((0, bg), (1, bu)):
                    ch = c if gi == 0 else (ncht // 2 + c)
                    T = t1[par][gi]
                    bT = b_t1[par][gi]
                    A = cm.banks[bk]
                    w0, w1, w2 = (dwt[:, k, ch:ch + 1] for k in range(3))
                    kb.op("act", lambda e, T=T, A=A, w1=w1, nout=nout: e.activation(
                        out=T[:, 0:nout], in_=A[:, 1:nout + 1], func=AF.Copy, scale=w1),
                        reads=[cm.bank_b[bk], b_dw], writes=[bT])
                    kb.op("dve", lambda e, T=T, A=A, w0=w0, nout=nout: e.scalar_tensor_tensor(
                        out=T[:, 0:nout], in0=A[:, 0:nout], scalar=w0, in1=T[:, 0:nout], op0=ALU.mult, op1=ALU.add),
                        reads=[cm.bank_b[bk], b_dw, bT], writes=[bT])
                    kb.op("dve", lambda e, T=T, A=A, w2=w2, nout=nout: e.scalar_tensor_tensor(
                        out=T[:, 0:nout], in0=A[:, 2:nout + 2], scalar=w2, in1=T[:, 0:nout], op0=ALU.mult, op1=ALU.add),
                        reads=[cm.bank_b[bk], b_dw, bT], writes=[bT])
                kb.op("act", lambda e, par=par, nout=nout: e.activation(out=sg[par][:, 0:nout], in_=t1[par][0][:, 0:nout],
                                                                        func=AF.Silu),
                      reads=[b_t1[par][0]], writes=[b_sg[par]])
                kb.op("pool", lambda e, par=par, c=c, o0=o0, nout=nout: e.tensor_tensor(
                    out=aT[:, c, o0:o0 + nout], in0=sg[par][:, 0:nout], in1=t1[par][1][:, 0:nout], op=ALU.mult),
                    reads=[b_sg[par], b_t1[par][1]], writes=[b_aT[c][b]])
        if c != npair - 1:
            continue
        def mm_fn(tt, yb):
            bqs = [bi for bi, (o0, n_) in enumerate(blks) if o0 < (tt + 1) * 128 and o0 + n_ > tt * 128]
            for h in range(2):
                for c in range(npair):
                    kb.op("pe", lambda e, c=c, h=h, tt=tt: e.matmul(
                        cm.banks[yb[h]][:], lhsT=aT[:, c, tt * 128:(tt + 1) * 128], rhs=wd[:, c, h * 512:(h + 1) * 512],
                        start=(c == 0), stop=(c == npair - 1)),
                        reads=[b_aT[c][bq] for bq in bqs] + [b_wd[c]], writes=[cm.bank_b[yb[h]]], sig=(c == npair - 1))
        r0 = hf * half_tok
        emit_epilogue(kb, cm, lnp, lns, half_tok // 128, xh, 1 + r0, out[r0:r0 + half_tok, :], xres, b_xres, otile, b_ot, mm_fn)


def DFF_COLS(npair):
    return npair * 128


def build_ffn(ntok=TOK, npair=NPAIR, half_tok=1024):
    nc = bass.Bass("TRN2", target_bir_lowering=False)
    xh = nc.dram_tensor("xh", [ntok + 2, D], F32, kind="ExternalInput").ap()
    w_up = nc.dram_tensor("w_up", [npair, 128, KC * 2 * 128], F32, kind="ExternalInput").ap()
    dw = nc.dram_tensor("dw", [3, 2 * npair * 128], F32, kind="ExternalInput").ap()
    w_down = nc.dram_tensor("w_down", [npair * 128, D], F32, kind="ExternalInput").ap()
    ln_g = nc.dram_tensor("ln_g", [D], F32, kind="ExternalInput").ap()
    ln_b = nc.dram_tensor("ln_b", [D], F32, kind="ExternalInput").ap()
    out = nc.dram_tensor("out", [ntok, D], F32, kind="ExternalOutput").ap()
    with ExitStack() as es:
        kb = KB(nc, es)
        cm = Common(kb)
        emit_ffn(kb, cm, xh, out, w_up, dw, w_down, ln_g, ln_b, ntok=ntok, npair=npair, half_tok=half_tok)
        kb.finish()
    return nc


def load_cols(kb, cm, src, dst, b_dst, bank, name, alloc=None):
    R, C = src.shape
    nch = C // 128
    assert R * nch <= 512
    st = (alloc or kb.sb)(name + "_rows", [R, C], F32)
    b_st = kb.buf(name + "_rows")
    kb.dma(lambda e: e.dma_start(out=st[:], in_=src), writes=[b_st])
    for c in range(nch):
        kb.op("pe", lambda e, c=c: e.transpose(out=cm.banks[bank][:, c * R:(c + 1) * R],
                                               in_=st[0:R, c * 128:(c + 1) * 128], identity=cm.ident_f[0:R, 0:R]),
              reads=[b_st, cm.b_ident], writes=[cm.bank_b[bank]], sig=(c == nch - 1))
    kb.op("dve", lambda e: e.tensor_copy(out=dst, in_=cm.banks[bank][:, 0:nch * R].rearrange("p (c r) -> p r c", r=R)),
          reads=[cm.bank_b[bank]], writes=[b_dst])


CH = 512
CJ = CH // 128
CK = 31
HALO_C = 16


def emit_conv(kb, cm, xh, out, w_in, cdw_w, cdw_b, cln_g, cln_b, sconv_w, w_out, ln_g, ln_b, ntok=TOK):
    nc = kb.nc
    es_par = ExitStack()
    es_ab = ExitStack()
    es_c = ExitStack()
    es_d = ExitStack()
    mk = lambda es_: (lambda name, shape, dt: es_.enter_context(nc.sbuf_tensor(name, list(shape), dt)))
    sb_par, sb_ab, sb_c, sb_d = mk(es_par), mk(es_ab), mk(es_c), mk(es_d)
    nrows = ntok + 2 * HALO_C
    ntile_in = (nrows + 127) // 128
    xT = kb.sb("c_xT", [128, KC, ntile_in * 128], BF16)
    b_xT = [kb.buf("c_xT%d" % i) for i in range(ntile_in)]
    xst = [kb.sb("c_xst%d" % i, [128, D], F32) for i in range(2)]
    b_xst = [kb.buf("c_xst%d" % i) for i in range(2)]
    xbf = [kb.sb("c_xbf%d" % i, [128, D], BF16) for i in range(2)]
    b_xbf = [kb.buf("c_xbf%d" % i) for i in range(2)]
    lnp = LNParams(kb, "c_ln", ln_g, ln_b)
    lns = LNScratch(kb, "c_lns", n=2)
    dwc = kb.sb("c_dwc", [128, CK, CJ], F32)
    b_par = kb.buf("c_par")
    vec = kb.sb("c_vec", [128, 3, CJ], F32)
    scw = kb.sb("c_scw", [128, 3, CJ], F32)
    dg = kb.sb("c_dg", [128, CK * CJ, 128], BF16)
    b_dg = kb.buf("c_dg")
    ones_m = kb.sb("c_ones", [128, 128], F32)
    b_ones = kb.buf("c_ones")
    wo = kb.sb("c_wo", [128, KC, D], BF16)
    b_wo = [kb.buf("c_wo%d" % c) for c in range(KC)]
    ucols = ntok + 30
    uT = kb.sb("c_uT", [128, CJ, ucols], BF16)
    nub = (ucols + 511) // 512
    b_uT = [[kb.buf("c_uT%d_%d" % (j, b)) for b in range(nub)] for j in range(CJ)]
    yT = kb.sb("c_yT", [128, 2 * CJ, ntok], BF16)
    nob = ntok // 512
    b_yT = [[kb.buf("c_yT%d_%d" % (j, b)) for b in range(nob)] for j in range(2 * CJ)]
    load_cols(kb, cm, cdw_w, dwc[:], b_par, 0, "c_dww", sb_par)
    load_cols(kb, cm, cdw_b.rearrange("(o c) -> o c", o=1), vec[:, 0:1, :], b_par, 1, "c_dwb", sb_par)
    load_cols(kb, cm, cln_g.rearrange("(o c) -> o c", o=1), vec[:, 1:2, :], b_par, 2, "c_lng", sb_par)
    load_cols(kb, cm, cln_b.rearrange("(o c) -> o c", o=1), vec[:, 2:3, :], b_par, 3, "c_lnb", sb_par)
    load_cols(kb, cm, sconv_w, scw[:], b_par, 4, "c_scw", sb_par)
    kb.barrier()
    es_par.close()
    for k in range(CK):
        for j in range(CJ):
            kb.op("dve", lambda e, k=k, j=j: e.tensor_scalar(
                out=dg[:, k * CJ + j, :], in0=cm.ident_f[:], scalar1=dwc[:, k, j:j + 1], scalar2=None, op0=ALU.mult),
                reads=[b_par, cm.b_ident], writes=[b_dg], sig=(k == CK - 1 and j == CJ - 1))
    kb.op("dve", lambda e: e.memset(ones_m[:], 1.0 / CH), writes=[b_ones])
    w_out_v = w_out.rearrange("(c p) n -> p c n", p=128)
    for c in range(KC):
        kb.dma(lambda e, c=c: e.dma_start(out=wo[:, c, :], in_=w_out_v[:, c, :]), writes=[b_wo[c]], q="pool")
    wi = [sb_ab("c_wi%d" % i, [128, 3, KC, 128], BF16) for i in range(2)]
    b_wi = [kb.buf("c_wi%d" % i) for i in range(2)]
    tmpA = [sb_ab("c_tmpA%d" % i, [128, 512], F32) for i in range(2)]
    b_tmpA = [kb.buf("c_tmpA%d" % i) for i in range(2)]
    pT = sb_ab("c_pT", [128, ntok + 2], F32)
    b_pT = kb.buf("c_pT")
    qT = sb_ab("c_qT", [128, ntok], F32)
    b_qT = kb.buf("c_qT")

    for t in range(ntile_in):
        rows = min(128, nrows - t * 128)
        s = t % 2
        load_transpose_bf16(kb, cm, xh[t * 128:t * 128 + rows, :], rows, xbf[s], b_xbf[s], xT, t * 128, b_xT[t],
                            bank=6 + s, evac_eng=("act" if s == 0 else "dve"))

    def xtiles(c0, n):
        return [b_xT[t] for t in range(c0 // 128, (c0 + n - 1) // 128 + 1)]

    def load_group(g):
        s = g % 2
        n = 2 if g < 4 else 3
        kb.dma(lambda e: e.dma_start(out=wi[s][:, 0:n, :, :].rearrange("p g k c -> p g (k c)"), in_=w_in[g][:, 0:n, :]),
               writes=[b_wi[s]], q="pool")
        return s

    def proj(s, i, bank, xc0, n):
        for kc in range(KC):
            kb.op("pe", lambda e, kc=kc: e.matmul(cm.banks[bank][:, 0:n], lhsT=wi[s][:, i, kc, :],
                                                  rhs=xT[:, kc, xc0:xc0 + n], start=(kc == 0), stop=(kc == KC - 1)),
                  reads=[b_wi[s]] + xtiles(xc0, n), writes=[cm.bank_b[bank]], sig=(kc == KC - 1))

    it = 0
    for j in range(CJ):
        s = load_group(j)
        for b in range(nub):
            u0 = b * 512
            n = min(512, ucols - u0)
            par = it % 2
            it += 1
            ba, bg = (0, 1) if par == 0 else (2, 3)
            proj(s, 0, ba, u0 + 1, n)
            proj(s, 1, bg, u0 + 1, n)
            kb.op("act", lambda e, par=par, bg=bg, n=n: e.activation(out=tmpA[par][:, 0:n], in_=cm.banks[bg][:, 0:n],
                                                                     func=AF.Sigmoid),
                  reads=[cm.bank_b[bg]], writes=[b_tmpA[par]])
            kb.op("dve", lambda e, par=par, ba=ba, n=n, j=j, u0=u0: e.tensor_tensor(
                out=uT[:, j, u0:u0 + n], in0=cm.banks[ba][:, 0:n], in1=tmpA[par][:, 0:n], op=ALU.mult),
                reads=[cm.bank_b[ba], b_tmpA[par]], writes=[b_uT[j][b]])

    pcols = ntok + 2
    npb = (pcols + 511) // 512
    for j in range(CJ):
        s = load_group(4 + j)
        for b in range(npb):
            p0 = b * 512
            n = min(512, pcols - p0)
            par = it % 2
            it += 1
            ba, bg = (0, 1) if par == 0 else (2, 3)
            proj(s, 1, ba, p0 + 15, n)
            proj(s, 2, bg, p0 + 15, n)
            kb.op("act", lambda e, par=par, bg=bg, n=n: e.copy(out=tmpA[par][:, 0:n], in_=cm.banks[bg][:, 0:n]),
                  reads=[cm.bank_b[bg]], writes=[b_tmpA[par]])
            kb.op("dve", lambda e, par=par, ba=ba, n=n, p0=p0: e.tensor_tensor(
                out=pT[:, p0:p0 + n], in0=cm.banks[ba][:, 0:n], in1=tmpA[par][:, 0:n], op=ALU.mult),
                reads=[cm.bank_b[ba], b_tmpA[par]], writes=[b_pT])
        w0, w1, w2 = (scw[:, k, j:j + 1] for k in range(3))
        kb.op("act", lambda e, w1=w1: e.activation(out=qT[:, 0:ntok], in_=pT[:, 1:ntok + 1], func=AF.Copy, scale=w1),
              reads=[b_pT, b_par], writes=[b_qT])
        kb.op("dve", lambda e, w0=w0: e.scalar_tensor_tensor(out=qT[:, 0:ntok], in0=pT[:, 0:ntok], scalar=w0,
                                                             in1=qT[:, 0:ntok], op0=ALU.mult, op1=ALU.add),
              reads=[b_pT, b_par, b_qT], writes=[b_qT])
        kb.op("dve", lambda e, w2=w2: e.scalar_tensor_tensor(out=qT[:, 0:ntok], in0=pT[:, 2:ntok + 2], scalar=w2,
                                                             in1=qT[:, 0:ntok], op0=ALU.mult, op1=ALU.add),
              reads=[b_pT, b_par, b_qT], writes=[b_qT])
        for b in range(nob):
            bk = 4 + (b % 2)
            proj(s, 0, bk, b * 512 + 16, 512)
            kb.op("dve", lambda e, bk=bk, b=b, j=j: e.tensor_tensor(
                out=yT[:, CJ + j, b * 512:(b + 1) * 512], in0=cm.banks[bk][:], in1=qT[:, b * 512:(b + 1) * 512],
                op=ALU.mult),
                reads=[cm.bank_b[bk], b_qT], writes=[b_yT[CJ + j][b]])

    kb.barrier()
    es_ab.close()
    ub = sb_c("c_ub", [128, CJ, 512], F32)
    b_ub = [kb.buf("c_ub%d" % j) for j in range(CJ)]
    ub2 = sb_c("c_ub2", [128, CJ, 512], F32)
    b_ub2 = [kb.buf("c_ub2%d" % j) for j in range(CJ)]
    st_m = sb_c("c_stm", [128, 512], F32)
    st_r = sb_c("c_str", [128, 512], F32)
    b_st = kb.buf("c_st")
    for b in range(nob):
        for j in range(CJ):
            bk = j % 2
            rd = [b_uT[j][bb] for bb in range(nub) if bb * 512 < b * 512 + 542 and (bb + 1) * 512 > b * 512]
            for k in range(CK):
                kb.op("pe", lambda e, k=k, j=j, bk=bk, b=b: e.matmul(
                    cm.banks[bk][:], lhsT=dg[:, k * CJ + j, :], rhs=uT[:, j, b * 512 + k:b * 512 + k + 512],
                    start=(k == 0), stop=(k == CK - 1)),
                    reads=[b_dg] + rd, writes=[cm.bank_b[bk]], sig=(k == CK - 1))
            kb.op("act", lambda e, j=j, bk=bk: e.activation(out=ub[:, j, :], in_=cm.banks[bk][:], func=AF.Identity,
                                                            bias=vec[:, 0, j:j + 1], scale=1.0),
                  reads=[cm.bank_b[bk], b_par], writes=[b_ub[j]])
            kb.op("act", lambda e, j=j: e.activation(out=ub2[:, j, :], in_=ub[:, j, :], func=AF.Square),
                  reads=[b_ub[j]], writes=[b_ub2[j]])
        for j in range(CJ):
            kb.op("pe", lambda e, j=j: e.matmul(cm.banks[2][:], lhsT=ones_m[:], rhs=ub[:, j, :],
                                                start=(j == 0), stop=(j == CJ - 1)),
                  reads=[b_ones, b_ub[j]], writes=[cm.bank_b[2]], sig=(j == CJ - 1))
        for j in range(CJ):
            kb.op("pe", lambda e, j=j: e.matmul(cm.banks[3][:], lhsT=ones_m[:], rhs=ub2[:, j, :],
                                                start=(j == 0), stop=(j == CJ - 1)),
                  reads=[b_ones, b_ub2[j]], writes=[cm.bank_b[3]], sig=(j == CJ - 1))
        kb.op("act", lambda e: e.copy(out=st_m[:], in_=cm.banks[2][:]), reads=[cm.bank_b[2]], writes=[b_st])
        kb.op("dve", lambda e: e.tensor_tensor(out=st_r[:], in0=st_m[:], in1=st_m[:], op=ALU.mult),
              reads=[b_st], writes=[b_st])
        kb.op("dve", lambda e: e.tensor_tensor(out=st_r[:], in0=cm.banks[3][:], in1=st_r[:], op=ALU.subtract),
              reads=[b_st, cm.bank_b[3]], writes=[b_st])
        kb.op("dve", lambda e: e.tensor_scalar_add(out=st_r[:], in0=st_r[:], scalar1=LN_EPS), reads=[b_st], writes=[b_st])
        kb.op("act", lambda e: e.activation(out=st_r[:], in_=st_r[:], func=AF.Sqrt), reads=[b_st], writes=[b_st])
        kb.op("dve", lambda e: e.reciprocal(out=st_r[:], in_=st_r[:]), reads=[b_st], writes=[b_st])
        for j in range(CJ):
            kb.op("dve", lambda e, j=j: e.tensor_tensor(out=ub[:, j, :], in0=ub[:, j, :], in1=st_m[:], op=ALU.subtract),
                  reads=[b_ub[j], b_st], writes=[b_ub[j]])
            kb.op("pool", lambda e, j=j: e.tensor_tensor(out=ub[:, j, :], in0=ub[:, j, :], in1=st_r[:], op=ALU.mult),
                  reads=[b_ub[j], b_st], writes=[b_ub[j]])
            kb.op("act", lambda e, j=j, b=b: e.activation(out=yT[:, j, b * 512:(b + 1) * 512], in_=ub[:, j, :],
                                                          func=AF.Silu, bias=vec[:, 2, j:j + 1], scale=vec[:, 1, j:j + 1]),
                  reads=[b_ub[j], b_par], writes=[b_yT[j][b]])

    kb.barrier()
    es_c.close()
    otile = [sb_d("c_ot%d" % i, [128, D], F32) for i in range(2)]
    b_ot = [kb.buf("c_ot%d" % i) for i in range(2)]
    def mm_fn(tg, yb):
        for h in range(2):
            for c in range(2 * CJ):
                kb.op("pe", lambda e, c=c, h=h, tg=tg: e.matmul(
                    cm.banks[yb[h]][:], lhsT=yT[:, c, tg * 128:(tg + 1) * 128], rhs=wo[:, c, h * 512:(h + 1) * 512],
                    start=(c == 0), stop=(c == 2 * CJ - 1)),
                    reads=[b_yT[c][tg // 4], b_wo[c]], writes=[cm.bank_b[yb[h]]], sig=(c == 2 * CJ - 1))
    emit_epilogue(kb, cm, lnp, lns, ntok // 128, xh, HALO_C, out, xst, b_xst, otile, b_ot, mm_fn)
    kb.barrier()
    es_d.close()


def build_conv(ntok=TOK):
    nc = bass.Bass("TRN2", target_bir_lowering=False)
    di = lambda n, s: nc.dram_tensor(n, s, F32, kind="ExternalInput").ap()
    xh = di("xh", [ntok + 2 * HALO_C, D])
    w_in = di("w_in", [8, 128, 3, D])
    cdw_w = di("cdw_w", [CK, CH])
    cdw_b = di("cdw_b", [CH])
    cln_g = di("cln_g", [CH])
    cln_b = di("cln_b", [CH])
    sconv_w = di("sconv_w", [3, CH])
    w_out = di("w_out", [D, D])
    ln_g = di("ln_g", [D])
    ln_b = di("ln_b", [D])
    out = nc.dram_tensor("out", [ntok, D], F32, kind="ExternalOutput").ap()
    with ExitStack() as es:
        kb = KB(nc, es)
        cm = Common(kb)
        emit_conv(kb, cm, xh, out, w_in, cdw_w, cdw_b, cln_g, cln_b, sconv_w, w_out, ln_g, ln_b, ntok=ntok)
        kb.finish()
    return nc


HA = 1024
NTA = TOK + 2 * HA
NEG = -1e30
DILS = (1, 4, 16)
NA_OFFS = {0: (-2, -1, 0, 1, 2, 3), 15: (-3, -2, -1, 0, 1, 2)}
NA_DEF = (-2, -1, 0, 1, 2)
NA_NV = 22


def na_mask_index():
    idx = {}
    n = 0
    for cls, offs in (("I", NA_DEF), (0, NA_OFFS[0]), (1, NA_DEF), (14, NA_DEF), (15, NA_OFFS[15])):
        for d in offs:
            idx[(cls, d)] = n
            n += 1
    return idx, n


def dil_vtiles():
    tiles = []
    for dil in DILS:
        nq = TOK // dil // 128
        for r in range(dil):
            for s in range(nq + 1):
                tiles.append((dil, r, 128 * s - 64))
    return tiles


def emit_attn(kb, cm, xh, out, w_in, w_out, bias_all, ln_g, ln_b, n_pairs=8):
    nc = kb.nc
    dvt = dil_vtiles()
    dvt_idx = {t: i for i, t in enumerate(dvt)}
    NVT = len(dvt)
    mask_idx, nmask = na_mask_index()

    xT = kb.sb("a_xT", [128, KC, NTA], BF16)
    b_xT = [kb.buf("a_xT%d" % i) for i in range(NTA // 128)]
    yT = kb.sb("a_yT", [128, KC, TOK], BF16)
    b_yT = [kb.buf("a_yT%d" % i) for i in range(KC)]
    onesp = kb.sb("a_onesp", [128, 2, 128], BF16)
    b_onesp = kb.buf("a_onesp")
    kb.op("dve", lambda e: e.memset(onesp[:], 0.0), writes=[b_onesp])
    kb.op("dve", lambda e: e.memset(onesp[:, 0, 0:64], 1.0), writes=[b_onesp])
    kb.op("dve", lambda e: e.memset(onesp[:, 1, 64:128], 1.0), writes=[b_onesp])

    with ExitStack() as es2:
        sb2 = lambda name, shape, dt: es2.enter_context(nc.sbuf_tensor(name, list(shape), dt))
        with ExitStack() as es0:
            xbf = [es0.enter_context(nc.sbuf_tensor("a_xbf%d" % i, [128, D], BF16)) for i in range(2)]
            b_xbf = [kb.buf("a_xbf%d" % i) for i in range(2)]
            for t in range(NTA // 128):
                s = t % 2
                load_transpose_bf16(kb, cm, xh[t * 128:(t + 1) * 128, :], 128, xbf[s], b_xbf[s], xT, t * 128, b_xT[t],
                                    bank=6 + s, evac_eng=("act" if s == 0 else "dve"))
        kb.barrier()

        w_bf = [sb2("a_wbf%d" % i, [128, 3, KC, 128], BF16) for i in range(2)]
        b_wbf = [kb.buf("a_wbf%d" % i) for i in range(2)]
        QT = sb2("a_QT", [128, TOK], BF16)
        b_QT = [kb.buf("a_QT%d" % i) for i in range(TOK // 512)]
        KT = sb2("a_KT", [128, NTA], BF16)
        b_KT = [kb.buf("a_KT%d" % i) for i in range(NTA // 512)]
        VT = sb2("a_VT", [128, NTA], BF16)
        b_VT = [kb.buf("a_VT%d" % i) for i in range(NTA // 512)]
        Vp = sb2("a_Vp", [128, NVT, 2, 128], BF16)
        b_Vp = [kb.buf("a_Vp%d" % i) for i in range(NVT)]
        kb.op("pool", lambda e: e.memset(Vp[:], 0.0), writes=b_Vp)
        NCOMB = nmask
        bias_t = sb2("a_bias", [128, NCOMB, 2, 128], BF16)
        b_bias = kb.buf("a_bias")
        NGS = 6
        PT = sb2("a_PT", [128, 2 * NGS, 2, 128], BF16)
        b_PT = [kb.buf("a_PT%d" % i) for i in range(NGS)]
        accN = sb2("a_accN", [128, TOK], F32)
        accD = sb2("a_accD", [128, TOK], F32)
        b_acc = kb.buf("a_acc")
        rden = [sb2("a_rden%d" % i, [128, 128], F32) for i in range(2)]
        b_rden = [kb.buf("a_rden%d" % i) for i in range(2)]

        def load_pair_params(pr_):
            s_ = pr_ % 2
            kb.dma(lambda e: e.dma_start(out=w_bf[s_][:].rearrange("p g k c -> p g (k c)"), in_=w_in[pr_]),
                   writes=[b_wbf[s_]], q="pool")

        def load_pair_bias(pr_):
            ncomb = NCOMB if pr_ < 4 else 24
            for m0 in range(0, ncomb, 7):
                m1 = min(ncomb, m0 + 7)
                kb.dma(lambda e, m0=m0, m1=m1: e.dma_start(
                    out=bias_t[:, m0:m1, :, :].rearrange("p m h q -> p (m h q)"),
                    in_=bias_all[pr_][:, m0:m1, :, :].rearrange("p m h q -> p (m h q)")),
                    writes=[b_bias], q="pool")

        load_pair_params(0)
        b_S = [kb.buf("a_S%d" % i) for i in range(3)]
        ND_slots = [(2, 0), (3, 0)]
        cnt = {"w": 0, "step": 0, "job": 0, "pj": 0}

        def blocks(c0, c1, width, bufs):
            return [bufs[i] for i in range(c0 // width, (c1 - 1) // width + 1)]

        for pr in range(n_pairs):
            is_na = pr < 4
            hp = pr % 4
            s = pr % 2
            load_pair_bias(pr)
            if pr + 1 < n_pairs:
                load_pair_params(pr + 1)
            def projT(i, dst, dst_bufs, c_lo, c_hi, xoff, scale, eng="dve"):
                for c0 in range(c_lo, c_hi, 512):
                    n = min(512, c_hi - c0)
                    bk = 4 + (cnt["pj"] % 2)
                    cnt["pj"] += 1
                    for kc in range(KC):
                        kb.op("pe", lambda e, kc=kc, bk=bk, c0=c0, n=n: e.matmul(
                            cm.banks[bk][:, 0:n], lhsT=w_bf[s][:, i, kc, :], rhs=xT[:, kc, c0 + xoff:c0 + xoff + n],
                            start=(kc == 0), stop=(kc == KC - 1)),
                            reads=[b_wbf[s]] + blocks(c0 + xoff, c0 + xoff + n, 128, b_xT), writes=[cm.bank_b[bk]],
                            sig=(kc == KC - 1))
                    if scale is None and eng == "dve":
                        kb.op("dve", lambda e, bk=bk, c0=c0, n=n: e.tensor_copy(out=dst[:, c0:c0 + n], in_=cm.banks[bk][:, 0:n]),
                              reads=[cm.bank_b[bk]], writes=blocks(c0, c0 + n, 512, dst_bufs))
                    elif scale is None:
                        kb.op("act", lambda e, bk=bk, c0=c0, n=n: e.copy(out=dst[:, c0:c0 + n], in_=cm.banks[bk][:, 0:n]),
                              reads=[cm.bank_b[bk]], writes=blocks(c0, c0 + n, 512, dst_bufs))
                    else:
                        kb.op("act", lambda e, bk=bk, c0=c0, n=n: e.mul(out=dst[:, c0:c0 + n], in_=cm.banks[bk][:, 0:n], mul=scale),
                              reads=[cm.bank_b[bk]], writes=blocks(c0, c0 + n, 512, dst_bufs))

            projT(0, QT, b_QT, 0, TOK, HA, 0.125)
            if is_na:
                projT(1, KT, b_KT, HA - 384, HA + TOK + 384, 0, None)
                projT(2, VT, b_VT, HA - 384, HA + TOK + 384, 0, None, eng="act")
            else:
                projT(1, KT, b_KT, 0, NTA, 0, None)
                projT(2, VT, b_VT, 0, NTA, 0, None, eng="act")

            def vtile(slot, xc0, step):
                bk = 4 + (cnt["pj"] % 2)
                cnt["pj"] += 1
                hi = xc0 + step * 127 + 1
                pv = cm.banks[bk].bitcast(BF16)
                kb.op("pe", lambda e: e.transpose(out=pv[:, 0:128], in_=VT[:, xc0:hi:step], identity=cm.ident_b[:]),
                      reads=blocks(xc0, hi, 512, b_VT) + [cm.b_ident], writes=[cm.bank_b[bk]])
                kb.op("act", lambda e: e.copy(out=Vp[:, slot, 0, 0:64], in_=pv[:, 0:64]),
                      reads=[cm.bank_b[bk]], writes=[b_Vp[slot]])
                kb.op("dve", lambda e: e.tensor_copy(out=Vp[:, slot, 1, 64:128], in_=pv[:, 64:128]),
                      reads=[cm.bank_b[bk]], writes=[b_Vp[slot]])

            if is_na:
                for t in range(NA_NV):
                    vtile(t, HA + (t - 3) * 128, 1)
            else:
                for i, (dil, r, ms) in enumerate(dvt):
                    vtile(i, HA + dil * ms + r, dil)

            jobs = []
            if is_na:
                for qi in range(16):
                    offs = NA_OFFS.get(qi, NA_DEF)
                    cls = qi if qi in (0, 1, 14, 15) else "I"
                    steps = []
                    for d in offs:
                        steps.append(dict(kc0=HA + (qi + d) * 128, kstep=1, vslot=qi + d + 3, kvcol=qi + d + 3,
                                          comb=mask_idx[(cls, d)]))
                    jobs.append(dict(qc0=qi * 128, qstep=1, steps=steps, pat=None))
            else:
                for p, dil in enumerate(DILS):
                    for r in range(dil):
                        for j in range(TOK // dil // 128):
                            steps = []
                            nj = TOK // dil // 128
                            v = (1 if j == 0 else 0) + (2 if j == nj - 1 else 0)
                            for kt in range(2):
                                ms = 128 * (j + kt) - 64
                                vi = dvt_idx[(dil, r, ms)]
                                steps.append(dict(kc0=HA + dil * ms + r, kstep=dil, vslot=vi, kvcol=NA_NV + vi,
                                                  comb=p * 8 + v * 2 + kt))
                            jobs.append(dict(qc0=r + dil * 128 * j, qstep=dil, steps=steps, pat=p))


            def emit_group(jb, si0, nu):
                g = cnt["step"]
                cnt["step"] += 1
                ssl = g % 3
                gs = g % NGS
                bp = cm.bankpair[(0, 3, 2)[ssl]]
                sbufs = [b_S[ssl]] + ([cm.bank_b[4], cm.bank_b[5]] if ssl == 2 else [])
                q0, qs = jb["qc0"], jb["qstep"]
                qhi = q0 + qs * 127 + 1
                for u in range(nu):
                    st = jb["steps"][si0 + u]
                    st["slot"] = 2 * gs + u
                    st["gs"] = gs
                    k0, ks = st["kc0"], st["kstep"]
                    khi = k0 + ks * 127 + 1
                    for h in range(2):
                        o = bp[:, h * 512 + u * 128:h * 512 + (u + 1) * 128]
                        kb.op("pe", lambda e, o=o, h=h, k0=k0, khi=khi, ks=ks: e.matmul(
                            o, lhsT=KT[h * 64:(h + 1) * 64, k0:khi:ks], rhs=QT[h * 64:(h + 1) * 64, q0:qhi:qs],
                            start=True, stop=True),
                            reads=blocks(k0, khi, 512, b_KT) + blocks(q0, qhi, 512, b_QT), writes=sbufs,
                            sig=(h == 1 and u == nu - 1))
                c0 = jb["steps"][si0]["comb"]
                for u in range(nu):
                    assert jb["steps"][si0 + u]["comb"] == c0 + u
                sview = bp[:].rearrange("p (b u q) -> p b u q", b=2, q=128)[:, :, 0:nu, :]
                kb.op("dve", lambda e: e.tensor_tensor(out=sview, in0=sview,
                                                       in1=bias_t[:, c0:c0 + nu, :, :].rearrange("p u h q -> p h u q"),
                                                       op=ALU.add),
                      reads=sbufs + [b_bias], writes=sbufs)
                kb.op("act", lambda e: e.activation(out=PT[:, 2 * gs:2 * gs + nu, :, :].rearrange("p u h q -> p h u q"),
                                                    in_=sview, func=AF.Exp),
                      reads=sbufs, writes=[b_PT[gs]])

            def emit_pv(jb):
                jb["nd"] = cnt["job"] % 2
                cnt["job"] += 1
                nbk, nco = ND_slots[jb["nd"]]
                ns = len(jb["steps"])
                for which in range(2):
                    o = cm.banks[nbk][:, nco + which * 128:nco + (which + 1) * 128]
                    for si in range(ns):
                        st = jb["steps"][si]
                        slot = st["slot"]
                        for h in range(2):
                            lhs = Vp[:, st["vslot"], h, :] if which == 0 else onesp[:, h, :]
                            rd = [b_Vp[st["vslot"]]] if which == 0 else [b_onesp]
                            first = (si == 0 and h == 0)
                            last = (si == ns - 1 and h == 1)
                            kb.op("pe", lambda e, o=o, lhs=lhs, h=h, first=first, last=last, slot=slot: e.matmul(
                                o, lhsT=lhs, rhs=PT[:, slot, h, :], start=first, stop=last),
                                reads=rd + [b_PT[st["gs"]]], writes=[cm.bank_b[nbk]], sig=(last and which == 1))
                q0, qs = jb["qc0"], jb["qstep"]
                qhi = q0 + qs * 127 + 1
                num = cm.banks[nbk][:, nco:nco + 128]
                den = cm.banks[nbk][:, nco + 128:nco + 256]
                if is_na:
                    ri = jb["nd"]
                    kb.op("dve", lambda e: e.reciprocal(out=rden[ri][:], in_=den), reads=[cm.bank_b[nbk]],
                          writes=[b_rden[ri]])
                    kb.op("dve", lambda e: e.tensor_tensor(out=yT[:, hp, q0:qhi:qs], in0=num, in1=rden[ri][:],
                                                           op=ALU.mult),
                          reads=[cm.bank_b[nbk], b_rden[ri]], writes=[b_yT[hp]])
                elif jb["pat"] == 0:
                    kb.op("dve", lambda e: e.tensor_copy(out=accN[:, q0:qhi:qs], in_=num), reads=[cm.bank_b[nbk]], writes=[b_acc])
                    kb.op("dve", lambda e: e.tensor_copy(out=accD[:, q0:qhi:qs], in_=den), reads=[cm.bank_b[nbk]], writes=[b_acc])
                else:
                    kb.op("dve", lambda e: e.tensor_tensor(out=accN[:, q0:qhi:qs], in0=num, in1=accN[:, q0:qhi:qs],
                                                           op=ALU.add), reads=[cm.bank_b[nbk], b_acc], writes=[b_acc])
                    kb.op("dve", lambda e: e.tensor_tensor(out=accD[:, q0:qhi:qs], in0=den, in1=accD[:, q0:qhi:qs],
                                                           op=ALU.add), reads=[cm.bank_b[nbk], b_acc], writes=[b_acc])

            for ji in range(len(jobs) + 1):
                if ji < len(jobs):
                    ns_ = len(jobs[ji]["steps"])
                    for si in range(0, ns_, 2):
                        emit_group(jobs[ji], si, min(2, ns_ - si))
                if ji >= 1:
                    emit_pv(jobs[ji - 1])
            if not is_na:
                kb.op("dve", lambda e: e.reciprocal(out=accD[:], in_=accD[:]), reads=[b_acc], writes=[b_acc])
                kb.op("dve", lambda e: e.tensor_tensor(out=yT[:, 4 + hp, :], in0=accN[:], in1=accD[:], op=ALU.mult),
                      reads=[b_acc], writes=[b_yT[4 + hp]])
    kb.barrier()

    lnp = LNParams(kb, "a_ln", ln_g, ln_b)
    lns = LNScratch(kb, "a_lns", n=2)
    wo = kb.sb("a_wo", [128, KC, D], BF16)
    b_wo = [kb.buf("a_wo%d" % c) for c in range(KC)]
    w_out_v = w_out.rearrange("(c p) n -> p c n", p=128)
    for c in range(KC):
        kb.dma(lambda e, c=c: e.dma_start(out=wo[:, c, :], in_=w_out_v[:, c, :]), writes=[b_wo[c]], q="pool")
    xres = [kb.sb("a_xres%d" % i, [128, D], F32) for i in range(2)]
    b_xres = [kb.buf("a_xres%d" % i) for i in range(2)]
    otile = [kb.sb("a_ot%d" % i, [128, D], F32) for i in range(2)]
    b_ot = [kb.buf("a_ot%d" % i) for i in range(2)]
    def mm_fn(tg, yb):
        for h in range(2):
            for c in range(KC):
                kb.op("pe", lambda e, c=c, h=h, tg=tg: e.matmul(
                    cm.banks[yb[h]][:], lhsT=yT[:, c, tg * 128:(tg + 1) * 128], rhs=wo[:, c, h * 512:(h + 1) * 512],
                    start=(c == 0), stop=(c == KC - 1)),
                    reads=[b_yT[c], b_wo[c]], writes=[cm.bank_b[yb[h]]], sig=(c == KC - 1))
    emit_epilogue(kb, cm, lnp, lns, TOK // 128, xh, HA, out, xres, b_xres, otile, b_ot, mm_fn)


def build_attn(n_pairs=8):
    nc = bass.Bass("TRN2", target_bir_lowering=False)
    di = lambda n, s: nc.dram_tensor(n, s, F32, kind="ExternalInput").ap()
    _, nmask = na_mask_index()
    xh = di("xh", [NTA, D])
    w_in = di("w_in", [8, 128, 3, D])
    w_out = di("w_out", [D, D])
    bias_all = di("bias_all", [8, 128, nmask, 2, 128])
    ln_g = di("ln_g", [D])
    ln_b = di("ln_b", [D])
    out = nc.dram_tensor("out", [TOK, D], F32, kind="ExternalOutput").ap()
    with ExitStack() as es:
        kb = KB(nc, es)
        cm = Common(kb)
        emit_attn(kb, cm, xh, out, w_in, w_out, bias_all, ln_g, ln_b, n_pairs=n_pairs)
        kb.finish()
    return nc


def t5_bucket_np(rel):
    nb = 16
    max_exact = 8
    ret = np.where(rel > 0, nb, 0)
    n = np.abs(rel)
    large = max_exact + (np.log(np.maximum(n, 1).astype(np.float32) / max_exact)
                         / np.float32(np.log(1024 / max_exact)) * (nb - max_exact)).astype(np.int32)
    large = np.minimum(large, nb - 1)
    return ret + np.where(n < max_exact, n, large)


def make_dil_bias(t5_bias):
    k = np.arange(128)[:, None]
    q = np.arange(128)[None, :]
    res = np.empty((3, 8, 2, 128, 128), np.float32)
    for p, dil in enumerate(DILS):
        for kt in range(2):
            rel = (128 * kt - 64 + k) - q
            bk = t5_bucket_np(rel * dil)
            g = t5_bias[bk]
            valid = np.abs(rel) <= 64
            for h in range(8):
                res[p, h, kt] = np.where(valid, g[:, :, h], np.float32(NEG))
    return res


def make_na_bias(rpb):
    k = np.arange(128)[:, None]
    q = np.arange(128)[None, :]
    res = np.zeros((8, 7, 128, 128), np.float32)
    for di_, d in enumerate(range(-3, 4)):
        rr = 2 * d + k // 64 - q // 64
        cr = k % 64 - q % 64
        ok = (np.abs(rr) <= 7) & (np.abs(cr) <= 15)
        rri = np.clip(rr + 7, 0, 14)
        cri = np.clip(cr + 15, 0, 30)
        for h in range(8):
            res[h, di_] = np.where(ok, rpb[h][rri, cri], np.float32(0))
    return res


def make_na_mask(seg):
    idx, n = na_mask_index()
    res = np.empty((n, 128, 128), np.float32)
    k = np.arange(128)[:, None]
    q = np.arange(128)[None, :]
    for (cls, d), m in idx.items():
        qi = 5 if cls == "I" else cls
        gq = seg * 16 + qi
        r = 2 * gq + q // 64
        c = q % 64
        kr = 2 * (gq + d) + k // 64
        kc = k % 64
        r0 = np.clip(r - 4, 0, 120)
        c0 = np.clip(c - 8, 0, 48)
        ok = (kr >= r0) & (kr < r0 + 8) & (kc >= c0) & (kc < c0 + 16)
        res[m] = np.where(ok, np.float32(0), np.float32(NEG))
    return res


def make_kvb(seg):
    t0 = seg * TOK
    cols = []
    p = np.arange(128)
    for t in range(NA_NV):
        tok = t0 + (t - 3) * 128 + p
        cols.append(tok)
    for (dil, r, ms) in dil_vtiles():
        tok = t0 + dil * (ms + p) + r
        cols.append(tok)
    tok = np.stack(cols, axis=1)
    return np.where((tok >= 0) & (tok < SEQ), np.float32(0), np.float32(NEG)).astype(np.float32)


def relayout_w_up(w_up):
    dff = w_up.shape[1] // 2
    npair = dff // 128
    w = w_up.reshape(KC, 128, 2, npair, 128)
    return np.ascontiguousarray(w.transpose(3, 1, 0, 2, 4)).reshape(npair, 128, KC * 2 * 128)


def relayout_groups(w, groups):
    res = np.zeros((len(groups), 128, 3, KC, 128), np.float32)
    wv = w.reshape(KC, 128, -1)
    for g, cols in enumerate(groups):
        for i, c0 in enumerate(cols):
            res[g, :, i] = wv[:, :, c0:c0 + 128].transpose(1, 0, 2)
    return res.reshape(len(groups), 128, 3, KC * 128)


CONV_GROUPS = [[j * 128, CH + j * 128] for j in range(CJ)] + \
              [[2 * CH + j * 128, 3 * CH + j * 128, 4 * CH + j * 128] for j in range(CJ)]
ATTN_GROUPS = [[(0 if pr < 4 else 3 * CH) + i * CH + (pr % 4) * 128 for i in range(3)] for pr in range(8)]


def make_bias_all(na_bias, na_mask, dil_bias, seg, nseg):
    idx, n = na_mask_index()
    res = np.zeros((8, n, 2, 128, 128), np.float32)
    k = np.arange(128)[:, None]
    for pr in range(8):
        hp = pr % 4
        for h in range(2):
            hg = hp * 2 + h
            if pr < 4:
                for (cls, d), m in idx.items():
                    res[pr, m, h] = np.where(na_mask[m] == 0, na_bias[hg, d + 3], np.float32(NEG))
            else:
                for p in range(3):
                    for v in range(4):
                        for kt in range(2):
                            t = dil_bias[p, hg, kt]
                            if kt == 0 and (v & 1) and seg == 0:
                                t = np.where(k < 64, np.float32(NEG), t)
                            if kt == 1 and (v & 2) and seg == nseg - 1:
                                t = np.where(k >= 64, np.float32(NEG), t)
                            res[pr, p * 8 + v * 2 + kt, h] = t
    return np.ascontiguousarray(res.transpose(0, 3, 1, 2, 4))


_PROGS = {}


def _prog(name):
    if name not in _PROGS:
        _PROGS[name] = {"attn": build_attn, "conv": build_conv, "ffn": build_ffn}[name]()
    return _PROGS[name]


def _shards_with_halo(x, halo):
    B, S, Dm = x.shape
    xp = np.zeros((B, S + 2 * halo, Dm), np.float32)
    xp[:, halo:halo + S] = x
    res = []
    for c in range(NCORES):
        b, seg = divmod(c, S // TOK)
        res.append(np.ascontiguousarray(xp[b, seg * TOK:seg * TOK + TOK + 2 * halo]))
    return res


def _gather(res, B, S):
    out = np.empty((B, S, D), np.float32)
    for c in range(NCORES):
        b, seg = divmod(c, S // TOK)
        out[b, seg * TOK:(seg + 1) * TOK] = res.results[c]["out"]
    return out


def _run(name, in_maps):
    return run_bass_kernel_spmd(_prog(name), in_maps, core_ids=list(range(NCORES)))


def kernel(x, t5_bias, attn_w_in, attn_w_out, na_rpb, conv_w_in, conf_dw_w, conf_dw_b, conf_ln_g, conf_ln_b,
           sconv_w, conv_w_out, ffn_w_up, ffn_dw_w, ffn_w_down, mix_ln_g, mix_ln_b, ffn_ln_g, ffn_ln_b):
    f = lambda a: np.ascontiguousarray(np.asarray(a, dtype=np.float32))
    x = f(x)
    B, S, _ = x.shape
    nseg = S // TOK
    dil_bias = make_dil_bias(f(t5_bias))
    na_masks = [make_na_mask(seg) for seg in range(nseg)]
    for i in range(4):
        j = i // 2
        if i % 2 == 0:
            xs = _shards_with_halo(x, HA)
            na_bias = make_na_bias(f(na_rpb[j]))
            common = dict(w_in=relayout_groups(f(attn_w_in[j]), ATTN_GROUPS), w_out=f(attn_w_out[j]),
                          ln_g=f(mix_ln_g[i]), ln_b=f(mix_ln_b[i]))
            biases = [make_bias_all(na_bias, na_masks[seg], dil_bias, seg, nseg) for seg in range(nseg)]
            maps = [dict(common, xh=xs[c], bias_all=biases[c % nseg]) for c in range(NCORES)]
            x = _gather(_run("attn", maps), B, S)
        else:
            xs = _shards_with_halo(x, HALO_C)
            common = dict(w_in=relayout_groups(f(conv_w_in[j]), CONV_GROUPS), cdw_w=f(conf_dw_w[j]), cdw_b=f(conf_dw_b[j]), cln_g=f(conf_ln_g[j]),
                          cln_b=f(conf_ln_b[j]), sconv_w=f(sconv_w[j]), w_out=f(conv_w_out[j]),
                          ln_g=f(mix_ln_g[i]), ln_b=f(mix_ln_b[i]))
            maps = [dict(common, xh=xs[c]) for c in range(NCORES)]
            x = _gather(_run("conv", maps), B, S)
        xs = _shards_with_halo(x, 1)
        common = dict(w_up=relayout_w_up(f(ffn_w_up[i])), dw=f(ffn_dw_w[i]), w_down=f(ffn_w_down[i]),
                      ln_g=f(ffn_ln_g[i]), ln_b=f(ffn_ln_b[i]))
        maps = [dict(common, xh=xs[c]) for c in range(NCORES)]
        x = _gather(_run("ffn", maps), B, S)
    return x
```

```python
from contextlib import ExitStack

import numpy as np
import concourse.bass as bass
import concourse.mybir as mybir
from concourse.bass_utils import run_bass_kernel_spmd

F32 = mybir.dt.float32
BF16 = mybir.dt.bfloat16
AF = mybir.ActivationFunctionType
ALU = mybir.AluOpType
AX = mybir.AxisListType

D = 1024
KC = D // 128
SEQ = 8192
BATCH = 2
NCORES = 8
TOK = 2048
DFF = 2816
NPAIR = DFF // 128
ALPHA = 8.0 ** 0.25
LN_EPS = 1e-5
import os
SAME_SYNC_ENGINES = set(os.environ.get('SAME_SYNC', 'pe,act,dve,pool,sp').split(','))
NDMA = 24


class Buf:
    __slots__ = ("name", "w", "r")

    def __init__(self, name):
        self.name = name
        self.w = None
        self.r = []


class KB:
    def __init__(self, nc, es):
        self.nc = nc
        self.es = es
        self.E = {"pe": nc.tensor, "act": nc.scalar, "dve": nc.vector, "pool": nc.gpsimd, "sp": nc.sync}
        self.sems = {}
        self.cnt = {}
        for k in ("pe", "act", "dve", "pool"):
            self.sems[k] = es.enter_context(nc.semaphore("s_" + k))
            self.cnt[k] = 0
        for i in range(NDMA):
            self.sems[("dma", i)] = es.enter_context(nc.semaphore("s_dma%d" % i))
            self.cnt[("dma", i)] = 0
        NSW = 12
        for i in range(NSW):
            self.sems[("swdma", i)] = es.enter_context(nc.semaphore("s_swdma%d" % i))
            self.cnt[("swdma", i)] = 0
        self.nsw = NSW
        self.rr_sw = 0
        self.rr = 0
        self.seen = {e: {} for e in self.E}
        self.pending = {e: False for e in self.E}
        self.out_events = []
        self.nbuf = 0

    def buf(self, name=None):
        self.nbuf += 1
        return Buf(name or "b%d" % self.nbuf)

    def sb(self, name, shape, dt):
        return self.es.enter_context(self.nc.sbuf_tensor(name, list(shape), dt))

    def ps(self, name, shape, dt):
        return self.es.enter_context(self.nc.psum_tensor(name, list(shape), dt))

    def _wait(self, eng, key, val):
        if val <= 0:
            return
        if key == eng and (eng not in SAME_SYNC_ENGINES or val > self.cnt[eng]):
            return
        if self.seen[eng].get(key, 0) >= val:
            return
        self.E[eng].wait_ge(self.sems[key], val)
        self.seen[eng][key] = val

    def _deps(self, eng, reads, writes):
        need = {}

        def add(ev):
            if ev is None:
                return
            k, v = ev
            if need.get(k, 0) < v:
                need[k] = v

        for b in reads:
            add(b.w)
        for b in writes:
            add(b.w)
            for r in b.r:
                add(r)
        for k, v in need.items():
            self._wait(eng, k, v)

    def _record(self, ev, reads, writes):
        for b in reads:
            b.r.append(ev)
            if len(b.r) > 64:
                mx = {}
                for k, v in b.r:
                    if mx.get(k, 0) < v:
                        mx[k] = v
                b.r = list(mx.items())
        for b in writes:
            b.w = ev
            b.r = []

    def op(self, eng, fn, reads=(), writes=(), sig=True):
        self._deps(eng, reads, writes)
        ins = fn(self.E[eng])
        if sig:
            self.cnt[eng] += 1
            ins.then_inc(self.sems[eng], 1)
            ev = (eng, self.cnt[eng])
            self.pending[eng] = False
        else:
            ev = (eng, self.cnt[eng] + 1)
            self.pending[eng] = True
        self._record(ev, reads, writes)
        return ev

    def dma(self, fn, reads=(), writes=(), q="sp", is_output=False):
        self._deps(q, reads, writes)
        if q == "pool":
            i = self.rr_sw
            self.rr_sw = (self.rr_sw + 1) % self.nsw
            key = ("swdma", i)
        else:
            i = self.rr
            self.rr = (self.rr + 1) % NDMA
            key = ("dma", i)
        self._wait(q, key, self.cnt[key])
        ins = fn(self.E[q])
        self.cnt[key] += 16
        ins.then_inc(self.sems[key], 16)
        ev = (key, self.cnt[key])
        self._record(ev, reads, writes)
        if is_output:
            self.out_events.append(ev)
        return ev

    def barrier(self):
        for e in ("pe", "act", "dve", "pool", "sp"):
            for k, v in self.cnt.items():
                if k != e:
                    self._wait(e, k, v)

    def finish(self):
        for e, p in self.pending.items():
            assert not p, "engine %s has unsignaled trailing ops" % e
        for k, v in self.out_events:
            self._wait("sp", k, v)


class Common:
    def __init__(self, kb):
        nc = kb.nc
        self.kb = kb
        self.bankpair = [kb.ps("bankpair%d" % i, [128, 1024], F32) for i in range(4)]
        self.banks = []
        for bp in self.bankpair:
            self.banks += [bp[:, 0:512], bp[:, 512:1024]]
        self.bank_b = [kb.buf("bank%d" % i) for i in range(8)]
        self.ident_f = kb.sb("ident_f", [128, 128], F32)
        self.ident_b = kb.sb("ident_b", [128, 128], BF16)
        self.b_ident = kb.buf("ident")
        self.ones_f = kb.sb("ones_f", [128, 128], F32)

        kb.op("pool", lambda e: e.memset(self.ones_f[:], 1.0), writes=[self.b_ident])
        kb.op("pool", lambda e: e.affine_select(
            out=self.ident_f[:], in_=self.ones_f[:], pattern=[[-1, 128]], compare_op=ALU.is_equal,
            fill=0.0, base=0, channel_multiplier=1), reads=[self.b_ident], writes=[self.b_ident])
        kb.op("pool", lambda e: e.tensor_copy(out=self.ident_b[:], in_=self.ident_f[:]),
              reads=[self.b_ident], writes=[self.b_ident])


def load_transpose_tile(kb, cm, x_src, xT, col0, b_xT, xtile, b_xtile, banks, evac_eng="act"):
    kb.dma(lambda e: e.dma_start(out=xtile[:], in_=x_src), writes=[b_xtile])
    transpose_tile(kb, cm, xtile, b_xtile, xT, col0, b_xT, banks, evac_eng)


def transpose_tile(kb, cm, xtile, b_xtile, xT, col0, b_xT, banks, evac_eng="act"):
    for half in range(2):
        bk = banks[half]
        for j in range(4):
            c = half * 4 + j
            kb.op("pe", lambda e, c=c, j=j, bk=bk: e.transpose(
                out=cm.banks[bk][:, j * 128:(j + 1) * 128], in_=xtile[:, c * 128:(c + 1) * 128],
                identity=cm.ident_f[:]),
                reads=[b_xtile, cm.b_ident], writes=[cm.bank_b[bk]], sig=(j == 3))
        src = cm.banks[bk][:].rearrange("p (c t) -> p c t", c=4)
        dst = xT[:, half * 4:(half + 1) * 4, col0:col0 + 128]
        if evac_eng == "act":
            kb.op("act", lambda e, src=src, dst=dst: e.copy(out=dst, in_=src),
                  reads=[cm.bank_b[bk]], writes=[b_xT])
        else:
            kb.op("dve", lambda e, src=src, dst=dst: e.tensor_copy(out=dst, in_=src),
                  reads=[cm.bank_b[bk]], writes=[b_xT])


def load_transpose_bf16(kb, cm, src_rows, rows, xtile, b_xtile, xT, col0, b_xT, bank, evac_eng):
    if rows < 128:
        kb.op("dve", lambda e: e.memset(xtile[:], 0.0), writes=[b_xtile])
    kb.dma(lambda e: e.dma_start(out=xtile[0:rows, :], in_=src_rows), writes=[b_xtile], q="pool")
    pv = cm.banks[bank].bitcast(BF16)
    for c in range(KC):
        kb.op("pe", lambda e, c=c: e.transpose(out=pv[:, c * 128:(c + 1) * 128], in_=xtile[:, c * 128:(c + 1) * 128],
                                               identity=cm.ident_b[:]),
              reads=[b_xtile, cm.b_ident], writes=[cm.bank_b[bank]], sig=(c == KC - 1))
    src = pv[:, 0:KC * 128].rearrange("p (c t) -> p c t", c=KC)
    dst = xT[:, :, col0:col0 + 128]
    if evac_eng == "act":
        kb.op("act", lambda e: e.copy(out=dst, in_=src), reads=[cm.bank_b[bank]], writes=[b_xT])
    else:
        kb.op("dve", lambda e: e.tensor_copy(out=dst, in_=src), reads=[cm.bank_b[bank]], writes=[b_xT])


def emit_epilogue(kb, cm, lnp, lns, ntiles, xh, row0, out, xres, b_xres, otile, b_ot, mm_fn, yb=(6, 7)):
    def load(tg):
        s = tg % 2
        kb.dma(lambda e: e.dma_start(out=xres[s][:], in_=xh[row0 + tg * 128:row0 + (tg + 1) * 128, :]),
               writes=[b_xres[s]])
    load(0)
    for tg in range(ntiles):
        s = tg % 2
        if tg + 1 < ntiles:
            load(tg + 1)
        mm_fn(tg, yb)
        residual_ln(kb, cm, lns, lnp, xres[s], b_xres[s], yb, otile[s][:], b_ot[s])
        kb.dma(lambda e, s=s, tg=tg: e.dma_start(out=out[tg * 128:(tg + 1) * 128, :], in_=otile[s][:]),
               reads=[b_ot[s]], is_output=True)


class LNParams:
    def __init__(self, kb, name, g_ap, b_ap):
        self.g = kb.sb(name + "_g", [128, D], F32)
        self.b = kb.sb(name + "_b", [128, D], F32)
        self.buf = kb.buf(name)
        kb.dma(lambda e: e.dma_start(out=self.g[:], in_=g_ap.partition_broadcast(128)), writes=[self.buf])
        kb.dma(lambda e: e.dma_start(out=self.b[:], in_=b_ap.partition_broadcast(128)), writes=[self.buf])


class LNScratch:
    def __init__(self, kb, name, n=2):
        self.n = n
        self.i = 0
        self.z = [kb.sb("%s_z%d" % (name, i), [128, D], F32) for i in range(n)]
        self.st = [kb.sb("%s_st%d" % (name, i), [128, 12], F32) for i in range(n)]
        self.mv = [kb.sb("%s_mv%d" % (name, i), [128, 4], F32) for i in range(n)]
        self.bz = [kb.buf("%s_bz%d" % (name, i)) for i in range(n)]
        self.bs = [kb.buf("%s_bs%d" % (name, i)) for i in range(n)]

    def next(self):
        i = self.i
        self.i = (self.i + 1) % self.n
        return i


def residual_ln(kb, cm, lns, lnp, xres, b_xres, ybanks, out_tile, b_out):
    i = lns.next()
    z, st, mv, bz, bs = lns.z[i], lns.st[i], lns.mv[i], lns.bz[i], lns.bs[i]
    for h in range(2):
        kb.op("dve", lambda e, h=h: e.scalar_tensor_tensor(
            out=z[:, h * 512:(h + 1) * 512], in0=xres[:, h * 512:(h + 1) * 512], scalar=ALPHA,
            in1=cm.banks[ybanks[h]][:], op0=ALU.mult, op1=ALU.add),
            reads=[b_xres, cm.bank_b[ybanks[h]]], writes=[bz])
    ln_core(kb, lns, i, lnp, out_tile, b_out)


def ln_core(kb, lns, i, lnp, out_tile, b_out, width=D):
    z, st, mv, bz, bs = lns.z[i], lns.st[i], lns.mv[i], lns.bz[i], lns.bs[i]
    nch = width // 512
    for h in range(nch):
        kb.op("dve", lambda e, h=h: e.bn_stats(out=st[:, h * 6:(h + 1) * 6], in_=z[:, h * 512:(h + 1) * 512]),
              reads=[bz], writes=[bs])
    kb.op("dve", lambda e: e.bn_aggr(out=mv[:, 0:2], in_=st[:, 0:6 * nch]), reads=[bs], writes=[bs])
    kb.op("dve", lambda e: e.tensor_scalar_add(out=mv[:, 2:3], in0=mv[:, 1:2], scalar1=LN_EPS), reads=[bs], writes=[bs])
    kb.op("act", lambda e: e.activation(out=mv[:, 2:3], in_=mv[:, 2:3], func=AF.Sqrt), reads=[bs], writes=[bs])
    kb.op("dve", lambda e: e.reciprocal(out=mv[:, 2:3], in_=mv[:, 2:3]), reads=[bs], writes=[bs])
    kb.op("dve", lambda e: e.scalar_tensor_tensor(out=mv[:, 3:4], in0=mv[:, 0:1], scalar=-1.0, in1=mv[:, 2:3],
                                                  op0=ALU.mult, op1=ALU.mult), reads=[bs], writes=[bs])
    kb.op("act", lambda e: e.activation(out=z[:, 0:width], in_=z[:, 0:width], func=AF.Identity,
                                        bias=mv[:, 3:4], scale=mv[:, 2:3]),
          reads=[bz, bs], writes=[bz])
    kb.op("dve", lambda e: e.tensor_tensor(out=z[:, 0:width], in0=z[:, 0:width], in1=lnp.g[:, 0:width], op=ALU.mult),
          reads=[bz, lnp.buf], writes=[bz])
    kb.op("pool", lambda e: e.tensor_tensor(out=out_tile, in0=z[:, 0:width], in1=lnp.b[:, 0:width], op=ALU.add),
          reads=[bz, lnp.buf], writes=[b_out])


def emit_ffn(kb, cm, xh, out, w_up, dw, w_down, ln_g, ln_b, ntok=TOK, npair=NPAIR, half_tok=1024):
    nc = kb.nc
    ncols = ntok + 2
    xT = kb.sb("f_xT", [128, KC, ncols + 126], BF16)
    ntile_in = (ncols + 127) // 128
    b_xT = [kb.buf("f_xT%d" % i) for i in range(ntile_in)]
    xt_stage = [kb.sb("f_xst%d" % i, [128, D], F32) for i in range(2)]
    b_xst = [kb.buf("f_xst%d" % i) for i in range(2)]
    xbf = [kb.sb("f_xbf%d" % i, [128, D], BF16) for i in range(2)]
    b_xbf = [kb.buf("f_xbf%d" % i) for i in range(2)]
    lnp = LNParams(kb, "f_ln", ln_g, ln_b)
    lns = LNScratch(kb, "f_lns")
    ncht = dw.shape[1] // 128
    dwt = kb.sb("f_dw", [128, 3, ncht], F32)
    b_dw = kb.buf("f_dw")
    for k in range(3):
        kb.dma(lambda e, k=k: e.dma_start(out=dwt[:, k, :], in_=dw[k, :].rearrange("(c p) -> p c", p=128),
                                          allow_slow_non_contiguous=True), writes=[b_dw])
    wd = kb.sb("f_wd", [128, npair, D], BF16)
    b_wd = [kb.buf("f_wd%d" % c) for c in range(npair)]
    NWU = 3
    wu = [kb.sb("f_wu%d" % i, [128, KC, 2, 128], BF16) for i in range(NWU)]
    b_wu = [kb.buf("f_wu%d" % i) for i in range(NWU)]
    aT = kb.sb("f_aT", [128, npair, half_tok], BF16)
    blks = []
    o = 0
    while o < half_tok:
        n_ = min(510, half_tok - o)
        blks.append((o, n_))
        o += n_
    b_aT = [[kb.buf("f_aT%d_%d" % (c, b)) for b in range(len(blks))] for c in range(npair)]
    t1 = [[kb.sb("f_t1_%d_%d" % (i, j), [128, 512], F32) for j in range(2)] for i in range(2)]
    b_t1 = [[kb.buf("f_t1_%d_%d" % (i, j)) for j in range(2)] for i in range(2)]
    sg = [kb.sb("f_sg%d" % i, [128, 512], F32) for i in range(2)]
    b_sg = [kb.buf("f_sg%d" % i) for i in range(2)]
    xres, b_xres = xt_stage, b_xst
    otile = [kb.sb("f_ot%d" % i, [128, D], F32) for i in range(2)]
    b_ot = [kb.buf("f_ot%d" % i) for i in range(2)]

    for t in range(ntile_in):
        rows = min(128, ncols - t * 128)
        s = t % 2
        load_transpose_bf16(kb, cm, xh[t * 128:t * 128 + rows, :], rows, xbf[s], b_xbf[s], xT, t * 128, b_xT[t],
                            bank=6 + s, evac_eng=("act" if s == 0 else "dve"))

    w_down_v = w_down.rearrange("(c p) n -> p c n", p=128)
    nhalf = ntok // half_tok
    seq = [(hf, c) for hf in range(nhalf) for c in range(npair)]

    def load_pair(idx):
        hf_, c_ = seq[idx]
        s_ = idx % NWU
        kb.dma(lambda e: e.dma_start(out=wu[s_][:].rearrange("p a b c -> p (a b c)"), in_=w_up[c_]),
               writes=[b_wu[s_]], q="pool")
        if hf_ == 0:
            kb.dma(lambda e: e.dma_start(out=wd[:, c_, :], in_=w_down_v[:, c_, :]), writes=[b_wd[c_]], q="pool")

    load_pair(0)
    for idx, (hf, c) in enumerate(seq):
        if True:
            s = idx % NWU
            if idx + 1 < len(seq):
                load_pair(idx + 1)
            for b, (o0, nout) in enumerate(blks):
                col0 = hf * half_tok + o0
                nin = nout + 2
                rd_x = [b_xT[t] for t in range(col0 // 128, (col0 + nin - 1) // 128 + 1)]
                par = (idx * len(blks) + b) % 2
                bg, bu = (0, 1) if par == 0 else (2, 3)
                for gi, bk in ((0, bg), (1, bu)):
                    for kc in range(KC):
                        kb.op("pe", lambda e, kc=kc, gi=gi, bk=bk, s=s, col0=col0, nin=nin: e.matmul(
                            cm.banks[bk][:, 0:nin], lhsT=wu[s][:, kc, gi, :], rhs=xT[:, kc, col0:col0 + nin],
                            start=(kc == 0), stop=(kc == KC - 1)),
                            reads=[b_wu[s]] + rd_x, writes=[cm.bank_b[bk]], sig=(kc == KC - 1))
                for gi, bk in ((0, bg), (1, bu)):
                    ch = c if gi == 0 else (ncht // 2 + c)
                    T = t1[par][gi]
                    bT = b_t1[par][gi]
                    A = cm.banks[bk]
                    w0, w1, w2 = (dwt[:, k, ch:ch + 1] for k in range(3))
                    kb.op("act", lambda e, T=T, A=A, w1=w1, nout=nout: e.activation(
                        out=T[:, 0:nout], in_=A[:, 1:nout + 1], func=AF.Copy, scale=w1),
                        reads=[cm.bank_b[bk], b_dw], writes=[bT])
                    kb.op("dve", lambda e, T=T, A=A, w0=w0, nout=nout: e.scalar_tensor_tensor(
                        out=T[:, 0:nout], in0=A[:, 0:nout], scalar=w0, in1=T[:, 0:nout], op0=ALU.mult, op1=ALU.add),
                        reads=[cm.bank_b[bk], b_dw, bT], writes=[bT])
                    kb.op("dve", lambda e, T=T, A=A, w2=w2, nout=nout: e.scalar_tensor_tensor(
                        out=T[:, 0:nout], in0=A[:, 2:nout + 2], scalar=w2, in1=T[:, 0:nout], op0=ALU.mult, op1=ALU.add),
                        reads=[cm.bank_b[bk], b_dw, bT], writes=[bT])
                kb.op("act", lambda e, par=par, nout=nout: e.activation(out=sg[par][:, 0:nout], in_=t1[par][0][:, 0:nout],
                                                                        func=AF.Silu),
                      reads=[b_t1[par][0]], writes=[b_sg[par]])
                kb.op("pool", lambda e, par=par, c=c, o0=o0, nout=nout: e.tensor_tensor(
                    out=aT[:, c, o0:o0 + nout], in0=sg[par][:, 0:nout], in1=t1[par][1][:, 0:nout], op=ALU.mult),
                    reads=[b_sg[par], b_t1[par][1]], writes=[b_aT[c][b]])
        if c != npair - 1:
            continue
        def mm_fn(tt, yb):
            bqs = [bi for bi, (o0, n_) in enumerate(blks) if o0 < (tt + 1) * 128 and o0 + n_ > tt * 128]
            for h in range(2):
                for c in range(npair):
                    kb.op("pe", lambda e, c=c, h=h, tt=tt: e.matmul(
                        cm.banks[yb[h]][:], lhsT=aT[:, c, tt * 128:(tt + 1) * 128], rhs=wd[:, c, h * 512:(h + 1) * 512],
                        start=(c == 0), stop=(c == npair - 1)),
                        reads=[b_aT[c][bq] for bq in bqs] + [b_wd[c]], writes=[cm.bank_b[yb[h]]], sig=(c == npair - 1))
        r0 = hf * half_tok
        emit_epilogue(kb, cm, lnp, lns, half_tok // 128, xh, 1 + r0, out[r0:r0 + half_tok, :], xres, b_xres, otile, b_ot, mm_fn)


def DFF_COLS(npair):
    return npair * 128


def build_ffn(ntok=TOK, npair=NPAIR, half_tok=1024):
    nc = bass.Bass("TRN2", target_bir_lowering=False)
    xh = nc.dram_tensor("xh", [ntok + 2, D], F32, kind="ExternalInput").ap()
    w_up = nc.dram_tensor("w_up", [npair, 128, KC * 2 * 128], F32, kind="ExternalInput").ap()
    dw = nc.dram_tensor("dw", [3, 2 * npair * 128], F32, kind="ExternalInput").ap()
    w_down = nc.dram_tensor("w_down", [npair * 128, D], F32, kind="ExternalInput").ap()
    ln_g = nc.dram_tensor("ln_g", [D], F32, kind="ExternalInput").ap()
    ln_b = nc.dram_tensor("ln_b", [D], F32, kind="ExternalInput").ap()
    out = nc.dram_tensor("out", [ntok, D], F32, kind="ExternalOutput").ap()
    with ExitStack() as es:
        kb = KB(nc, es)
        cm = Common(kb)
        emit_ffn(kb, cm, xh, out, w_up, dw, w_down, ln_g, ln_b, ntok=ntok, npair=npair, half_tok=half_tok)
        kb.finish()
    return nc


def load_cols(kb, cm, src, dst, b_dst, bank, name, alloc=None):
    R, C = src.shape
    nch = C // 128
    assert R * nch <= 512
    st = (alloc or kb.sb)(name + "_rows", [R, C], F32)
    b_st = kb.buf(name + "_rows")
    kb.dma(lambda e: e.dma_start(out=st[:], in_=src), writes=[b_st])
    for c in range(nch):
        kb.op("pe", lambda e, c=c: e.transpose(out=cm.banks[bank][:, c * R:(c + 1) * R],
                                               in_=st[0:R, c * 128:(c + 1) * 128], identity=cm.ident_f[0:R, 0:R]),
              reads=[b_st, cm.b_ident], writes=[cm.bank_b[bank]], sig=(c == nch - 1))
    kb.op("dve", lambda e: e.tensor_copy(out=dst, in_=cm.banks[bank][:, 0:nch * R].rearrange("p (c r) -> p r c", r=R)),
          reads=[cm.bank_b[bank]], writes=[b_dst])


CH = 512
CJ = CH // 128
CK = 31
HALO_C = 16


def emit_conv(kb, cm, xh, out, w_in, cdw_w, cdw_b, cln_g, cln_b, sconv_w, w_out, ln_g, ln_b, ntok=TOK):
    nc = kb.nc
    es_par = ExitStack()
    es_ab = ExitStack()
    es_c = ExitStack()
    es_d = ExitStack()
    mk = lambda es_: (lambda name, shape, dt: es_.enter_context(nc.sbuf_tensor(name, list(shape), dt)))
    sb_par, sb_ab, sb_c, sb_d = mk(es_par), mk(es_ab), mk(es_c), mk(es_d)
    nrows = ntok + 2 * HALO_C
    ntile_in = (nrows + 127) // 128
    xT = kb.sb("c_xT", [128, KC, ntile_in * 128], BF16)
    b_xT = [kb.buf("c_xT%d" % i) for i in range(ntile_in)]
    xst = [kb.sb("c_xst%d" % i, [128, D], F32) for i in range(2)]
    b_xst = [kb.buf("c_xst%d" % i) for i in range(2)]
    xbf = [kb.sb("c_xbf%d" % i, [128, D], BF16) for i in range(2)]
    b_xbf = [kb.buf("c_xbf%d" % i) for i in range(2)]
    lnp = LNParams(kb, "c_ln", ln_g, ln_b)
    lns = LNScratch(kb, "c_lns", n=2)
    dwc = kb.sb("c_dwc", [128, CK, CJ], F32)
    b_par = kb.buf("c_par")
    vec = kb.sb("c_vec", [128, 3, CJ], F32)
    scw = kb.sb("c_scw", [128, 3, CJ], F32)
    dg = kb.sb("c_dg", [128, CK * CJ, 128], BF16)
    b_dg = kb.buf("c_dg")
    ones_m = kb.sb("c_ones", [128, 128], F32)
    b_ones = kb.buf("c_ones")
    wo = kb.sb("c_wo", [128, KC, D], BF16)
    b_wo = [kb.buf("c_wo%d" % c) for c in range(KC)]
    ucols = ntok + 30
    uT = kb.sb("c_uT", [128, CJ, ucols], BF16)
    nub = (ucols + 511) // 512
    b_uT = [[kb.buf("c_uT%d_%d" % (j, b)) for b in range(nub)] for j in range(CJ)]
    yT = kb.sb("c_yT", [128, 2 * CJ, ntok], BF16)
    nob = ntok // 512
    b_yT = [[kb.buf("c_yT%d_%d" % (j, b)) for b in range(nob)] for j in range(2 * CJ)]
    load_cols(kb, cm, cdw_w, dwc[:], b_par, 0, "c_dww", sb_par)
    load_cols(kb, cm, cdw_b.rearrange("(o c) -> o c", o=1), vec[:, 0:1, :], b_par, 1, "c_dwb", sb_par)
    load_cols(kb, cm, cln_g.rearrange("(o c) -> o c", o=1), vec[:, 1:2, :], b_par, 2, "c_lng", sb_par)
    load_cols(kb, cm, cln_b.rearrange("(o c) -> o c", o=1), vec[:, 2:3, :], b_par, 3, "c_lnb", sb_par)
    load_cols(kb, cm, sconv_w, scw[:], b_par, 4, "c_scw", sb_par)
    kb.barrier()
    es_par.close()
    for k in range(CK):
        for j in range(CJ):
            kb.op("dve", lambda e, k=k, j=j: e.tensor_scalar(
                out=dg[:, k * CJ + j, :], in0=cm.ident_f[:], scalar1=dwc[:, k, j:j + 1], scalar2=None, op0=ALU.mult),
                reads=[b_par, cm.b_ident], writes=[b_dg], sig=(k == CK - 1 and j == CJ - 1))
    kb.op("dve", lambda e: e.memset(ones_m[:], 1.0 / CH), writes=[b_ones])
    w_out_v = w_out.rearrange("(c p) n -> p c n", p=128)
    for c in range(KC):
        kb.dma(lambda e, c=c: e.dma_start(out=wo[:, c, :], in_=w_out_v[:, c, :]), writes=[b_wo[c]], q="pool")
    wi = [sb_ab("c_wi%d" % i, [128, 3, KC, 128], BF16) for i in range(2)]
    b_wi = [kb.buf("c_wi%d" % i) for i in range(2)]
    tmpA = [sb_ab("c_tmpA%d" % i, [128, 512], F32) for i in range(2)]
    b_tmpA = [kb.buf("c_tmpA%d" % i) for i in range(2)]
    pT = sb_ab("c_pT", [128, ntok + 2], F32)
    b_pT = kb.buf("c_pT")
    qT = sb_ab("c_qT", [128, ntok], F32)
    b_qT = kb.buf("c_qT")

    for t in range(ntile_in):
        rows = min(128, nrows - t * 128)
        s = t % 2
        load_transpose_bf16(kb, cm, xh[t * 128:t * 128 + rows, :], rows, xbf[s], b_xbf[s], xT, t * 128, b_xT[t],
                            bank=6 + s, evac_eng=("act" if s == 0 else "dve"))

    def xtiles(c0, n):
        return [b_xT[t] for t in range(c0 // 128, (c0 + n - 1) // 128 + 1)]

    def load_group(g):
        s = g % 2
        n = 2 if g < 4 else 3
        kb.dma(lambda e: e.dma_start(out=wi[s][:, 0:n, :, :].rearrange("p g k c -> p g (k c)"), in_=w_in[g][:, 0:n, :]),
               writes=[b_wi[s]], q="pool")
        return s

    def proj(s, i, bank, xc0, n):
        for kc in range(KC):
            kb.op("pe", lambda e, kc=kc: e.matmul(cm.banks[bank][:, 0:n], lhsT=wi[s][:, i, kc, :],
                                                  rhs=xT[:, kc, xc0:xc0 + n], start=(kc == 0), stop=(kc == KC - 1)),
                  reads=[b_wi[s]] + xtiles(xc0, n), writes=[cm.bank_b[bank]], sig=(kc == KC - 1))

    it = 0
    for j in range(CJ):
        s = load_group(j)
        for b in range(nub):
            u0 = b * 512
            n = min(512, ucols - u0)
            par = it % 2
            it += 1
            ba, bg = (0, 1) if par == 0 else (2, 3)
            proj(s, 0, ba, u0 + 1, n)
            proj(s, 1, bg, u0 + 1, n)
            kb.op("act", lambda e, par=par, bg=bg, n=n: e.activation(out=tmpA[par][:, 0:n], in_=cm.banks[bg][:, 0:n],
                                                                     func=AF.Sigmoid),
                  reads=[cm.bank_b[bg]], writes=[b_tmpA[par]])
            kb.op("dve", lambda e, par=par, ba=ba, n=n, j=j, u0=u0: e.tensor_tensor(
                out=uT[:, j, u0:u0 + n], in0=cm.banks[ba][:, 0:n], in1=tmpA[par][:, 0:n], op=ALU.mult),
                reads=[cm.bank_b[ba], b_tmpA[par]], writes=[b_uT[j][b]])

    pcols = ntok + 2
    npb = (pcols + 511) // 512
    for j in range(CJ):
        s = load_group(4 + j)
        for b in range(npb):
            p0 = b * 512
            n = min(512, pcols - p0)
            par = it % 2
            it += 1
            ba, bg = (0, 1) if par == 0 else (2, 3)
            proj(s, 1, ba, p0 + 15, n)
            proj(s, 2, bg, p0 + 15, n)
            kb.op("act", lambda e, par=par, bg=bg, n=n: e.copy(out=tmpA[par][:, 0:n], in_=cm.banks[bg][:, 0:n]),
                  reads=[cm.bank_b[bg]], writes=[b_tmpA[par]])
            kb.op("dve", lambda e, par=par, ba=ba, n=n, p0=p0: e.tensor_tensor(
                out=pT[:, p0:p0 + n], in0=cm.banks[ba][:, 0:n], in1=tmpA[par][:, 0:n], op=ALU.mult),
                reads=[cm.bank_b[ba], b_tmpA[par]], writes=[b_pT])
        w0, w1, w2 = (scw[:, k, j:j + 1] for k in range(3))
        kb.op("act", lambda e, w1=w1: e.activation(out=qT[:, 0:ntok], in_=pT[:, 1:ntok + 1], func=AF.Copy, scale=w1),
              reads=[b_pT, b_par], writes=[b_qT])
        kb.op("dve", lambda e, w0=w0: e.scalar_tensor_tensor(out=qT[:, 0:ntok], in0=pT[:, 0:ntok], scalar=w0,
                                                             in1=qT[:, 0:ntok], op0=ALU.mult, op1=ALU.add),
              reads=[b_pT, b_par, b_qT], writes=[b_qT])
        kb.op("dve", lambda e, w2=w2: e.scalar_tensor_tensor(out=qT[:, 0:ntok], in0=pT[:, 2:ntok + 2], scalar=w2,
                                                             in1=qT[:, 0:ntok], op0=ALU.mult, op1=ALU.add),
              reads=[b_pT, b_par, b_qT], writes=[b_qT])
        for b in range(nob):
            bk = 4 + (b % 2)
            proj(s, 0, bk, b * 512 + 16, 512)
            kb.op("dve", lambda e, bk=bk, b=b, j=j: e.tensor_tensor(
                out=yT[:, CJ + j, b * 512:(b + 1) * 512], in0=cm.banks[bk][:], in1=qT[:, b * 512:(b + 1) * 512],
                op=ALU.mult),
                reads=[cm.bank_b[bk], b_qT], writes=[b_yT[CJ + j][b]])

    kb.barrier()
    es_ab.close()
    ub = sb_c("c_ub", [128, CJ, 512], F32)
    b_ub = [kb.buf("c_ub%d" % j) for j in range(CJ)]
    ub2 = sb_c("c_ub2", [128, CJ, 512], F32)
    b_ub2 = [kb.buf("c_ub2%d" % j) for j in range(CJ)]
    st_m = sb_c("c_stm", [128, 512], F32)
    st_r = sb_c("c_str", [128, 512], F32)
    b_st = kb.buf("c_st")
    for b in range(nob):
        for j in range(CJ):
            bk = j % 2
            rd = [b_uT[j][bb] for bb in range(nub) if bb * 512 < b * 512 + 542 and (bb + 1) * 512 > b * 512]
            for k in range(CK):
                kb.op("pe", lambda e, k=k, j=j, bk=bk, b=b: e.matmul(
                    cm.banks[bk][:], lhsT=dg[:, k * CJ + j, :], rhs=uT[:, j, b * 512 + k:b * 512 + k + 512],
                    start=(k == 0), stop=(k == CK - 1)),
                    reads=[b_dg] + rd, writes=[cm.bank_b[bk]], sig=(k == CK - 1))
            kb.op("act", lambda e, j=j, bk=bk: e.activation(out=ub[:, j, :], in_=cm.banks[bk][:], func=AF.Identity,
                                                            bias=vec[:, 0, j:j + 1], scale=1.0),
                  reads=[cm.bank_b[bk], b_par], writes=[b_ub[j]])
            kb.op("act", lambda e, j=j: e.activation(out=ub2[:, j, :], in_=ub[:, j, :], func=AF.Square),
                  reads=[b_ub[j]], writes=[b_ub2[j]])
        for j in range(CJ):
            kb.op("pe", lambda e, j=j: e.matmul(cm.banks[2][:], lhsT=ones_m[:], rhs=ub[:, j, :],
                                                start=(j == 0), stop=(j == CJ - 1)),
                  reads=[b_ones, b_ub[j]], writes=[cm.bank_b[2]], sig=(j == CJ - 1))
        for j in range(CJ):
            kb.op("pe", lambda e, j=j: e.matmul(cm.banks[3][:], lhsT=ones_m[:], rhs=ub2[:, j, :],
                                                start=(j == 0), stop=(j == CJ - 1)),
                  reads=[b_ones, b_ub2[j]], writes=[cm.bank_b[3]], sig=(j == CJ - 1))
        kb.op("act", lambda e: e.copy(out=st_m[:], in_=cm.banks[2][:]), reads=[cm.bank_b[2]], writes=[b_st])
        kb.op("dve", lambda e: e.tensor_tensor(out=st_r[:], in0=st_m[:], in1=st_m[:], op=ALU.mult),
              reads=[b_st], writes=[b_st])
        kb.op("dve", lambda e: e.tensor_tensor(out=st_r[:], in0=cm.banks[3][:], in1=st_r[:], op=ALU.subtract),
              reads=[b_st, cm.bank_b[3]], writes=[b_st])
        kb.op("dve", lambda e: e.tensor_scalar_add(out=st_r[:], in0=st_r[:], scalar1=LN_EPS), reads=[b_st], writes=[b_st])
        kb.op("act", lambda e: e.activation(out=st_r[:], in_=st_r[:], func=AF.Sqrt), reads=[b_st], writes=[b_st])
        kb.op("dve", lambda e: e.reciprocal(out=st_r[:], in_=st_r[:]), reads=[b_st], writes=[b_st])
        for j in range(CJ):
            kb.op("dve", lambda e, j=j: e.tensor_tensor(out=ub[:, j, :], in0=ub[:, j, :], in1=st_m[:], op=ALU.subtract),
                  reads=[b_ub[j], b_st], writes=[b_ub[j]])
            kb.op("pool", lambda e, j=j: e.tensor_tensor(out=ub[:, j, :], in0=ub[:, j, :], in1=st_r[:], op=ALU.mult),
                  reads=[b_ub[j], b_st], writes=[b_ub[j]])
            kb.op("act", lambda e, j=j, b=b: e.activation(out=yT[:, j, b * 512:(b + 1) * 512], in_=ub[:, j, :],
                                                          func=AF.Silu, bias=vec[:, 2, j:j + 1], scale=vec[:, 1, j:j + 1]),
                  reads=[b_ub[j], b_par], writes=[b_yT[j][b]])

    kb.barrier()
    es_c.close()
    otile = [sb_d("c_ot%d" % i, [128, D], F32) for i in range(2)]
    b_ot = [kb.buf("c_ot%d" % i) for i in range(2)]
    def mm_fn(tg, yb):
        for h in range(2):
            for c in range(2 * CJ):
                kb.op("pe", lambda e, c=c, h=h, tg=tg: e.matmul(
                    cm.banks[yb[h]][:], lhsT=yT[:, c, tg * 128:(tg + 1) * 128], rhs=wo[:, c, h * 512:(h + 1) * 512],
                    start=(c == 0), stop=(c == 2 * CJ - 1)),
                    reads=[b_yT[c][tg // 4], b_wo[c]], writes=[cm.bank_b[yb[h]]], sig=(c == 2 * CJ - 1))
    emit_epilogue(kb, cm, lnp, lns, ntok // 128, xh, HALO_C, out, xst, b_xst, otile, b_ot, mm_fn)
    kb.barrier()
    es_d.close()


def build_conv(ntok=TOK):
    nc = bass.Bass("TRN2", target_bir_lowering=False)
    di = lambda n, s: nc.dram_tensor(n, s, F32, kind="ExternalInput").ap()
    xh = di("xh", [ntok + 2 * HALO_C, D])
    w_in = di("w_in", [8, 128, 3, D])
    cdw_w = di("cdw_w", [CK, CH])
    cdw_b = di("cdw_b", [CH])
    cln_g = di("cln_g", [CH])
    cln_b = di("cln_b", [CH])
    sconv_w = di("sconv_w", [3, CH])
    w_out = di("w_out", [D, D])
    ln_g = di("ln_g", [D])
    ln_b = di("ln_b", [D])
    out = nc.dram_tensor("out", [ntok, D], F32, kind="ExternalOutput").ap()
    with ExitStack() as es:
        kb = KB(nc, es)
        cm = Common(kb)
        emit_conv(kb, cm, xh, out, w_in, cdw_w, cdw_b, cln_g, cln_b, sconv_w, w_out, ln_g, ln_b, ntok=ntok)
        kb.finish()
    return nc


HA = 1024
NTA = TOK + 2 * HA
NEG = -1e30
DILS = (1, 4, 16)
NA_OFFS = {0: (-2, -1, 0, 1, 2, 3), 15: (-3, -2, -1, 0, 1, 2)}
NA_DEF = (-2, -1, 0, 1, 2)
NA_NV = 22


def na_mask_index():
    idx = {}
    n = 0
    for cls, offs in (("I", NA_DEF), (0, NA_OFFS[0]), (1, NA_DEF), (14, NA_DEF), (15, NA_OFFS[15])):
        for d in offs:
            idx[(cls, d)] = n
            n += 1
    return idx, n


def dil_vtiles():
    tiles = []
    for dil in DILS:
        nq = TOK // dil // 128
        for r in range(dil):
            for s in range(nq + 1):
                tiles.append((dil, r, 128 * s - 64))
    return tiles


def emit_attn(kb, cm, xh, out, w_in, w_out, bias_all, ln_g, ln_b, n_pairs=8):
    nc = kb.nc
    dvt = dil_vtiles()
    dvt_idx = {t: i for i, t in enumerate(dvt)}
    NVT = len(dvt)
    mask_idx, nmask = na_mask_index()

    xT = kb.sb("a_xT", [128, KC, NTA], BF16)
    b_xT = [kb.buf("a_xT%d" % i) for i in range(NTA // 128)]
    yT = kb.sb("a_yT", [128, KC, TOK], BF16)
    b_yT = [kb.buf("a_yT%d" % i) for i in range(KC)]
    onesp = kb.sb("a_onesp", [128, 2, 128], BF16)
    b_onesp = kb.buf("a_onesp")
    kb.op("dve", lambda e: e.memset(onesp[:], 0.0), writes=[b_onesp])
    kb.op("dve", lambda e: e.memset(onesp[:, 0, 0:64], 1.0), writes=[b_onesp])
    kb.op("dve", lambda e: e.memset(onesp[:, 1, 64:128], 1.0), writes=[b_onesp])

    with ExitStack() as es2:
        sb2 = lambda name, shape, dt: es2.enter_context(nc.sbuf_tensor(name, list(shape), dt))
        with ExitStack() as es0:
            xbf = [es0.enter_context(nc.sbuf_tensor("a_xbf%d" % i, [128, D], BF16)) for i in range(2)]
            b_xbf = [kb.buf("a_xbf%d" % i) for i in range(2)]
            for t in range(NTA // 128):
                s = t % 2
                load_transpose_bf16(kb, cm, xh[t * 128:(t + 1) * 128, :], 128, xbf[s], b_xbf[s], xT, t * 128, b_xT[t],
                                    bank=6 + s, evac_eng=("act" if s == 0 else "dve"))
        kb.barrier()

        w_bf = [sb2("a_wbf%d" % i, [128, 3, KC, 128], BF16) for i in range(2)]
        b_wbf = [kb.buf("a_wbf%d" % i) for i in range(2)]
        QT = sb2("a_QT", [128, TOK], BF16)
        b_QT = [kb.buf("a_QT%d" % i) for i in range(TOK // 512)]
        KT = sb2("a_KT", [128, NTA], BF16)
        b_KT = [kb.buf("a_KT%d" % i) for i in range(NTA // 512)]
        VT = sb2("a_VT", [128, NTA], BF16)
        b_VT = [kb.buf("a_VT%d" % i) for i in range(NTA // 512)]
        Vp = sb2("a_Vp", [128, NVT, 2, 128], BF16)
        b_Vp = [kb.buf("a_Vp%d" % i) for i in range(NVT)]
        kb.op("pool", lambda e: e.memset(Vp[:], 0.0), writes=b_Vp)
        NCOMB = nmask
        bias_t = sb2("a_bias", [128, NCOMB, 2, 128], BF16)
        b_bias = kb.buf("a_bias")
        NGS = 6
        PT = sb2("a_PT", [128, 2 * NGS, 2, 128], BF16)
        b_PT = [kb.buf("a_PT%d" % i) for i in range(NGS)]
        accN = sb2("a_accN", [128, TOK], F32)
        accD = sb2("a_accD", [128, TOK], F32)
        b_acc = kb.buf("a_acc")
        rden = [sb2("a_rden%d" % i, [128, 128], F32) for i in range(2)]
        b_rden = [kb.buf("a_rden%d" % i) for i in range(2)]

        def load_pair_params(pr_):
            s_ = pr_ % 2
            kb.dma(lambda e: e.dma_start(out=w_bf[s_][:].rearrange("p g k c -> p g (k c)"), in_=w_in[pr_]),
                   writes=[b_wbf[s_]], q="pool")

        def load_pair_bias(pr_):
            ncomb = NCOMB if pr_ < 4 else 24
            for m0 in range(0, ncomb, 7):
                m1 = min(ncomb, m0 + 7)
                kb.dma(lambda e, m0=m0, m1=m1: e.dma_start(
                    out=bias_t[:, m0:m1, :, :].rearrange("p m h q -> p (m h q)"),
                    in_=bias_all[pr_][:, m0:m1, :, :].rearrange("p m h q -> p (m h q)")),
                    writes=[b_bias], q="pool")

        load_pair_params(0)
        b_S = [kb.buf("a_S%d" % i) for i in range(3)]
        ND_slots = [(2, 0), (3, 0)]
        cnt = {"w": 0, "step": 0, "job": 0, "pj": 0}

        def blocks(c0, c1, width, bufs):
            return [bufs[i] for i in range(c0 // width, (c1 - 1) // width + 1)]

        for pr in range(n_pairs):
            is_na = pr < 4
            hp = pr % 4
            s = pr % 2
            load_pair_bias(pr)
            if pr + 1 < n_pairs:
                load_pair_params(pr + 1)
            def projT(i, dst, dst_bufs, c_lo, c_hi, xoff, scale, eng="dve"):
                for c0 in range(c_lo, c_hi, 512):
                    n = min(512, c_hi - c0)
                    bk = 4 + (cnt["pj"] % 2)
                    cnt["pj"] += 1
                    for kc in range(KC):
                        kb.op("pe", lambda e, kc=kc, bk=bk, c0=c0, n=n: e.matmul(
                            cm.banks[bk][:, 0:n], lhsT=w_bf[s][:, i, kc, :], rhs=xT[:, kc, c0 + xoff:c0 + xoff + n],
                            start=(kc == 0), stop=(kc == KC - 1)),
                            reads=[b_wbf[s]] + blocks(c0 + xoff, c0 + xoff + n, 128, b_xT), writes=[cm.bank_b[bk]],
                            sig=(kc == KC - 1))
                    if scale is None and eng == "dve":
                        kb.op("dve", lambda e, bk=bk, c0=c0, n=n: e.tensor_copy(out=dst[:, c0:c0 + n], in_=cm.banks[bk][:, 0:n]),
                              reads=[cm.bank_b[bk]], writes=blocks(c0, c0 + n, 512, dst_bufs))
                    elif scale is None:
                        kb.op("act", lambda e, bk=bk, c0=c0, n=n: e.copy(out=dst[:, c0:c0 + n], in_=cm.banks[bk][:, 0:n]),
                              reads=[cm.bank_b[bk]], writes=blocks(c0, c0 + n, 512, dst_bufs))
                    else:
                        kb.op("act", lambda e, bk=bk, c0=c0, n=n: e.mul(out=dst[:, c0:c0 + n], in_=cm.banks[bk][:, 0:n], mul=scale),
                              reads=[cm.bank_b[bk]], writes=blocks(c0, c0 + n, 512, dst_bufs))

            projT(0, QT, b_QT, 0, TOK, HA, 0.125)
            if is_na:
                projT(1, KT, b_KT, HA - 384, HA + TOK + 384, 0, None)
                projT(2, VT, b_VT, HA - 384, HA + TOK + 384, 0, None, eng="act")
            else:
                projT(1, KT, b_KT, 0, NTA, 0, None)
                projT(2, VT, b_VT, 0, NTA, 0, None, eng="act")

            def vtile(slot, xc0, step):
                bk = 4 + (cnt["pj"] % 2)
                cnt["pj"] += 1
                hi = xc0 + step * 127 + 1
                pv = cm.banks[bk].bitcast(BF16)
                kb.op("pe", lambda e: e.transpose(out=pv[:, 0:128], in_=VT[:, xc0:hi:step], identity=cm.ident_b[:]),
                      reads=blocks(xc0, hi, 512, b_VT) + [cm.b_ident], writes=[cm.bank_b[bk]])
                kb.op("act", lambda e: e.copy(out=Vp[:, slot, 0, 0:64], in_=pv[:, 0:64]),
                      reads=[cm.bank_b[bk]], writes=[b_Vp[slot]])
                kb.op("dve", lambda e: e.tensor_copy(out=Vp[:, slot, 1, 64:128], in_=pv[:, 64:128]),
                      reads=[cm.bank_b[bk]], writes=[b_Vp[slot]])

            if is_na:
                for t in range(NA_NV):
                    vtile(t, HA + (t - 3) * 128, 1)
            else:
                for i, (dil, r, ms) in enumerate(dvt):
                    vtile(i, HA + dil * ms + r, dil)

            jobs = []
            if is_na:
                for qi in range(16):
                    offs = NA_OFFS.get(qi, NA_DEF)
                    cls = qi if qi in (0, 1, 14, 15) else "I"
                    steps = []
                    for d in offs:
                        steps.append(dict(kc0=HA + (qi + d) * 128, kstep=1, vslot=qi + d + 3, kvcol=qi + d + 3,
                                          comb=mask_idx[(cls, d)]))
                    jobs.append(dict(qc0=qi * 128, qstep=1, steps=steps, pat=None))
            else:
                for p, dil in enumerate(DILS):
                    for r in range(dil):
                        for j in range(TOK // dil // 128):
                            steps = []
                            nj = TOK // dil // 128
                            v = (1 if j == 0 else 0) + (2 if j == nj - 1 else 0)
                            for kt in range(2):
                                ms = 128 * (j + kt) - 64
                                vi = dvt_idx[(dil, r, ms)]
                                steps.append(dict(kc0=HA + dil * ms + r, kstep=dil, vslot=vi, kvcol=NA_NV + vi,
                                                  comb=p * 8 + v * 2 + kt))
                            jobs.append(dict(qc0=r + dil * 128 * j, qstep=dil, steps=steps, pat=p))


            def emit_group(jb, si0, nu):
                g = cnt["step"]
                cnt["step"] += 1
                ssl = g % 3
                gs = g % NGS
                bp = cm.bankpair[(0, 3, 2)[ssl]]
                sbufs = [b_S[ssl]] + ([cm.bank_b[4], cm.bank_b[5]] if ssl == 2 else [])
                q0, qs = jb["qc0"], jb["qstep"]
                qhi = q0 + qs * 127 + 1
                for u in range(nu):
                    st = jb["steps"][si0 + u]
                    st["slot"] = 2 * gs + u
                    st["gs"] = gs
                    k0, ks = st["kc0"], st["kstep"]
                    khi = k0 + ks * 127 + 1
                    for h in range(2):
                        o = bp[:, h * 512 + u * 128:h * 512 + (u + 1) * 128]
                        kb.op("pe", lambda e, o=o, h=h, k0=k0, khi=khi, ks=ks: e.matmul(
                            o, lhsT=KT[h * 64:(h + 1) * 64, k0:khi:ks], rhs=QT[h * 64:(h + 1) * 64, q0:qhi:qs],
                            start=True, stop=True),
                            reads=blocks(k0, khi, 512, b_KT) + blocks(q0, qhi, 512, b_QT), writes=sbufs,
                            sig=(h == 1 and u == nu - 1))
                c0 = jb["steps"][si0]["comb"]
                for u in range(nu):
                    assert jb["steps"][si0 + u]["comb"] == c0 + u
                sview = bp[:].rearrange("p (b u q) -> p b u q", b=2, q=128)[:, :, 0:nu, :]
                kb.op("dve", lambda e: e.tensor_tensor(out=sview, in0=sview,
                                                       in1=bias_t[:, c0:c0 + nu, :, :].rearrange("p u h q -> p h u q"),
                                                       op=ALU.add),
                      reads=sbufs + [b_bias], writes=sbufs)
                kb.op("act", lambda e: e.activation(out=PT[:, 2 * gs:2 * gs + nu, :, :].rearrange("p u h q -> p h u q"),
                                                    in_=sview, func=AF.Exp),
                      reads=sbufs, writes=[b_PT[gs]])

            def emit_pv(jb):
                jb["nd"] = cnt["job"] % 2
                cnt["job"] += 1
                nbk, nco = ND_slots[jb["nd"]]
                ns = len(jb["steps"])
                for which in range(2):
                    o = cm.banks[nbk][:, nco + which * 128:nco + (which + 1) * 128]
                    for si in range(ns):
                        st = jb["steps"][si]
                        slot = st["slot"]
                        for h in range(2):
                            lhs = Vp[:, st["vslot"], h, :] if which == 0 else onesp[:, h, :]
                            rd = [b_Vp[st["vslot"]]] if which == 0 else [b_onesp]
                            first = (si == 0 and h == 0)
                            last = (si == ns - 1 and h == 1)
                            kb.op("pe", lambda e, o=o, lhs=lhs, h=h, first=first, last=last, slot=slot: e.matmul(
                                o, lhsT=lhs, rhs=PT[:, slot, h, :], start=first, stop=last),
                                reads=rd + [b_PT[st["gs"]]], writes=[cm.bank_b[nbk]], sig=(last and which == 1))
                q0, qs = jb["qc0"], jb["qstep"]
                qhi = q0 + qs * 127 + 1
                num = cm.banks[nbk][:, nco:nco + 128]
                den = cm.banks[nbk][:, nco + 128:nco + 256]
                if is_na:
                    ri = jb["nd"]
                    kb.op("dve", lambda e: e.reciprocal(out=rden[ri][:], in_=den), reads=[cm.bank_b[nbk]],
                          writes=[b_rden[ri]])
                    kb.op("dve", lambda e: e.tensor_tensor(out=yT[:, hp, q0:qhi:qs], in0=num, in1=rden[ri][:],
                                                           op=ALU.mult),
                          reads=[cm.bank_b[nbk], b_rden[ri]], writes=[b_yT[hp]])
                elif jb["pat"] == 0:
                    kb.op("dve", lambda e: e.tensor_copy(out=accN[:, q0:qhi:qs], in_=num), reads=[cm.bank_b[nbk]], writes=[b_acc])
                    kb.op("dve", lambda e: e.tensor_copy(out=accD[:, q0:qhi:qs], in_=den), reads=[cm.bank_b[nbk]], writes=[b_acc])
                else:
                    kb.op("dve", lambda e: e.tensor_tensor(out=accN[:, q0:qhi:qs], in0=num, in1=accN[:, q0:qhi:qs],
                                                           op=ALU.add), reads=[cm.bank_b[nbk], b_acc], writes=[b_acc])
                    kb.op("dve", lambda e: e.tensor_tensor(out=accD[:, q0:qhi:qs], in0=den, in1=accD[:, q0:qhi:qs],
                                                           op=ALU.add), reads=[cm.bank_b[nbk], b_acc], writes=[b_acc])

            LA = 1 if is_na else 2
            for ji in range(len(jobs) + LA):
                if ji < len(jobs):
                    ns_ = len(jobs[ji]["steps"])
                    for si in range(0, ns_, 2):
                        emit_group(jobs[ji], si, min(2, ns_ - si))
                if ji >= LA:
                    emit_pv(jobs[ji - LA])
            if not is_na:
                kb.op("dve", lambda e: e.reciprocal(out=accD[:], in_=accD[:]), reads=[b_acc], writes=[b_acc])
                kb.op("dve", lambda e: e.tensor_tensor(out=yT[:, 4 + hp, :], in0=accN[:], in1=accD[:], op=ALU.mult),
                      reads=[b_acc], writes=[b_yT[4 + hp]])
    kb.barrier()

    lnp = LNParams(kb, "a_ln", ln_g, ln_b)
    lns = LNScratch(kb, "a_lns", n=2)
    wo = kb.sb("a_wo", [128, KC, D], BF16)
    b_wo = [kb.buf("a_wo%d" % c) for c in range(KC)]
    w_out_v = w_out.rearrange("(c p) n -> p c n", p=128)
    for c in range(KC):
        kb.dma(lambda e, c=c: e.dma_start(out=wo[:, c, :], in_=w_out_v[:, c, :]), writes=[b_wo[c]], q="pool")
    xres = [kb.sb("a_xres%d" % i, [128, D], F32) for i in range(2)]
    b_xres = [kb.buf("a_xres%d" % i) for i in range(2)]
    otile = [kb.sb("a_ot%d" % i, [128, D], F32) for i in range(2)]
    b_ot = [kb.buf("a_ot%d" % i) for i in range(2)]
    def mm_fn(tg, yb):
        for h in range(2):
            for c in range(KC):
                kb.op("pe", lambda e, c=c, h=h, tg=tg: e.matmul(
                    cm.banks[yb[h]][:], lhsT=yT[:, c, tg * 128:(tg + 1) * 128], rhs=wo[:, c, h * 512:(h + 1) * 512],
                    start=(c == 0), stop=(c == KC - 1)),
                    reads=[b_yT[c], b_wo[c]], writes=[cm.bank_b[yb[h]]], sig=(c == KC - 1))
    emit_epilogue(kb, cm, lnp, lns, TOK // 128, xh, HA, out, xres, b_xres, otile, b_ot, mm_fn)


def build_attn(n_pairs=8):
    nc = bass.Bass("TRN2", target_bir_lowering=False)
    di = lambda n, s: nc.dram_tensor(n, s, F32, kind="ExternalInput").ap()
    _, nmask = na_mask_index()
    xh = di("xh", [NTA, D])
    w_in = di("w_in", [8, 128, 3, D])
    w_out = di("w_out", [D, D])
    bias_all = di("bias_all", [8, 128, nmask, 2, 128])
    ln_g = di("ln_g", [D])
    ln_b = di("ln_b", [D])
    out = nc.dram_tensor("out", [TOK, D], F32, kind="ExternalOutput").ap()
    with ExitStack() as es:
        kb = KB(nc, es)
        cm = Common(kb)
        emit_attn(kb, cm, xh, out, w_in, w_out, bias_all, ln_g, ln_b, n_pairs=n_pairs)
        kb.finish()
    return nc


def t5_bucket_np(rel):
    nb = 16
    max_exact = 8
    ret = np.where(rel > 0, nb, 0)
    n = np.abs(rel)
    large = max_exact + (np.log(np.maximum(n, 1).astype(np.float32) / max_exact)
                         / np.float32(np.log(1024 / max_exact)) * (nb - max_exact)).astype(np.int32)
    large = np.minimum(large, nb - 1)
    return ret + np.where(n < max_exact, n, large)


def make_dil_bias(t5_bias):
    k = np.arange(128)[:, None]
    q = np.arange(128)[None, :]
    res = np.empty((3, 8, 2, 128, 128), np.float32)
    for p, dil in enumerate(DILS):
        for kt in range(2):
            rel = (128 * kt - 64 + k) - q
            bk = t5_bucket_np(rel * dil)
            g = t5_bias[bk]
            valid = np.abs(rel) <= 64
            for h in range(8):
                res[p, h, kt] = np.where(valid, g[:, :, h], np.float32(NEG))
    return res


def make_na_bias(rpb):
    k = np.arange(128)[:, None]
    q = np.arange(128)[None, :]
    res = np.zeros((8, 7, 128, 128), np.float32)
    for di_, d in enumerate(range(-3, 4)):
        rr = 2 * d + k // 64 - q // 64
        cr = k % 64 - q % 64
        ok = (np.abs(rr) <= 7) & (np.abs(cr) <= 15)
        rri = np.clip(rr + 7, 0, 14)
        cri = np.clip(cr + 15, 0, 30)
        for h in range(8):
            res[h, di_] = np.where(ok, rpb[h][rri, cri], np.float32(0))
    return res


def make_na_mask(seg):
    idx, n = na_mask_index()
    res = np.empty((n, 128, 128), np.float32)
    k = np.arange(128)[:, None]
    q = np.arange(128)[None, :]
    for (cls, d), m in idx.items():
        qi = 5 if cls == "I" else cls
        gq = seg * 16 + qi
        r = 2 * gq + q // 64
        c = q % 64
        kr = 2 * (gq + d) + k // 64
        kc = k % 64
        r0 = np.clip(r - 4, 0, 120)
        c0 = np.clip(c - 8, 0, 48)
        ok = (kr >= r0) & (kr < r0 + 8) & (kc >= c0) & (kc < c0 + 16)
        res[m] = np.where(ok, np.float32(0), np.float32(NEG))
    return res


def make_kvb(seg):
    t0 = seg * TOK
    cols = []
    p = np.arange(128)
    for t in range(NA_NV):
        tok = t0 + (t - 3) * 128 + p
        cols.append(tok)
    for (dil, r, ms) in dil_vtiles():
        tok = t0 + dil * (ms + p) + r
        cols.append(tok)
    tok = np.stack(cols, axis=1)
    return np.where((tok >= 0) & (tok < SEQ), np.float32(0), np.float32(NEG)).astype(np.float32)


def relayout_w_up(w_up):
    dff = w_up.shape[1] // 2
    npair = dff // 128
    w = w_up.reshape(KC, 128, 2, npair, 128)
    return np.ascontiguousarray(w.transpose(3, 1, 0, 2, 4)).reshape(npair, 128, KC * 2 * 128)


def relayout_groups(w, groups):
    res = np.zeros((len(groups), 128, 3, KC, 128), np.float32)
    wv = w.reshape(KC, 128, -1)
    for g, cols in enumerate(groups):
        for i, c0 in enumerate(cols):
            res[g, :, i] = wv[:, :, c0:c0 + 128].transpose(1, 0, 2)
    return res.reshape(len(groups), 128, 3, KC * 128)


CONV_GROUPS = [[j * 128, CH + j * 128] for j in range(CJ)] + \
              [[2 * CH + j * 128, 3 * CH + j * 128, 4 * CH + j * 128] for j in range(CJ)]
ATTN_GROUPS = [[(0 if pr < 4 else 3 * CH) + i * CH + (pr % 4) * 128 for i in range(3)] for pr in range(8)]


def make_bias_all(na_bias, na_mask, dil_bias, seg, nseg):
    idx, n = na_mask_index()
    res = np.zeros((8, n, 2, 128, 128), np.float32)
    k = np.arange(128)[:, None]
    for pr in range(8):
        hp = pr % 4
        for h in range(2):
            hg = hp * 2 + h
            if pr < 4:
                for (cls, d), m in idx.items():
                    res[pr, m, h] = np.where(na_mask[m] == 0, na_bias[hg, d + 3], np.float32(NEG))
            else:
                for p in range(3):
                    for v in range(4):
                        for kt in range(2):
                            t = dil_bias[p, hg, kt]
                            if kt == 0 and (v & 1) and seg == 0:
                                t = np.where(k < 64, np.float32(NEG), t)
                            if kt == 1 and (v & 2) and seg == nseg - 1:
                                t = np.where(k >= 64, np.float32(NEG), t)
                            res[pr, p * 8 + v * 2 + kt, h] = t
    return np.ascontiguousarray(res.transpose(0, 3, 1, 2, 4))


_PROGS = {}


def _prog(name):
    if name not in _PROGS:
        _PROGS[name] = {"attn": build_attn, "conv": build_conv, "ffn": build_ffn}[name]()
    return _PROGS[name]


def _shards_with_halo(x, halo):
    B, S, Dm = x.shape
    xp = np.zeros((B, S + 2 * halo, Dm), np.float32)
    xp[:, halo:halo + S] = x
    res = []
    for c in range(NCORES):
        b, seg = divmod(c, S // TOK)
        res.append(np.ascontiguousarray(xp[b, seg * TOK:seg * TOK + TOK + 2 * halo]))
    return res


def _gather(res, B, S):
    out = np.empty((B, S, D), np.float32)
    for c in range(NCORES):
        b, seg = divmod(c, S // TOK)
        out[b, seg * TOK:(seg + 1) * TOK] = res.results[c]["out"]
    return out


def _run(name, in_maps):
    return run_bass_kernel_spmd(_prog(name), in_maps, core_ids=list(range(NCORES)))


def kernel(x, t5_bias, attn_w_in, attn_w_out, na_rpb, conv_w_in, conf_dw_w, conf_dw_b, conf_ln_g, conf_ln_b,
           sconv_w, conv_w_out, ffn_w_up, ffn_dw_w, ffn_w_down, mix_ln_g, mix_ln_b, ffn_ln_g, ffn_ln_b):
    f = lambda a: np.ascontiguousarray(np.asarray(a, dtype=np.float32))
    x = f(x)
    B, S, _ = x.shape
    nseg = S // TOK
    dil_bias = make_dil_bias(f(t5_bias))
    na_masks = [make_na_mask(seg) for seg in range(nseg)]
    for i in range(4):
        j = i // 2
        if i % 2 == 0:
            xs = _shards_with_halo(x, HA)
            na_bias = make_na_bias(f(na_rpb[j]))
            common = dict(w_in=relayout_groups(f(attn_w_in[j]), ATTN_GROUPS), w_out=f(attn_w_out[j]),
                          ln_g=f(mix_ln_g[i]), ln_b=f(mix_ln_b[i]))
            biases = [make_bias_all(na_bias, na_masks[seg], dil_bias, seg, nseg) for seg in range(nseg)]
            maps = [dict(common, xh=xs[c], bias_all=biases[c % nseg]) for c in range(NCORES)]
            x = _gather(_run("attn", maps), B, S)
        else:
            xs = _shards_with_halo(x, HALO_C)
            common = dict(w_in=relayout_groups(f(conv_w_in[j]), CONV_GROUPS), cdw_w=f(conf_dw_w[j]), cdw_b=f(conf_dw_b[j]), cln_g=f(conf_ln_g[j]),
                          cln_b=f(conf_ln_b[j]), sconv_w=f(sconv_w[j]), w_out=f(conv_w_out[j]),
                          ln_g=f(mix_ln_g[i]), ln_b=f(mix_ln_b[i]))
            maps = [dict(common, xh=xs[c]) for c in range(NCORES)]
            x = _gather(_run("conv", maps), B, S)
        xs = _shards_with_halo(x, 1)
        common = dict(w_up=relayout_w_up(f(ffn_w_up[i])), dw=f(ffn_dw_w[i]), w_down=f(ffn_w_down[i]),
                      ln_g=f(ffn_ln_g[i]), ln_b=f(ffn_ln_b[i]))
        maps = [dict(common, xh=xs[c]) for c in range(NCORES)]
        x = _gather(_run("ffn", maps), B, S)
    return x
```
